# Optimizing a Trainium2 kernel written in Bass

```python
import jax, jax.numpy as jnp
from jax import lax
import numpy as np

D_MODEL = 2048
BATCH = 4
SEQ = 2048
DEPTH = 2
DEC_BATCH = 128
DEC_SEQ = 4
PAST_LEN = 16384
PAGE_SIZE = 128

G_RWKV = D_MODEL // 2
G_CONV = D_MODEL - G_RWKV
HEAD_SIZE = 64
N_HEADS = G_RWKV // HEAD_SIZE
CONV_GROUP = 64
N_CONV_GROUPS = G_CONV // CONV_GROUP
CONV_W = 3
D_FF = ((8 * D_MODEL // 3 + 255) // 256) * 256
LORA_DECAY = max(32, int(round(1.8 * D_MODEL ** 0.5 / 32)) * 32)
LORA_ICLR = max(32, int(round(1.8 * D_MODEL ** 0.5 / 32)) * 32)
LORA_MV = max(32, int(round(1.3 * D_MODEL ** 0.5 / 32)) * 32)
LORA_GATE = max(32, int(round(0.6 * D_MODEL ** 0.8 / 32)) * 32)
RMS_EPS = 1e-6
GN_EPS = 64e-5

kernel_name = 'hymba_rwkv7_shortconv_convffn_adaln_step'


def rmsnorm(x, g):
    xf = x.astype(jnp.float32)
    y = xf * lax.rsqrt(jnp.mean(xf * xf, axis=-1, keepdims=True) + RMS_EPS)
    return (y * g.astype(jnp.float32)).astype(x.dtype)


def causal_dwconv(buf, x, w):
    T = x.shape[1]
    full = jnp.concatenate([buf.astype(x.dtype), x], axis=1)
    y = full[:, 0:T] * w[0]
    for j in range(1, CONV_W):
        y = y + full[:, j:j + T] * w[j]
    return y, full[:, full.shape[1] - (CONV_W - 1):]


def wkv7_scan(S0, r, w, k, v, kk, b):
    def step(S, inp):
        r_t, w_t, k_t, v_t, kk_t, b_t = inp
        sa = jnp.einsum('bhij,bhj->bhi', S, -kk_t)
        S = S * w_t[:, :, None, :] + sa[..., None] * b_t[:, :, None, :] + v_t[..., None] * k_t[:, :, None, :]
        y = jnp.einsum('bhij,bhj->bhi', S, r_t)
        return S, y
    xs = tuple(jnp.moveaxis(t, 1, 0) for t in (r, w, k, v, kk, b))
    S, ys = lax.scan(step, S0, xs)
    return jnp.moveaxis(ys, 0, 1), S


def run_trunk(x, c, st_wkv, st_shift, st_conv, st_ffn, p):
    B, T, _ = x.shape
    f32 = jnp.float32
    new_wkv, new_shift, new_conv, new_ffn = [], [], [], []
    v_first = None
    for l in range(DEPTH):
        mod = jax.nn.silu(c) @ p['ada_w'][l] + p['ada_b'][l]
        sh1, sc1, ga1, sh2, sc2, ga2 = jnp.split(mod[:, None, :], 6, axis=-1)

        h = rmsnorm(x, p['norm_g'][l, 0]) * (1 + sc1) + sh1
        h_last = st_shift[l].astype(h.dtype)
        xx = jnp.concatenate([h_last[:, None], h[:, :-1]], axis=1) - h
        w_in = p['w_in'][l]
        proj = h @ w_in
        p_rkv, p_conv = proj[..., :3 * G_RWKV], proj[..., 3 * G_RWKV:]
        p_rkv_prev = jnp.concatenate([(h_last @ w_in[:, :3 * G_RWKV])[:, None], p_rkv[:, :-1]], axis=1)
        rkv = p_rkv + (p_rkv_prev - p_rkv) * p['mu_rkv'][l].reshape(-1)
        r, k, v = jnp.split(rkv.astype(f32), 3, axis=-1)

        mu = p['mu_x'][l]
        xw, xa, xg, xv = (h + xx * mu[i] for i in range(4))
        w_log = -jax.nn.softplus(-(p['decay_w0'][l] + jnp.tanh(xw @ p['decay_lora1'][l]) @ p['decay_lora2'][l]).astype(f32)) - 0.5
        decay = jnp.exp(-jnp.exp(w_log))
        a = jax.nn.sigmoid((p['iclr_a0'][l] + (xa @ p['iclr_lora1'][l]) @ p['iclr_lora2'][l]).astype(f32))
        g = jax.nn.sigmoid(xg @ p['gate_lora1'][l]) @ p['gate_lora2'][l]
        if v_first is None:
            v_first = v
        else:
            nu = jax.nn.sigmoid((p['vres_v0'][l - 1] + (xv @ p['vres_lora1'][l - 1]) @ p['vres_lora2'][l - 1]).astype(f32))
            v = v + (v_first - v) * nu
        kk = (k * p['k_k'][l]).reshape(B, T, N_HEADS, HEAD_SIZE)
        kk = kk / jnp.maximum(jnp.sqrt(jnp.sum(kk * kk, axis=-1, keepdims=True)), 1e-12)
        k = k * (1 + (a - 1) * p['k_a'][l])
        r_h, w_h, k_h, v_h, a_h = (t.reshape(B, T, N_HEADS, HEAD_SIZE) for t in (r, decay, k, v, a))
        y, S = wkv7_scan(st_wkv[l].astype(f32), r_h, w_h, k_h, v_h, kk, kk * a_h)
        mean = jnp.mean(y, axis=-1, keepdims=True)
        var = jnp.mean(jnp.square(y - mean), axis=-1, keepdims=True)
        yn = ((y - mean) * lax.rsqrt(var + GN_EPS)).reshape(B, T, G_RWKV) * p['ln_x_w'][l] + p['ln_x_b'][l]
        bonus = (jnp.sum(r_h * k_h * p['r_k'][l], axis=-1, keepdims=True) * v_h).reshape(B, T, G_RWKV)
        o_rwkv = ((yn + bonus) * g).astype(x.dtype)

        bg, cg, hc = jnp.split(p_conv, 3, axis=-1)
        zc, conv_buf = causal_dwconv(st_conv[l], cg * hc, p['conv_w'][l])
        o_conv = bg * zc

        x = x + ga1 * (jnp.concatenate([o_rwkv, o_conv], axis=-1) @ p['w_out'][l])

        h2 = rmsnorm(x, p['norm_g'][l, 1]) * (1 + sc2) + sh2
        u, ffn_buf = causal_dwconv(st_ffn[l], h2 @ p['ffn_up'][l], p['ffn_conv'][l])
        ua, ub = jnp.split(u, 2, axis=-1)
        x = x + ga2 * ((jax.nn.silu(ua) * ub) @ p['ffn_down'][l])

        new_wkv.append(S.astype(st_wkv.dtype))
        new_shift.append(h[:, -1].astype(st_shift.dtype))
        new_conv.append(conv_buf.astype(st_conv.dtype))
        new_ffn.append(ffn_buf.astype(st_ffn.dtype))
    return (rmsnorm(x, p['final_norm_g']), jnp.stack(new_wkv), jnp.stack(new_shift),
            jnp.stack(new_conv), jnp.stack(new_ffn))


def setup_inputs(seed: int = 0) -> dict:
    key = jax.random.key(seed)
    keys = jax.random.split(key, 40)
    counter = [0]

    def nxt():
        kk = keys[counter[0]]
        counter[0] += 1
        return kk

    def nrm(shape, s=1.0):
        return jax.random.normal(nxt(), shape, jnp.float32) * s

    def uni(shape, lo=0.0, hi=1.0):
        return jax.random.uniform(nxt(), shape, jnp.float32, lo, hi)

    D, G, GC, F = D_MODEL, G_RWKV, G_CONV, D_FF
    return {
        'x_prompt': nrm((BATCH, SEQ, D)),
        'x_sample': nrm((DEC_BATCH, DEC_SEQ, D)),
        'c_prompt': nrm((BATCH, D)),
        'c_sample': nrm((DEC_BATCH, D)),
        'state_wkv': nrm((DEPTH, DEC_BATCH, N_HEADS, HEAD_SIZE, HEAD_SIZE), 0.5),
        'state_shift': nrm((DEPTH, DEC_BATCH, D)),
        'state_conv': nrm((DEPTH, DEC_BATCH, CONV_W - 1, GC)),
        'state_ffn': nrm((DEPTH, DEC_BATCH, CONV_W - 1, 2 * F)),
        'ada_w': nrm((DEPTH, D, 6 * D), 0.5 * D ** -0.5),
        'ada_b': nrm((DEPTH, 6 * D), 0.02),
        'norm_g': 1.0 + nrm((DEPTH, 2, D), 0.05),
        'final_norm_g': 1.0 + nrm((D,), 0.05),
        'w_in': nrm((DEPTH, D, 3 * G + 3 * GC), D ** -0.5),
        'mu_x': uni((DEPTH, 4, D)),
        'mu_rkv': uni((DEPTH, 3, G)),
        'decay_w0': uni((DEPTH, G), -6.0, -1.0),
        'decay_lora1': nrm((DEPTH, D, LORA_DECAY), D ** -0.5),
        'decay_lora2': nrm((DEPTH, LORA_DECAY, G), 0.5 * LORA_DECAY ** -0.5),
        'iclr_a0': nrm((DEPTH, G), 0.1),
        'iclr_lora1': nrm((DEPTH, D, LORA_ICLR), D ** -0.5),
        'iclr_lora2': nrm((DEPTH, LORA_ICLR, G), 0.5 * LORA_ICLR ** -0.5),
        'gate_lora1': nrm((DEPTH, D, LORA_GATE), D ** -0.5),
        'gate_lora2': nrm((DEPTH, LORA_GATE, G), LORA_GATE ** -0.5),
        'vres_v0': nrm((DEPTH - 1, G), 0.1),
        'vres_lora1': nrm((DEPTH - 1, D, LORA_MV), D ** -0.5),
        'vres_lora2': nrm((DEPTH - 1, LORA_MV, G), 0.5 * LORA_MV ** -0.5),
        'k_k': 0.85 + nrm((DEPTH, G), 0.05),
        'k_a': 1.0 + nrm((DEPTH, G), 0.05),
        'r_k': nrm((DEPTH, N_HEADS, HEAD_SIZE), 0.1),
        'ln_x_w': 1.0 + nrm((DEPTH, G), 0.05),
        'ln_x_b': nrm((DEPTH, G), 0.02),
        'conv_w': nrm((DEPTH, CONV_W, GC), CONV_W ** -0.5),
        'w_out': nrm((DEPTH, D, D), D ** -0.5),
        'ffn_up': nrm((DEPTH, D, 2 * F), D ** -0.5),
        'ffn_conv': nrm((DEPTH, CONV_W, 2 * F), CONV_W ** -0.5),
        'ffn_down': nrm((DEPTH, F, D), F ** -0.5),
    }


def reference(x_prompt, x_sample, c_prompt, c_sample, state_wkv, state_shift, state_conv, state_ffn,
              ada_w, ada_b, norm_g, final_norm_g, w_in, mu_x, mu_rkv, decay_w0, decay_lora1, decay_lora2,
              iclr_a0, iclr_lora1, iclr_lora2, gate_lora1, gate_lora2, vres_v0, vres_lora1, vres_lora2,
              k_k, k_a, r_k, ln_x_w, ln_x_b, conv_w, w_out, ffn_up, ffn_conv, ffn_down):
    p = dict(ada_w=ada_w, ada_b=ada_b, norm_g=norm_g, final_norm_g=final_norm_g, w_in=w_in, mu_x=mu_x,
             mu_rkv=mu_rkv, decay_w0=decay_w0, decay_lora1=decay_lora1, decay_lora2=decay_lora2,
             iclr_a0=iclr_a0, iclr_lora1=iclr_lora1, iclr_lora2=iclr_lora2, gate_lora1=gate_lora1,
             gate_lora2=gate_lora2, vres_v0=vres_v0, vres_lora1=vres_lora1, vres_lora2=vres_lora2,
             k_k=k_k, k_a=k_a, r_k=r_k, ln_x_w=ln_x_w, ln_x_b=ln_x_b, conv_w=conv_w, w_out=w_out,
             ffn_up=ffn_up, ffn_conv=ffn_conv, ffn_down=ffn_down)
    bp = x_prompt.shape[0]
    dt = x_prompt.dtype
    z_wkv = jnp.zeros((DEPTH, bp) + state_wkv.shape[2:], dt)
    z_shift = jnp.zeros((DEPTH, bp) + state_shift.shape[2:], dt)
    z_conv = jnp.zeros((DEPTH, bp) + state_conv.shape[2:], dt)
    z_ffn = jnp.zeros((DEPTH, bp) + state_ffn.shape[2:], dt)
    y_prompt, wkv_p, shift_p, conv_p, ffn_p = run_trunk(x_prompt, c_prompt, z_wkv, z_shift, z_conv, z_ffn, p)
    y_sample, wkv_s, shift_s, conv_s, ffn_s = run_trunk(x_sample, c_sample, state_wkv, state_shift,
                                                        state_conv, state_ffn, p)
    return (y_prompt, y_sample, wkv_p, shift_p, conv_p, ffn_p, wkv_s, shift_s, conv_s, ffn_s)
```

```python
import numpy as np
import concourse.bass as bass
import concourse.mybir as mybir
from concourse.bass_utils import run_bass_kernel_spmd
from contextlib import ExitStack

F32 = mybir.dt.float32
BF16 = mybir.dt.bfloat16
ALU = mybir.AluOpType
AF = mybir.ActivationFunctionType

D = 2048; KC = 16; G = 1024; FF = 5632; NPT = 1024; NSQ = 16; E = 1122; SB0 = 1026
NGRP = 17
ADAB = 0; NG0 = 96; NG1 = 112; MUX = 128; MURKV = 192; A0 = 216; V0 = 224; KKc = 232; KAc = 240
RKc = 248; LNW = 256; LNB = 264; CW = 272; FCW = 296; FNG = 560; NSP = 576
C_ID = 0; C_ONES = 128; C_BD = 256; C_MUP = 384; C_MLP = 640; C_UTP = 768; C_MUS = 1024; C_MLS = 1152
C_UTS = 1216; C_CMS = 1344; C_RMS = 2368; NCONST = 2384
DEC = 0.6065306597126334


class _Rec:
    def __init__(self):
        self.call = None

    def __getattr__(self, name):
        def f(*a, **k):
            assert self.call is None
            self.call = (name, a, k)
            return None
        return f


class Sched:
    NDMA = 40

    def __init__(self, nc):
        self.nc = nc
        self.ops = []
        self.state = {}
        self.pending_dma = []
        self.floor = {}
        self.stopped = False
        self.last_op = {}

    def _access(self, op, key, write):
        name, sub = key
        ents = self.state.setdefault(name, [])
        hit = [e for e in ents if e[0] is None or sub is None or e[0] == sub]
        for e in hit:
            if e[1] is not None:
                op["deps"].add(e[1])
            if write:
                op["deps"].update(e[2])
        if write:
            for e in hit:
                ents.remove(e)
            ents.append([sub, op["i"], []])
        else:
            if not hit:
                ents.append([sub, None, [op["i"]]])
            else:
                for e in hit:
                    if len(e[2]) > 8:
                        e[2][:] = [x for x in e[2] if self.ops[x]["eng"] != op["eng"] or self.ops[x]["dma"]]
                    e[2].append(op["i"])

    def add(self, eng, fn, r=(), w=(), dma=False, coll=False, extra=()):
        if self.stopped:
            return dict(i=-1)
        if fn is not None:
            rec = _Rec()
            fn(rec)
            name_, a_, k_ = rec.call
            fn = lambda e, name_=name_, a_=a_, k_=k_: getattr(e, name_)(*a_, **k_)
        op = dict(eng=eng, fn=fn, deps=set(extra), dma=dma, coll=coll, i=len(self.ops), sig=None, ndep=0)
        self.ops.append(op)
        for k in r:
            self._access(op, k, isinstance(k[0], str) and k[0].startswith("ps") and k[0][2:].isdigit())
        for k in w:
            self._access(op, k, True)
        op["deps"].discard(op["i"])
        if dma:
            self.pending_dma.append(op["i"])
        elif fn is not None:
            self.last_op[eng] = op["i"]
        return op

    def barrier(self):
        if self.stopped:
            return
        engs = ["pe", "act", "dve", "pool", "sp"]
        deps = list(self.pending_dma)
        self.pending_dma = []
        for e in engs:
            if self.last_op.get(e) is not None:
                deps.append(self.last_op[e])
        for e in engs:
            self.add(e, None, extra=deps)
        self.state = {}

    def _skip(self, p, c):
        return p["eng"] == "pe" and c["eng"] == "pe" and not p["dma"] and not c["dma"]

    def emit(self):
        nc = self.nc
        ops = self.ops
        for op in ops:
            for d in op["deps"]:
                if not self._skip(ops[d], op):
                    ops[d]["ndep"] += 1
        with ExitStack() as st:
            esem = {e: st.enter_context(nc.semaphore("se_" + e)) for e in ["pe", "act", "dve", "pool", "sp"]}
            dsem = [st.enter_context(nc.semaphore(f"sd{i}")) for i in range(self.NDMA)]
            cnt = {e: 0 for e in esem}
            dcnt = [0] * self.NDMA
            nd = 0
            csem = {}
            for op in ops:
                if op["dma"]:
                    if op["coll"]:
                        csem[op["i"]] = st.enter_context(nc.semaphore(f"sc{op['i']}"))
                        op["sig"] = (("c", op["i"]), 1)
                    else:
                        s = nd % self.NDMA
                        nd += 1
                        dcnt[s] += 16
                        op["sig"] = (("d", s), dcnt[s])
                elif op["ndep"] > 0:
                    cnt[op["eng"]] += 1
                    op["sig"] = (("e", op["eng"]), cnt[op["eng"]])

            def semof(k):
                if k[0] == "c":
                    return csem[k[1]]
                return dsem[k[1]] if k[0] == "d" else esem[k[1]]

            def run(en, e):
                waited = {}
                for op in ops:
                    if op["eng"] != en:
                        continue
                    need = {}
                    for d in op["deps"]:
                        p = ops[d]
                        if self._skip(p, op):
                            continue
                        k, v = p["sig"]
                        if waited.get(k, 0) < v:
                            need[k] = max(need.get(k, 0), v)
                    if op["dma"] and not op["coll"]:
                        k, v = op["sig"]
                        if v - 16 > 0 and waited.get(k, 0) < v - 16:
                            need[k] = max(need.get(k, 0), v - 16)
                    for k, v in need.items():
                        e.wait_ge(semof(k), v)
                        waited[k] = v
                    if op["fn"] is None:
                        assert op["sig"] is None
                        continue
                    ins = op["fn"](e)
                    if op["sig"] is not None:
                        k, v = op["sig"]
                        ins.then_inc(semof(k), (1 if op["coll"] else 16) if op["dma"] else 1)

            with nc.Block() as block:
                @block.tensor
                def _(e):
                    run("pe", e)

                @block.scalar
                def _(e):
                    run("act", e)

                @block.vector
                def _(e):
                    run("dve", e)

                @block.gpsimd
                def _(e):
                    run("pool", e)

                @block.sync
                def _(e):
                    run("sp", e)


class T:
    def __init__(self, name, t):
        self.name = name
        self.t = t

    def k(self, sub=None):
        return (self.name, sub)

    def __getitem__(self, idx):
        return self.t[idx]


DEBUG_STOP = None


class _StopBuild(Exception):
    pass


_SCHED = [None]


def mark(name):
    if DEBUG_STOP == name:
        _SCHED[0].stopped = True


def build_program():
    nc = bass.Bass("TRN2", target_bir_lowering=False)
    S = Sched(nc)
    _SCHED[0] = S

    def din(name, shape):
        return nc.dram_tensor(name, shape, F32, kind="ExternalInput").ap()

    def dout(name, shape):
        return nc.dram_tensor(name, shape, F32, kind="ExternalOutput").ap()

    def dint(name, shape):
        return nc.dram_tensor(name, shape, F32, kind="Internal").ap()

    xT = din("xT", [D, E]); cT = din("cT", [D, NGRP]); flag_d = din("flag", [128, 1])
    shiftT = din("shiftT", [2, D, NSQ]); convT = din("convT", [2, G, NSQ, 2]); ffnT = din("ffnT", [2, 2 * FF, NSQ, 2])
    wkv_in = din("wkv_in", [2, NSQ, 16, 64, 64]); smallp = din("smallp", [2, 128, NSP]); w0row = din("w0row", [2, 1, G])
    consts_d = din("consts", [128, NCONST])
    ada_w = din("ada_w", [2, 96, 128, KC * 128]); w_in = din("w_in", [2, 48, 128, KC * 128])
    dl1 = din("decay_lora1", [2, D, 96]); dl2 = din("decay_lora2", [2, 96, G])
    il1 = din("iclr_lora1", [2, D, 96]); il2 = din("iclr_lora2", [2, 96, G])
    gl1 = din("gate_lora1", [2, D, 256]); gl2 = din("gate_lora2", [2, 256, G])
    vl1 = din("vres_lora1", [1, D, 64]); vl2 = din("vres_lora2", [1, 64, G])
    w_out = din("w_out", [2, 16, 128, KC * 128]); ffn_up = din("ffn_up", [2, 88, 128, KC * 128]); ffn_down = din("ffn_down", [2, 16, 128, 44 * 128])
    yT = dout("yT", [D, NPT + 64]); wkvo = dout("wkvo", [2, NGRP, 16, 64, 64]); shifto = dout("shifto", [2, D, NGRP])
    convo = dout("convo", [2, G, NGRP, 2]); ffno = dout("ffno", [2, 2 * FF, NGRP, 2])
    x_sp = dint("x_sp", [D, E]); vf_d = dint("vf_d", [8, 128, E])
    cin_s = [dint(f"cin_s{i}", [128, 128]) for i in range(16)]
    cout_s = [dint(f"cout_s{i}", [256, 128]) for i in range(16)]
    cin_h = [dint(f"cin_h{i}", [128, 32]) for i in range(4)]
    cout_h = [dint(f"cout_h{i}", [256, 32]) for i in range(4)]
    groups = [[0, 1], [2, 3], [4, 5], [6, 7]]
    OUTK = [("yT", None), ("wkvo", None), ("shifto", None), ("convo", None), ("ffno", None)]

    root = ExitStack()

    uid = [0]

    def sb(st, name, shape, dt=F32):
        uid[0] += 1
        nm = f"s{uid[0]}_{name}"
        return T(nm, st.enter_context(nc.sbuf_tensor(nm, shape, dt)))

    def dve(fn, r, w): S.add("dve", fn, r, w)
    def act(fn, r, w): S.add("act", fn, r, w)
    def pe(fn, r, w): S.add("pe", fn, r, w)
    def pool(fn, r, w): S.add("pool", fn, r, w)
    def dma(fn, r, w): S.add("sp", fn, r, w, dma=True)
    def dmac(fn, r, w): S.add("pool", fn, r, w, dma=True)

    h = sb(root, "h", [128, KC, E], BF16)
    og = sb(root, "og", [128, KC * E], BF16)
    o3 = og[:, :].rearrange("p (k e) -> p k e", e=E)
    cst = sb(root, "cst", [128, NCONST])
    spt = sb(root, "spt", [128, NSP])
    flag = sb(root, "flag", [128, 1])
    psb = [T(f"ps{i}", root.enter_context(nc.psum_tensor(f"ps{i}", [128, 512], F32))) for i in range(8)]
    pctr = [0, 0]

    def psm():
        pctr[0] = (pctr[0] + 1) % 4
        return psb[pctr[0]]

    def psx():
        pctr[1] = (pctr[1] + 1) % 4
        return psb[4 + pctr[1]]

    dma(lambda e: e.dma_start(out=cst[:, :], in_=consts_d), [], [cst.k()])
    dma(lambda e: e.dma_start(out=flag[:, :], in_=flag_d), [], [flag.k()])
    ident = cst[:, C_ID:C_ID + 128]; ones = cst[:, C_ONES:C_ONES + 128]; bd = cst[:, C_BD:C_BD + 128]
    TA = [(1, 375), (375, 749), (749, E)]
    TB = [(0, 374), (374, 748), (748, E)]
    RKV_T = [(1, 386), (385, 770), (769, 1026), (1026, E)]

    def sview(ap2d):
        return ap2d[:, SB0:E].rearrange("p (s t) -> p s t", t=6)

    def exchange(idx_list_in, idx_list_out, src_tile, src_ap, dst_tile, dst_ap, cin, cout, rows_cols):
        dma(lambda e: e.dma_start(out=cin, in_=src_ap), [src_tile.k()], [(cin.name, None)])
        S.add("pool", lambda e: e.collective_compute("AllGather", ALU.bypass, replica_groups=groups, ins=[cin], outs=[cout]),
              [(cin.name, None)], [(cout.name, None)], dma=True, coll=True)
        dma(lambda e: e.dma_start(out=dst_ap, in_=cout[0:128, :]), [(cout.name, None)], [dst_tile.k()])

    xcm = [None]

    def load_small(l):
        dma(lambda e: e.dma_start(out=spt[:, :], in_=smallp[l]), [], [spt.k()])

    def phase_A(l, st, x, first):
        load_small(l)
        modt = sb(st, f"modt", [128, 96, NGRP])
        sc = sb(st, "sc", [128, KC, NGRP]); scb = sb(st, "scb", [128, KC, NGRP], BF16)
        dma(lambda e: e.dma_start(out=sc[:, :, :], in_=cT.rearrange("(k p) s -> p k s", p=128)), [], [sc.k()])
        act(lambda e: e.activation(scb[:, :, :], sc[:, :, :], AF.Silu), [sc.k()], [scb.k()])
        mark(f"A{l}s")
        awb = [sb(st, f"awb{i}", [128, KC, 128], BF16) for i in range(3)]
        for j in range(96):
            wb = awb[j % 3]
            dmac(lambda e, wb=wb, j=j: e.dma_start(out=wb[:, :, :], in_=ada_w[l, j].rearrange("p (k n) -> p k n", n=128)), [], [wb.k()])
            for jj in range(1):
                ps = psm()
                for kc in range(KC):
                    pe(lambda e, ps=ps, wb=wb, kc=kc, jj=jj: e.matmul(ps[:, 0:NGRP], wb[:, kc, :], scb[:, kc, :], start=(kc == 0), stop=(kc == KC - 1)), [wb.k(), scb.k()], [ps.k()])
                jb = j
                dve(lambda e, ps=ps, jb=jb: e.tensor_scalar(modt[:, jb, :], ps[:, 0:NGRP], spt[:, ADAB + jb:ADAB + jb + 1], None, ALU.add), [ps.k(), spt.k()], [modt.k(jb)])
                mark(f"A{l}m{jb}")
        mark(f"A{l}m")
        md = {}
        for nm, sci, ngo in (("A1", 16, NG0), ("A2", 64, NG1)):
            t = sb(root if False else st, nm, [128, KC, NGRP])
            md[nm] = t
            dve(lambda e, t=t, sci=sci: e.tensor_scalar(t[:, :, :], modt[:, sci:sci + 16, :], 1.0, None, ALU.add), [modt.k()], [t.k()])
            dve(lambda e, t=t, ngo=ngo: e.tensor_tensor(t[:, :, :], t[:, :, :], spt[:, ngo:ngo + 16].unsqueeze(2).to_broadcast([128, KC, NGRP]), ALU.mult), [t.k(), spt.k()], [t.k()])
        for nm, off in (("B1", 0), ("GA1", 32), ("B2", 48), ("GA2", 80)):
            t = sb(st, nm, [128, KC, NGRP])
            md[nm] = t
            dve(lambda e, t=t, off=off: e.tensor_copy(t[:, :, :], modt[:, off:off + 16, :]), [modt.k()], [t.k()])
        mark(f"A{l}d")
        return md

    keep_tiles = {nm: sb(root, "keep_" + nm, [128, KC, NGRP]) for nm in ("GA1", "A2", "B2", "GA2")}
    eps_t = sb(root, "eps_t", [128, 2])
    dve(lambda e: e.memset(eps_t[:, 0:1], 1e-6), [], [eps_t.k()])
    dve(lambda e: e.memset(eps_t[:, 1:2], 64e-5), [], [eps_t.k()])

    def norm_mod(st, x, A, B, gcol_unused, hl, tagsfx="", pre=None):
        if pre is None:
            rstd = sb(st, "rstd" + tagsfx, [128, E]); tmp = sb(st, "ntmp" + tagsfx, [128, E])
            sq = [sb(st, f"sq{i}" + tagsfx, [128, 512]) for i in range(2)]
        else:
            rstd, tmp, sq = pre
        n = 0
        for (c0, c1) in TB:
            ps = psm()
            for fc in range(KC):
                q = sq[n % 2]; n += 1
                act(lambda e, q=q, fc=fc, c0=c0, c1=c1: e.activation(q[:, 0:c1 - c0], x[:, fc, c0:c1], AF.Square), [x.k(fc)], [q.k()])
                pe(lambda e, ps=ps, q=q, fc=fc, c0=c0, c1=c1: e.matmul(ps[:, 0:c1 - c0], ones, q[:, 0:c1 - c0], start=(fc == 0), stop=(fc == KC - 1)), [q.k(), cst.k()], [ps.k()])
            act(lambda e, ps=ps, c0=c0, c1=c1: e.activation(tmp[:, c0:c1], ps[:, 0:c1 - c0], AF.Sqrt, bias=eps_t[:, 0:1], scale=1.0 / D), [ps.k(), eps_t.k()], [tmp.k()])
        dve(lambda e: e.reciprocal(rstd[:, :], tmp[:, :]), [tmp.k()], [rstd.k()])
        for fc in range(KC):
            dve(lambda e, fc=fc: e.tensor_tensor(tmp[:, :], x[:, fc, :], rstd[:, :], ALU.mult), [x.k(fc), rstd.k()], [tmp.k()])
            act(lambda e, fc=fc: e.activation(h[:, fc, 0:SB0], tmp[:, 0:SB0], AF.Identity, bias=B[:, fc, 0:1], scale=A[:, fc, 0:1]), [tmp.k(), A.k(), B.k()], [h.k(fc)])
            if hl is not None:
                act(lambda e, fc=fc: e.activation(hl[:, fc, 0:1], tmp[:, SB0 - 1:SB0], AF.Identity, bias=B[:, fc, 0:1], scale=A[:, fc, 0:1]), [tmp.k(), A.k(), B.k()], [hl.k()])
            ts = sview(tmp[:, :])
            dve(lambda e, fc=fc, ts=ts: e.tensor_tensor(ts, ts, A[:, fc, 1:NGRP].unsqueeze(2).to_broadcast([128, NSQ, 6]), ALU.mult), [tmp.k(), A.k()], [tmp.k()])
            if hl is not None:
                dve(lambda e, fc=fc, ts=ts: e.tensor_tensor(hl[:, fc, 1:NGRP], ts[:, :, 5], B[:, fc, 1:NGRP], ALU.add), [tmp.k(), B.k()], [hl.k()])
            dve(lambda e, fc=fc, ts=ts: e.tensor_tensor(sview(h[:, fc, :]), ts, B[:, fc, 1:NGRP].unsqueeze(2).to_broadcast([128, NSQ, 6]), ALU.add), [tmp.k(), B.k()], [h.k(fc)])

    def halo_exchange(st, ci, extra_cols=None):
        hs = sb(st, f"hs{ci}", [128, 32]); hr = sb(st, f"hr{ci}", [128, 32])
        dve(lambda e: e.tensor_copy(hs[:, :].rearrange("p (k t) -> p k t", t=2), h[:, :, SB0 - 2:SB0]), [h.k()], [hs.k()])
        exchange(None, None, hs, hs[:, :], hr, hr[:, :], cin_h[ci], cout_h[ci], None)
        dve(lambda e: e.tensor_scalar(h[:, :, 0:2], hr[:, :].rearrange("p (k t) -> p k t", t=2), flag[:, 0:1], None, ALU.mult), [hr.k(), flag.k()], [h.k()])

    def phase_B(l, st):
        nl = 512 if l == 1 else 448
        Wa = og[:, 0:KC * 512].rearrange("p (k n) -> p k n", n=512)
        Wb = og[:, KC * 512:2 * KC * 512].rearrange("p (k n) -> p k n", n=512)
        stg = [sb(st, f"stg{i}", [128, 512]) for i in range(2)]
        omx = sb(st, "omx", [128, 64])
        dve(lambda e: e.tensor_scalar(omx[:, :], spt[:, MUX:MUX + 64], -1.0, 1.0, ALU.mult, ALU.add), [spt.k()], [omx.k()])
        segs = [(dl1, 0, 96, 0), (il1, 96, 96, 1), (gl1, 192, 256, 2)] + ([(vl1, 448, 64, 3)] if l == 1 else [])
        for kc in range(KC):
            sg = stg[kc % 2]
            for (wd, c0, n, mi) in segs:
                ll = 0 if wd is vl1 else l
                dma(lambda e, sg=sg, wd=wd, ll=ll, c0=c0, n=n, kc=kc: e.dma_start(out=sg[:, c0:c0 + n], in_=wd[ll, kc * 128:(kc + 1) * 128, :]), [], [sg.k()])
            for (wd, c0, n, mi) in segs:
                dve(lambda e, sg=sg, c0=c0, n=n, mi=mi, kc=kc: e.tensor_scalar(Wa[:, kc, c0:c0 + n], sg[:, c0:c0 + n], omx[:, kc * 4 + mi:kc * 4 + mi + 1], None, ALU.mult), [sg.k(), omx.k()], [og.k()])
                dve(lambda e, sg=sg, c0=c0, n=n, mi=mi, kc=kc: e.tensor_scalar(Wb[:, kc, c0:c0 + n], sg[:, c0:c0 + n], spt[:, MUX + kc * 4 + mi:MUX + kc * 4 + mi + 1], None, ALU.mult), [sg.k(), spt.k()], [og.k()])
        th = sb(st, "th", [96, E], BF16); ia = sb(st, "ia", [96, E], BF16); gt = sb(st, "gt", [128, 2, E], BF16); vr = sb(st, "vr", [64, E], BF16)
        lgroups = [(0, 96, "th"), (96, 96, "ia"), (192, 128, "g0"), (320, 128, "g1")] + ([(448, 64, "vr")] if l == 1 else [])
        for (c0g, m, nm) in lgroups:
            for (c0, c1) in TA:
                ps = psm()
                for kc in range(KC):
                    pe(lambda e, ps=ps, kc=kc, c0=c0, c1=c1, c0g=c0g, m=m: e.matmul(ps[0:m, 0:c1 - c0], Wa[:, kc, c0g:c0g + m], h[:, kc, c0:c1], start=(kc == 0), stop=False), [og.k(), h.k(kc)], [ps.k()])
                    pe(lambda e, ps=ps, kc=kc, c0=c0, c1=c1, c0g=c0g, m=m: e.matmul(ps[0:m, 0:c1 - c0], Wb[:, kc, c0g:c0g + m], h[:, kc, c0 - 1:c1 - 1], start=False, stop=(kc == KC - 1)), [og.k(), h.k(kc)], [ps.k()])
                if nm == "th":
                    act(lambda e, ps=ps, c0=c0, c1=c1: e.activation(th[:, c0:c1], ps[0:96, 0:c1 - c0], AF.Tanh), [ps.k()], [th.k()])
                elif nm == "ia":
                    act(lambda e, ps=ps, c0=c0, c1=c1: e.activation(ia[:, c0:c1], ps[0:96, 0:c1 - c0], AF.Copy), [ps.k()], [ia.k()])
                elif nm == "vr":
                    act(lambda e, ps=ps, c0=c0, c1=c1: e.activation(vr[:, c0:c1], ps[0:64, 0:c1 - c0], AF.Copy), [ps.k()], [vr.k()])
                else:
                    gi = 0 if nm == "g0" else 1
                    act(lambda e, ps=ps, c0=c0, c1=c1, gi=gi: e.activation(gt[:, gi, c0:c1], ps[:, 0:c1 - c0], AF.Sigmoid), [ps.k()], [gt.k()])
        mark(f"B{l}L")
        dl2t = sb(st, "dl2t", [96, G], BF16); il2t = sb(st, "il2t", [96, G], BF16); gl2t = sb(st, "gl2t", [128, 2, G], BF16); vl2t = sb(st, "vl2t", [64, G], BF16)
        dmac(lambda e: e.dma_start(out=dl2t[:, :], in_=dl2[l]), [], [dl2t.k()])
        dmac(lambda e: e.dma_start(out=il2t[:, :], in_=il2[l]), [], [il2t.k()])
        dmac(lambda e: e.dma_start(out=gl2t[:, :, :], in_=gl2[l].rearrange("(k p) n -> p k n", p=128)), [], [gl2t.k()])
        if l == 1:
            dmac(lambda e: e.dma_start(out=vl2t[:, :], in_=vl2[0]), [], [vl2t.k()])
        w0bc = sb(st, "w0bc", [128, G])
        dma(lambda e: e.dma_start(out=w0bc[:, :], in_=w0row[l].partition_broadcast(128)), [], [w0bc.k()])
        omr = sb(st, "omr", [128, 24])
        dve(lambda e: e.tensor_scalar(omr[:, :], spt[:, MURKV:MURKV + 24], -1.0, 1.0, ALU.mult, ALU.add), [spt.k()], [omr.k()])
        oka = sb(st, "oka", [128, 8])
        dve(lambda e: e.tensor_scalar(oka[:, :], spt[:, KAc:KAc + 8], -1.0, 1.0, ALU.mult, ALU.add), [spt.k()], [oka.k()])

        wbuf = [sb(st, f"wbuf{i}", [128, KC, 128], BF16) for i in range(3)]
        wctr = [0]

        def load_w(blk):
            wb = wbuf[wctr[0] % 3]; wctr[0] += 1
            dmac(lambda e, wb=wb: e.dma_start(out=wb[:, :, :], in_=w_in[l, blk].rearrange("p (k n) -> p k n", n=128)), [], [wb.k()])
            return wb

        def proj(wb, dst, tiles):
            for (c0, c1) in tiles:
                ps = psm()
                for kc in range(KC):
                    pe(lambda e, ps=ps, kc=kc, c0=c0, c1=c1: e.matmul(ps[:, 0:c1 - c0], wb[:, kc, :], h[:, kc, c0:c1], start=(kc == 0), stop=(kc == KC - 1)), [wb.k(), h.k(kc)], [ps.k()])
                act(lambda e, ps=ps, c0=c0, c1=c1: e.activation(dst[:, c0:c1], ps[:, 0:c1 - c0], AF.Copy), [ps.k()], [dst.k()])

        names = ["Wr", "Wk2", "Wv", "Wkkn", "p_r", "p_k", "p_v", "a_sig", "yt"]
        Wt = {n: sb(st, n, [128, E]) for n in names}
        Wr, Wk2, Wv, Wkkn, p_r, p_k, p_v, a_sig, yt = [Wt[n] for n in names]
        for t_ in (Wr, Wk2, Wv, Wkkn, a_sig, yt, p_r, p_k, p_v):
            dve(lambda e, t_=t_: e.memset(t_[:, :], 0.0), [], [t_.k()])

        def cb(name, shape, dt=F32):
            return sb(st, name, shape, dt)
        thc = cb("thc", [96, 64], BF16); zt = cb("zt", [128, 128]); lw = cb("lw", [128, 128])
        E1 = cb("E1", [128, 128]); E0 = cb("E0", [128, 128]); Ei = cb("Ei", [128, 128])
        AR = cb("AR", [128, 256]); bt = cb("bt", [128, 128]); kt = cb("kt", [128, 128]); vc = cb("vc", [128, 64])
        Bt = cb("Bt", [128, 128]); Kt = cb("Kt", [128, 128]); Vt = cb("Vt", [128, 128])
        Vp = [cb(f"Vp{i}", [128, 128]) for i in range(2)]; Up = [cb(f"Up{i}", [128, 128]) for i in range(2)]
        for t_ in Vp + Up:
            dve(lambda e, t_=t_: e.memset(t_[:, :], 0.0), [], [t_.k()])
        LkA = cb("LkA", [128, 2, 256]); LbA = cb("LbA", [128, 2, 256])
        PP = [cb(f"PP{i}", [128, 4, 128]) for i in range(2)]
        TT = [cb(f"TT{i}", [128, 2, 128]) for i in range(2)]
        At = cb("At", [128, 128]); Gs = cb("Gs", [128, 128]); Gp = [cb(f"Gp{i}", [128, 128]) for i in range(2)]
        for t_ in Gp:
            dve(lambda e, t_=t_: e.memset(t_[:, :], 0.0), [], [t_.k()])
        gam = cb("gam", [128, 8]); Wc = cb("Wc", [128, 128])
        RPv = [stg[0], stg[1]]
        Xs = cb("Xs", [128, 128]); Us = cb("Us", [128, 128]); Y1 = cb("Y1", [128, 64]); tS = cb("tS", [128, 128])
        Sp = cb("Sp", [128, 128])
        Ss = cb("Ss", [128, NSQ, 128]); Sbd = cb("Sbd", [128, 4, 128])
        dve(lambda e: e.memset(Ss[:, :, :], 0.0), [], [Ss.k()])
        onat = cb("onat", [128, NGRP, 64])
        Apg = [cb(f"Apg{i}", [128, 64]) for i in range(2)]; Btg = [cb(f"Btg{i}", [64, 128]) for i in range(2)]; Ktg = [cb(f"Ktg{i}", [64, 128]) for i in range(2)]

        def unit(hp, kind, ci, full, Sget, corr=False):
            if kind == "p":
                C = 128; c0 = 2 + 128 * ci; nd = 6; Gn = 1
                cv = lambda t2: t2[:, c0:c0 + C]
                cm = lambda a: a
                MU = cst[0:128, C_MUP:C_MUP + 256]; ML = cst[0:128, C_MLP:C_MLP + 128]; UT = cst[0:128, C_UTP:C_UTP + 256]
            else:
                C = 64; nd = 1; Gn = NSQ
                cv = lambda t2: sview(t2)[:, :, 2:6]
                cm = lambda a: a.rearrange("p (s t) -> p s t", t=4)
                MU = cst[0:64, C_MUS:C_MUS + 128]; ML = cst[0:64, C_MLS:C_MLS + 64]; UT = cst[0:64, C_UTS:C_UTS + 128]
            hc = slice(hp * 128, (hp + 1) * 128)
            ps = psx()
            if kind == "p":
                pe(lambda e: e.matmul(ps[0:C, 0:128], th[:, c0:c0 + C], dl2t[:, hc], start=True, stop=True), [th.k(), dl2t.k()], [ps.k()])
            else:
                dve(lambda e: e.tensor_copy(cm(thc[:, 0:64]), cv(th[:, :])), [th.k()], [thc.k()])
                pe(lambda e: e.matmul(ps[0:C, 0:128], thc[:, 0:C], dl2t[:, hc], start=True, stop=True), [thc.k(), dl2t.k()], [ps.k()])
            dve(lambda e: e.tensor_tensor(zt[0:C, :], ps[0:C, 0:128], w0bc[0:C, hc], ALU.add), [ps.k(), w0bc.k()], [zt.k()])
            act(lambda e: e.activation(lw[0:C, :], zt[0:C, :], AF.Sigmoid), [zt.k()], [lw.k()])
            ps2 = psx()
            pe(lambda e: e.matmul(ps2[:, 0:2 * C], lw[0:C, :], UT, start=True, stop=True), [lw.k(), cst.k()], [ps2.k()])
            act(lambda e: e.activation(E1[:, 0:C], ps2[:, 0:C], AF.Exp), [ps2.k()], [E1.k()])
            act(lambda e: e.activation(E0[:, 0:C], ps2[:, C:2 * C], AF.Exp), [ps2.k()], [E0.k()])
            act(lambda e: e.activation(Ei[:, 0:C], ps2[:, 0:C], AF.Exp, scale=-1.0), [ps2.k()], [Ei.k()])
            dve(lambda e: e.scalar_tensor_tensor(cm(AR[:, 0:C]), cv(Wkkn[:, :]), -1.0, cm(E0[:, 0:C]), ALU.mult, ALU.mult), [Wkkn.k(), E0.k()], [AR.k()])
            dve(lambda e: e.tensor_tensor(cm(AR[:, C:2 * C]), cv(Wr[:, :]), cm(E1[:, 0:C]), ALU.mult), [Wr.k(), E1.k()], [AR.k()])
            dve(lambda e: e.tensor_tensor(cm(bt[:, 0:C]), cv(Wkkn[:, :]), cv(a_sig[:, :]), ALU.mult), [Wkkn.k(), a_sig.k()], [bt.k()])
            dve(lambda e: e.tensor_tensor(bt[:, 0:C], bt[:, 0:C], Ei[:, 0:C], ALU.mult), [bt.k(), Ei.k()], [bt.k()])
            dve(lambda e: e.tensor_tensor(cm(kt[:, 0:C]), cv(Wk2[:, :]), cm(Ei[:, 0:C]), ALU.mult), [Wk2.k(), Ei.k()], [kt.k()])
            if kind == "p":
                vsrc = Wv[:, c0:c0 + C]; vk = Wv.k()
            else:
                dve(lambda e: e.tensor_copy(cm(vc[:, 0:64]), cv(Wv[:, :])), [Wv.k()], [vc.k()])
                vsrc = vc[:, 0:C]; vk = vc.k()
            ps3 = psx()
            pe(lambda e: e.transpose(ps3[0:C, 0:128], bt[:, 0:C], ident), [bt.k(), cst.k()], [ps3.k()])
            pe(lambda e: e.transpose(ps3[0:C, 128:256], kt[:, 0:C], ident), [kt.k(), cst.k()], [ps3.k()])
            pe(lambda e: e.transpose(ps3[0:C, 256:384], vsrc, ident), [vk, cst.k()], [ps3.k()])
            act(lambda e: e.activation(Bt[0:C, :], ps3[0:C, 0:128], AF.Copy), [ps3.k()], [Bt.k()])
            act(lambda e: e.activation(Kt[0:C, :], ps3[0:C, 128:256], AF.Copy), [ps3.k()], [Kt.k()])
            dve(lambda e: e.tensor_copy(Vt[0:C, :], ps3[0:C, 256:384]), [ps3.k()], [Vt.k()])
            dve(lambda e: e.tensor_copy(Vp[0][0:C, 0:64], ps3[0:C, 256:320]), [ps3.k()], [Vp[0].k()])
            dve(lambda e: e.tensor_copy(Vp[1][0:C, 64:128], ps3[0:C, 320:384]), [ps3.k()], [Vp[1].k()])
            pa = psx(); pb = psx(); pc = psx()
            for hh in range(2):
                pr = slice(64 * hh, 64 * hh + 64)
                pe(lambda e, pr=pr, hh=hh: e.matmul(pa[0:C, 256 * hh:256 * hh + 2 * C], kt[pr, 0:C], AR[pr, 0:2 * C], start=True, stop=True), [kt.k(), AR.k()], [pa.k()])
                pe(lambda e, pr=pr, hh=hh: e.matmul(pb[0:C, 256 * hh:256 * hh + 2 * C], bt[pr, 0:C], AR[pr, 0:2 * C], start=True, stop=True), [bt.k(), AR.k()], [pb.k()])
                pe(lambda e, pr=pr, hh=hh: e.matmul(pc[0:C, 128 * hh:128 * hh + C], AR[pr, 0:C], bt[pr, 0:C], start=True, stop=True), [bt.k(), AR.k()], [pc.k()])
            MUb = MU.unsqueeze(1).to_broadcast([C, 2, 2 * C]); MLb = ML.unsqueeze(1).to_broadcast([C, 2, C])
            v256 = lambda p_: p_[0:C, 0:512].rearrange("p (s n) -> p s n", n=256)[:, :, 0:2 * C]
            v128 = lambda p_, ns: p_[0:C, 0:128 * ns].rearrange("p (s n) -> p s n", n=128)[:, :, 0:C]
            dve(lambda e: e.tensor_tensor(LbA[0:C, :, 0:2 * C], v256(pb), MUb, ALU.mult), [pb.k(), cst.k()], [LbA.k()])
            dve(lambda e: e.tensor_tensor(PP[0][0:C, 0:2, 0:C], v128(pc, 2), MLb, ALU.mult), [pc.k(), cst.k()], [PP[0].k()])
            dve(lambda e: e.tensor_tensor(LkA[0:C, :, 0:2 * C], v256(pa), MUb, ALU.mult), [pa.k(), cst.k()], [LkA.k()])
            dve(lambda e: e.tensor_tensor(TT[0][0:C, :, 0:C], LbA[0:C, :, 0:C], cst[0:C, C_ID:C_ID + C].unsqueeze(1).to_broadcast([C, 2, C]), ALU.add), [LbA.k(), cst.k()], [TT[0].k()])
            cur = 0
            for it in range(1, nd + 1):
                nxt = 1 - cur
                qs = psx()
                for hh in range(2):
                    Pt_ap = LbA[0:C, hh, 0:C] if it == 1 else PP[cur][0:C, 2 + hh, 0:C]
                    Pt_k = LbA.k() if it == 1 else PP[cur].k()
                    pe(lambda e, hh=hh, Pt_ap=Pt_ap, cur=cur: e.matmul(qs[0:C, 128 * hh:128 * hh + C], Pt_ap, PP[cur][0:C, hh, 0:C], start=True, stop=True), [Pt_k, PP[cur].k()], [qs.k()])
                if it < nd:
                    for hh in range(2):
                        Pt_ap = LbA[0:C, hh, 0:C] if it == 1 else PP[cur][0:C, 2 + hh, 0:C]
                        Pt_k = LbA.k() if it == 1 else PP[cur].k()
                        pe(lambda e, hh=hh, Pt_ap=Pt_ap, cur=cur: e.matmul(qs[0:C, 256 + 128 * hh:256 + 128 * hh + C], PP[cur][0:C, hh, 0:C], Pt_ap, start=True, stop=True), [Pt_k, PP[cur].k()], [qs.k()])
                if it > 1:
                    qt = psx()
                    ti = (it - 2) % 2
                    for hh in range(2):
                        pe(lambda e, hh=hh, cur=cur, ti=ti: e.matmul(qt[0:C, 128 * hh:128 * hh + C], PP[cur][0:C, hh, 0:C], TT[ti][0:C, hh, 0:C], start=True, stop=True), [PP[cur].k(), TT[ti].k()], [qt.k()])
                    dve(lambda e, qt=qt, ti=ti: e.tensor_tensor(TT[1 - ti][0:C, :, 0:C], v128(qt, 2), TT[ti][0:C, :, 0:C], ALU.add), [qt.k(), TT[ti].k()], [TT[1 - ti].k()])
                nsl = 4 if it < nd else 2
                act(lambda e, qs=qs, nxt=nxt, nsl=nsl: e.activation(PP[nxt][0:C, 0:nsl, 0:C], v128(qs, nsl), AF.Copy), [qs.k()], [PP[nxt].k()])
                cur = nxt
            qt = psx()
            ti = (nd - 1) % 2
            for hh in range(2):
                pe(lambda e, hh=hh, cur=cur, ti=ti: e.matmul(qt[0:C, 128 * hh:128 * hh + C], PP[cur][0:C, hh, 0:C], TT[ti][0:C, hh, 0:C], start=True, stop=True), [PP[cur].k(), TT[ti].k()], [qt.k()])
            TF = TT[1 - ti]
            dve(lambda e: e.tensor_tensor(TF[0:C, :, 0:C], v128(qt, 2), TT[ti][0:C, :, 0:C], ALU.add), [qt.k(), TT[ti].k()], [TF.k()])
            if corr:
                pat = psx()
                pe(lambda e: e.transpose(pat[0:C, 0:128], AR[:, 0:C], ident), [AR.k(), cst.k()], [pat.k()])
                act(lambda e: e.activation(At[0:C, :], pat[0:C, 0:128], AF.Copy), [pat.k()], [At.k()])
                pg = psx()
                for hh in range(2):
                    pe(lambda e, hh=hh: e.matmul(pg[0:C, 64 * hh:64 * hh + 64], TF[0:C, hh, 0:C], At[0:C, 64 * hh:64 * hh + 64], start=True, stop=True), [TF.k(), At.k()], [pg.k()])
                dve(lambda e: e.tensor_copy(Gs[0:C, :], pg[0:C, 0:128]), [pg.k()], [Gs.k()])
                act(lambda e: e.activation(Gp[0][0:C, 0:64], pg[0:C, 0:64], AF.Copy), [pg.k()], [Gp[0].k()])
                act(lambda e: e.activation(Gp[1][0:C, 64:128], pg[0:C, 64:128], AF.Copy), [pg.k()], [Gp[1].k()])
                prp = psx()
                for hh in range(2):
                    pe(lambda e, hh=hh: e.matmul(prp[:, 0:C], Gp[hh][0:C, :], LbA[0:C, hh, C:2 * C], start=(hh == 0), stop=(hh == 1)), [Gp[hh].k(), LbA.k()], [prp.k()])
                rpt = RPv[ci // 4]; rc0 = (ci % 4) * 128
                dve(lambda e: e.tensor_tensor(rpt[:, rc0:rc0 + C], prp[:, 0:C], AR[:, C:2 * C], ALU.add), [prp.k(), AR.k()], [rpt.k()])
                pnt = psx()
                pe(lambda e: e.matmul(pnt[:, 0:128], Gs[0:C, :], Bt[0:C, :], start=True, stop=True), [Gs.k(), Bt.k()], [pnt.k()])
                dve(lambda e: e.tensor_tensor(onat[:, :, :].rearrange("p g j -> p (g j)")[:, ci * 128:(ci + 1) * 128], pnt[:, 0:128], bd, ALU.mult), [pnt.k(), cst.k()], [onat.k()])
                dve(lambda e: e.tensor_copy(gam[:, ci:ci + 1], E1[:, C - 1:C]), [E1.k()], [gam.k()])
            px = psx()
            for g in range(Gn):
                Sg_ap, Sg_k = Sget(g)
                if kind == "p":
                    lhs = AR[:, 0:C]; lk = AR.k()
                else:
                    ap_ = Apg[g % 2]
                    dve(lambda e, ap_=ap_, g=g: e.tensor_tensor(ap_[:, :], AR[:, 0:64], cst[:, C_CMS + g * 64:C_CMS + (g + 1) * 64], ALU.mult), [AR.k(), cst.k()], [ap_.k()])
                    lhs = ap_[:, :]; lk = ap_.k()
                pe(lambda e, lhs=lhs, Sg_ap=Sg_ap, g=g: e.matmul(px[0:C, 0:128], lhs, Sg_ap, start=(g == 0), stop=(kind == "s" and g == Gn - 1)), [lk, Sg_k], [px.k()])
            if kind == "p":
                for hh in range(2):
                    pe(lambda e, hh=hh: e.matmul(px[0:C, 0:128], LkA[0:C, hh, 0:C], Vp[hh][0:C, :], start=False, stop=(hh == 1)), [LkA.k(), Vp[hh].k()], [px.k()])
                dve(lambda e: e.tensor_copy(Xs[0:C, :], px[0:C, 0:128]), [px.k()], [Xs.k()])
            else:
                px2 = psx()
                for hh in range(2):
                    pe(lambda e, hh=hh: e.matmul(px2[0:C, 0:128], LkA[0:C, hh, 0:C], Vp[hh][0:C, :], start=(hh == 0), stop=(hh == 1)), [LkA.k(), Vp[hh].k()], [px2.k()])
                dve(lambda e: e.tensor_copy(Xs[0:C, :], px[0:C, 0:128]), [px.k()], [Xs.k()])
                dve(lambda e: e.tensor_tensor(Xs[0:C, :], Xs[0:C, :], px2[0:C, 0:128], ALU.add), [px2.k(), Xs.k()], [Xs.k()])
            pu = psx()
            for hh in range(2):
                pe(lambda e, hh=hh: e.matmul(pu[0:C, 64 * hh:64 * hh + 64], TF[0:C, hh, 0:C], Xs[0:C, 64 * hh:64 * hh + 64], start=True, stop=True), [TF.k(), Xs.k()], [pu.k()])
            dve(lambda e: e.tensor_copy(Us[0:C, :], pu[0:C, 0:128]), [pu.k()], [Us.k()])
            if full:
                act(lambda e: e.activation(Up[0][0:C, 0:64], pu[0:C, 0:64], AF.Copy), [pu.k()], [Up[0].k()])
                act(lambda e: e.activation(Up[1][0:C, 64:128], pu[0:C, 64:128], AF.Copy), [pu.k()], [Up[1].k()])
                py = psx()
                for hh in range(2):
                    pe(lambda e, hh=hh: e.matmul(py[:, 0:C], Up[hh][0:C, :], LbA[0:C, hh, C:2 * C], start=(hh == 0), stop=False), [Up[hh].k(), LbA.k()], [py.k()])
                    pe(lambda e, hh=hh: e.matmul(py[:, 0:C], Vp[hh][0:C, :], LkA[0:C, hh, C:2 * C], start=False, stop=(hh == 1 and kind == "s")), [Vp[hh].k(), LkA.k()], [py.k()])
                if kind == "p":
                    Sg_ap, Sg_k = Sget(0)
                    pe(lambda e, Sg_ap=Sg_ap: e.matmul(py[:, 0:C], Sg_ap, AR[:, C:2 * C], start=False, stop=True), [Sg_k, AR.k()], [py.k()])
                    act(lambda e: e.activation(yt[:, c0:c0 + C], py[:, 0:C], AF.Copy), [py.k()], [yt.k()])
                else:
                    py1 = psx()
                    for g in range(Gn):
                        Sg_ap, Sg_k = Sget(g)
                        pe(lambda e, Sg_ap=Sg_ap, g=g: e.matmul(py1[:, 4 * g:4 * g + 4], Sg_ap, AR[:, C + 4 * g:C + 4 * g + 4], start=True, stop=True), [Sg_k, AR.k()], [py1.k()])
                    act(lambda e: e.activation(Y1[:, 0:64], py1[:, 0:64], AF.Copy), [py1.k()], [Y1.k()])
                    dve(lambda e: e.tensor_tensor(cv(yt[:, :]), cm(py[:, 0:64]), cm(Y1[:, 0:64]), ALU.add), [py.k(), Y1.k()], [yt.k()])
            for g in range(Gn):
                Sg_ap, Sg_k = Sget(g)
                pq = psx()
                if kind == "p":
                    lb_ = Bt[0:C, :]; lk_ = Kt[0:C, :]; kb = Bt.k(); kk_ = Kt.k()
                else:
                    bg_ = Btg[g % 2]; kg_ = Ktg[g % 2]
                    dve(lambda e, bg_=bg_, g=g: e.tensor_scalar(bg_[:, :], Bt[0:64, :], cst[0:64, C_RMS + g:C_RMS + g + 1], None, ALU.mult), [Bt.k(), cst.k()], [bg_.k()])
                    dve(lambda e, kg_=kg_, g=g: e.tensor_scalar(kg_[:, :], Kt[0:64, :], cst[0:64, C_RMS + g:C_RMS + g + 1], None, ALU.mult), [Kt.k(), cst.k()], [kg_.k()])
                    lb_ = bg_[:, :]; lk_ = kg_[:, :]; kb = bg_.k(); kk_ = kg_.k()
                pe(lambda e, pq=pq, lb_=lb_: e.matmul(pq[:, 0:128], lb_, Us[0:C, :], start=True, stop=False), [kb, Us.k()], [pq.k()])
                pe(lambda e, pq=pq, lk_=lk_: e.matmul(pq[:, 0:128], lk_, Vt[0:C, :], start=False, stop=True), [kk_, Vt.k()], [pq.k()])
                dve(lambda e, pq=pq: e.tensor_tensor(tS[:, :], pq[:, 0:128], bd, ALU.mult), [pq.k(), cst.k()], [tS.k()])
                dve(lambda e, Sg_ap=Sg_ap: e.tensor_tensor(Sg_ap, Sg_ap, tS[:, :], ALU.add), [Sg_k, tS.k()], [Sg_k])
                gcol = (C - 1) if kind == "p" else (4 * g + 3)
                dve(lambda e, Sg_ap=Sg_ap, gcol=gcol: e.tensor_scalar(Sg_ap, Sg_ap, E1[:, gcol:gcol + 1], None, ALU.mult), [Sg_k, E1.k()], [Sg_k])

        def state_out(hp, S_ap, S_k, gi):
            act(lambda e: e.activation(onat[0:64, gi, :], S_ap[0:64, 0:64], AF.Copy), [S_k], [onat.k()])
            dve(lambda e: e.tensor_copy(onat[64:128, gi, :], S_ap[64:128, 64:128]), [S_k], [onat.k()])

        for hp in range(8):
            hc = slice(hp * 128, (hp + 1) * 128)
            wr = load_w(hp)
            wk = load_w(8 + hp)
            wv = load_w(16 + hp)
            proj(wr, p_r, RKV_T); proj(wk, p_k, RKV_T); proj(wv, p_v, RKV_T)
            for (c0, c1) in TA:
                ps = psm()
                pe(lambda e, ps=ps, c0=c0, c1=c1: e.matmul(ps[:, 0:c1 - c0], il2t[:, hc], ia[:, c0:c1], start=True, stop=True), [il2t.k(), ia.k()], [ps.k()])
                act(lambda e, ps=ps, c0=c0, c1=c1: e.activation(a_sig[:, c0:c1], ps[:, 0:c1 - c0], AF.Sigmoid, bias=spt[:, A0 + hp:A0 + hp + 1]), [ps.k(), spt.k()], [a_sig.k()])
            for i, (p_, dst) in enumerate(((p_r, Wr), (p_k, Wk2), (p_v, Wv))):
                mc = MURKV + i * 8 + hp
                dve(lambda e, p_=p_, i=i: e.tensor_scalar(Wkkn[:, 2:E], p_[:, 2:E], omr[:, i * 8 + hp:i * 8 + hp + 1], None, ALU.mult), [p_.k(), omr.k()], [Wkkn.k()])
                dve(lambda e, p_=p_, dst=dst, mc=mc: e.scalar_tensor_tensor(dst[:, 2:E], p_[:, 1:E - 1], spt[:, mc:mc + 1], Wkkn[:, 2:E], ALU.mult, ALU.add), [p_.k(), spt.k(), Wkkn.k()], [dst.k()])
            if l == 0:
                dma(lambda e: e.dma_start(out=vf_d[hp], in_=Wv[:, :]), [Wv.k()], [("vf_d", hp)])
            else:
                dma(lambda e: e.dma_start(out=p_v[:, :], in_=vf_d[hp]), [("vf_d", hp)], [p_v.k()])
                for (c0, c1) in TA:
                    ps = psm()
                    pe(lambda e, ps=ps, c0=c0, c1=c1: e.matmul(ps[:, 0:c1 - c0], vl2t[:, hc], vr[:, c0:c1], start=True, stop=True), [vl2t.k(), vr.k()], [ps.k()])
                    act(lambda e, ps=ps, c0=c0, c1=c1: e.activation(p_r[:, c0:c1], ps[:, 0:c1 - c0], AF.Sigmoid, bias=spt[:, V0 + hp:V0 + hp + 1]), [ps.k(), spt.k()], [p_r.k()])
                dve(lambda e: e.tensor_tensor(p_v[:, 1:E], p_v[:, 1:E], Wv[:, 1:E], ALU.subtract), [p_v.k(), Wv.k()], [p_v.k()])
                dve(lambda e: e.tensor_tensor(p_v[:, 1:E], p_v[:, 1:E], p_r[:, 1:E], ALU.mult), [p_v.k(), p_r.k()], [p_v.k()])
                dve(lambda e: e.tensor_tensor(Wv[:, 1:E], Wv[:, 1:E], p_v[:, 1:E], ALU.add), [p_v.k(), Wv.k()], [Wv.k()])
            dve(lambda e: e.tensor_scalar(Wkkn[:, :], Wk2[:, :], spt[:, KKc + hp:KKc + hp + 1], None, ALU.mult), [Wk2.k(), spt.k()], [Wkkn.k()])
            dve(lambda e: e.tensor_tensor(p_k[:, :], Wkkn[:, :], Wkkn[:, :], ALU.mult), [Wkkn.k()], [p_k.k()])
            for (c0, c1) in TB:
                ps = psm()
                pe(lambda e, ps=ps, c0=c0, c1=c1: e.matmul(ps[:, 0:c1 - c0], bd, p_k[:, c0:c1], start=True, stop=True), [cst.k(), p_k.k()], [ps.k()])
                dve(lambda e, ps=ps, c0=c0, c1=c1: e.tensor_scalar(p_v[:, c0:c1], ps[:, 0:c1 - c0], 1e-24, None, ALU.max), [ps.k()], [p_v.k()])
            act(lambda e: e.activation(p_v[:, :], p_v[:, :], AF.Sqrt), [p_v.k()], [p_v.k()])
            dve(lambda e: e.reciprocal(p_k[:, :], p_v[:, :]), [p_v.k()], [p_k.k()])
            dve(lambda e: e.tensor_tensor(Wkkn[:, :], Wkkn[:, :], p_k[:, :], ALU.mult), [Wkkn.k(), p_k.k()], [Wkkn.k()])
            dve(lambda e: e.tensor_scalar(p_k[:, :], a_sig[:, :], spt[:, KAc + hp:KAc + hp + 1], oka[:, hp:hp + 1], ALU.mult, ALU.add), [a_sig.k(), spt.k(), oka.k()], [p_k.k()])
            dve(lambda e: e.tensor_tensor(Wk2[:, :], Wk2[:, :], p_k[:, :], ALU.mult), [Wk2.k(), p_k.k()], [Wk2.k()])
            mark(f"B{l}pre{hp}")
            dve(lambda e: e.memset(Sp[:, :], 0.0), [], [Sp.k()])
            for ci in range(8):
                unit(hp, "p", ci, True, lambda g: (Sp[:, :], Sp.k()), corr=True)
            xi = l * 8 + hp
            exchange(None, None, Sp, Sp[:, :], Wc, Wc[:, :], cin_s[xi], cout_s[xi], None)
            dve(lambda e: e.tensor_scalar(Wc[:, :], Wc[:, :], flag[:, 0:1], None, ALU.mult), [Wc.k(), flag.k()], [Wc.k()])
            for ci in range(8):
                c0_ = 2 + 128 * ci
                rpt = RPv[ci // 4]; rc0 = (ci % 4) * 128
                pyc = psx()
                pe(lambda e, pyc=pyc, rpt=rpt, rc0=rc0: e.matmul(pyc[:, 0:128], Wc[:, :], rpt[:, rc0:rc0 + 128], start=True, stop=True), [Wc.k(), rpt.k()], [pyc.k()])
                dve(lambda e, pyc=pyc, c0_=c0_: e.tensor_tensor(yt[:, c0_:c0_ + 128], yt[:, c0_:c0_ + 128], pyc[:, 0:128], ALU.add), [pyc.k(), yt.k()], [yt.k()])
                pw = psx()
                pe(lambda e, pw=pw, ci=ci: e.matmul(pw[:, 0:128], onat[:, :, :].rearrange("p g j -> p (g j)")[:, ci * 128:(ci + 1) * 128], Wc[:, :], start=True, stop=True), [onat.k(), Wc.k()], [pw.k()])
                dve(lambda e, pw=pw: e.tensor_tensor(Wc[:, :], Wc[:, :], pw[:, 0:128], ALU.add), [pw.k(), Wc.k()], [Wc.k()])
                dve(lambda e, ci=ci: e.tensor_scalar(Wc[:, :], Wc[:, :], gam[:, ci:ci + 1], None, ALU.mult), [Wc.k(), gam.k()], [Wc.k()])
            dve(lambda e: e.tensor_tensor(Sp[:, :], Sp[:, :], Wc[:, :], ALU.add), [Sp.k(), Wc.k()], [Sp.k()])
            mark(f"B{l}p2x_{hp}")
            state_out(hp, Sp[:, :], Sp.k(), 0)
            mark(f"B{l}p2_{hp}")
            dma(lambda e: e.dma_start(out=Ss[0:64, :, 0:64], in_=wkv_in[l, :, 2 * hp].rearrange("s j i -> j s i")), [], [Ss.k()])
            dma(lambda e: e.dma_start(out=Ss[64:128, :, 64:128], in_=wkv_in[l, :, 2 * hp + 1].rearrange("s j i -> j s i")), [], [Ss.k()])
            unit(hp, "s", 0, True, lambda g: (Ss[:, g, :], Ss.k()))
            for g in range(NSQ):
                state_out(hp, Ss[:, g, :], Ss.k(), 1 + g)
            dma(lambda e: e.dma_start(out=wkvo[l, :, 2 * hp:2 * hp + 2].rearrange("g h j i -> (h j) g i"), in_=onat[:, :, :]), [onat.k()], [("wkvo", (l, hp))])
            mark(f"B{l}s_{hp}")
            dve(lambda e: e.tensor_tensor(p_k[:, :], yt[:, :], yt[:, :], ALU.mult), [yt.k()], [p_k.k()])
            for (c0, c1) in TB:
                ps = psm()
                pe(lambda e, ps=ps, c0=c0, c1=c1: e.matmul(ps[:, 0:c1 - c0], bd, yt[:, c0:c1], start=True, stop=True), [cst.k(), yt.k()], [ps.k()])
                act(lambda e, ps=ps, c0=c0, c1=c1: e.activation(p_v[:, c0:c1], ps[:, 0:c1 - c0], AF.Copy, scale=1.0 / 64), [ps.k()], [p_v.k()])
                ps2 = psm()
                pe(lambda e, ps2=ps2, c0=c0, c1=c1: e.matmul(ps2[:, 0:c1 - c0], bd, p_k[:, c0:c1], start=True, stop=True), [cst.k(), p_k.k()], [ps2.k()])
                act(lambda e, ps2=ps2, c0=c0, c1=c1: e.activation(p_r[:, c0:c1], ps2[:, 0:c1 - c0], AF.Copy, scale=1.0 / 64), [ps2.k()], [p_r.k()])
            dve(lambda e: e.tensor_tensor(p_k[:, :], p_v[:, :], p_v[:, :], ALU.mult), [p_v.k()], [p_k.k()])
            dve(lambda e: e.tensor_tensor(p_r[:, :], p_r[:, :], p_k[:, :], ALU.subtract), [p_r.k(), p_k.k()], [p_r.k()])
            dve(lambda e: e.tensor_scalar(p_r[:, :], p_r[:, :], 0.0, None, ALU.max), [p_r.k()], [p_r.k()])
            act(lambda e: e.activation(p_r[:, :], p_r[:, :], AF.Sqrt, bias=eps_t[:, 1:2], scale=1.0), [p_r.k(), eps_t.k()], [p_r.k()])
            dve(lambda e: e.reciprocal(p_k[:, :], p_r[:, :]), [p_r.k()], [p_k.k()])
            dve(lambda e: e.tensor_tensor(yt[:, :], yt[:, :], p_v[:, :], ALU.subtract), [yt.k(), p_v.k()], [yt.k()])
            dve(lambda e: e.tensor_tensor(yt[:, :], yt[:, :], p_k[:, :], ALU.mult), [yt.k(), p_k.k()], [yt.k()])
            dve(lambda e: e.tensor_scalar(yt[:, :], yt[:, :], spt[:, LNW + hp:LNW + hp + 1], spt[:, LNB + hp:LNB + hp + 1], ALU.mult, ALU.add), [yt.k(), spt.k()], [yt.k()])
            dve(lambda e: e.scalar_tensor_tensor(p_k[:, :], Wr[:, :], spt[:, RKc + hp:RKc + hp + 1], Wk2[:, :], ALU.mult, ALU.mult), [Wr.k(), Wk2.k(), spt.k()], [p_k.k()])
            for (c0, c1) in TB:
                ps = psm()
                pe(lambda e, ps=ps, c0=c0, c1=c1: e.matmul(ps[:, 0:c1 - c0], bd, p_k[:, c0:c1], start=True, stop=True), [cst.k(), p_k.k()], [ps.k()])
                dve(lambda e, ps=ps, c0=c0, c1=c1: e.tensor_tensor(p_v[:, c0:c1], ps[:, 0:c1 - c0], Wv[:, c0:c1], ALU.mult), [ps.k(), Wv.k()], [p_v.k()])
            dve(lambda e: e.tensor_tensor(yt[:, :], yt[:, :], p_v[:, :], ALU.add), [yt.k(), p_v.k()], [yt.k()])
            for (c0, c1) in TA:
                ps = psm()
                for k2 in range(2):
                    pe(lambda e, ps=ps, c0=c0, c1=c1, k2=k2: e.matmul(ps[:, 0:c1 - c0], gl2t[:, k2, hc], gt[:, k2, c0:c1], start=(k2 == 0), stop=(k2 == 1)), [gl2t.k(), gt.k()], [ps.k()])
                dve(lambda e, ps=ps, c0=c0, c1=c1: e.tensor_tensor(o3[:, hp, c0:c1], ps[:, 0:c1 - c0], yt[:, c0:c1], ALU.mult), [ps.k(), yt.k()], [og.k(hp)])
            mark(f"B{l}hp{hp}")
        for fc in range(8):
            wbg = load_w(24 + fc)
            wcg = load_w(32 + fc)
            whc = load_w(40 + fc)
            proj(wbg, p_r, TB); proj(wcg, p_k, TB); proj(whc, p_v, TB)
            dve(lambda e: e.tensor_tensor(yt[:, :], p_k[:, :], p_v[:, :], ALU.mult), [p_k.k(), p_v.k()], [yt.k()])
            dma(lambda e: e.dma_start(out=sview(yt[:, :])[:, :, 0:2], in_=convT[l, fc * 128:(fc + 1) * 128]), [], [yt.k()])
            cw = CW + fc * 3
            dve(lambda e: e.tensor_scalar(a_sig[:, 2:E], yt[:, 0:E - 2], spt[:, cw:cw + 1], None, ALU.mult), [yt.k(), spt.k()], [a_sig.k()])
            dve(lambda e: e.scalar_tensor_tensor(a_sig[:, 2:E], yt[:, 1:E - 1], spt[:, cw + 1:cw + 2], a_sig[:, 2:E], ALU.mult, ALU.add), [yt.k(), spt.k(), a_sig.k()], [a_sig.k()])
            dve(lambda e: e.scalar_tensor_tensor(a_sig[:, 2:E], yt[:, 2:E], spt[:, cw + 2:cw + 3], a_sig[:, 2:E], ALU.mult, ALU.add), [yt.k(), spt.k(), a_sig.k()], [a_sig.k()])
            dve(lambda e: e.tensor_tensor(o3[:, 8 + fc, 2:E], a_sig[:, 2:E], p_r[:, 2:E], ALU.mult), [a_sig.k(), p_r.k()], [og.k(8 + fc)])
            dma(lambda e: e.dma_start(out=convo[l, fc * 128:(fc + 1) * 128, 0, :], in_=yt[:, SB0 - 2:SB0]), [yt.k()], [("convo", (l, fc, 0))])
            dma(lambda e: e.dma_start(out=convo[l, fc * 128:(fc + 1) * 128, 1:NGRP, :], in_=sview(yt[:, :])[:, :, 4:6]), [yt.k()], [("convo", (l, fc, 1))])

    def phase_CD(l, st, x, md):
        fub = [sb(st, f"fub{i}", [128, KC, 128], BF16) for i in range(4)]
        wob = fub[0:2]
        ua = sb(st, "ua", [128, E]); ub = sb(st, "ub", [128, E]); ca = sb(st, "ca", [128, E]); cbb = sb(st, "cbb", [128, E])
        for fo in range(KC):
            wb = wob[fo % 2]
            dmac(lambda e, wb=wb, fo=fo: e.dma_start(out=wb[:, :, :], in_=w_out[l, fo].rearrange("p (k n) -> p k n", n=128)), [], [wb.k()])
            for (c0, c1) in TB:
                ps = psm()
                for kc in range(KC):
                    pe(lambda e, ps=ps, wb=wb, kc=kc, c0=c0, c1=c1: e.matmul(ps[:, 0:c1 - c0], wb[:, kc, :], o3[:, kc, c0:c1], start=(kc == 0), stop=(kc == KC - 1)), [wb.k(), og.k(kc)], [ps.k()])
                resid(x, fo, ps, c0, c1, md["GA1"], st)
        hl2 = None
        norm_mod(st, x, md["A2"], md["B2"], None, None, "b", pre=(ua, ub, [ca, cbb]))
        mark(f"C{l}")
        S.barrier()
        halo_exchange(st, 2 * l + 1)
        g3 = og[:, 0:15 * E].rearrange("p (k e) -> p k e", e=E)
        fdb = [sb(st, f"fdb{i}", [128, 15, 128], BF16) for i in range(2)]
        fctr = [0]
        hb0 = 0
        for nh in (15, 15, 14):
            for hbl in range(nh):
                hb = hb0 + hbl
                res = []
                for half, dst, cdst in ((0, ua, ca), (1, ub, cbb)):
                    blk = half * 44 + hb
                    wb = fub[fctr[0] % 4]; fctr[0] += 1
                    dmac(lambda e, wb=wb, blk=blk: e.dma_start(out=wb[:, :, :], in_=ffn_up[l, blk].rearrange("p (k n) -> p k n", n=128)), [], [wb.k()])
                    for (c0, c1) in TB:
                        ps = psm()
                        for kc in range(KC):
                            pe(lambda e, ps=ps, wb=wb, kc=kc, c0=c0, c1=c1: e.matmul(ps[:, 0:c1 - c0], wb[:, kc, :], h[:, kc, c0:c1], start=(kc == 0), stop=(kc == KC - 1)), [wb.k(), h.k(kc)], [ps.k()])
                        act(lambda e, ps=ps, c0=c0, c1=c1, dst=dst: e.activation(dst[:, c0:c1], ps[:, 0:c1 - c0], AF.Copy), [ps.k()], [dst.k()])
                    dma(lambda e, dst=dst, blk=blk: e.dma_start(out=sview(dst[:, :])[:, :, 0:2], in_=ffnT[l, blk * 128:(blk + 1) * 128]), [], [dst.k()])
                    fw = FCW + blk * 3
                    dve(lambda e, dst=dst, cdst=cdst, fw=fw: e.tensor_scalar(cdst[:, 2:E], dst[:, 0:E - 2], spt[:, fw:fw + 1], None, ALU.mult), [dst.k(), spt.k()], [cdst.k()])
                    dve(lambda e, dst=dst, cdst=cdst, fw=fw: e.scalar_tensor_tensor(cdst[:, 2:E], dst[:, 1:E - 1], spt[:, fw + 1:fw + 2], cdst[:, 2:E], ALU.mult, ALU.add), [dst.k(), spt.k(), cdst.k()], [cdst.k()])
                    dve(lambda e, dst=dst, cdst=cdst, fw=fw: e.scalar_tensor_tensor(cdst[:, 2:E], dst[:, 2:E], spt[:, fw + 2:fw + 3], cdst[:, 2:E], ALU.mult, ALU.add), [dst.k(), spt.k(), cdst.k()], [cdst.k()])
                    dma(lambda e, dst=dst, blk=blk: e.dma_start(out=ffno[l, blk * 128:(blk + 1) * 128, 0, :], in_=dst[:, SB0 - 2:SB0]), [dst.k()], [("ffno", (l, blk, 0))])
                    dma(lambda e, dst=dst, blk=blk: e.dma_start(out=ffno[l, blk * 128:(blk + 1) * 128, 1:NGRP, :], in_=sview(dst[:, :])[:, :, 4:6]), [dst.k()], [("ffno", (l, blk, 1))])
                act(lambda e: e.activation(ua[:, 2:E], ca[:, 2:E], AF.Silu), [ca.k()], [ua.k()])
                dve(lambda e, hbl=hbl: e.tensor_tensor(g3[:, hbl, 2:E], ua[:, 2:E], cbb[:, 2:E], ALU.mult), [ua.k(), cbb.k()], [og.k(hbl)])
            for fo in range(KC):
                wb = fdb[fo % 2]
                dmac(lambda e, wb=wb, fo=fo, nh=nh, hb0=hb0: e.dma_start(out=wb[:, 0:nh, :], in_=ffn_down[l, fo, :, hb0 * 128:(hb0 + nh) * 128].rearrange("p (k n) -> p k n", n=128)), [], [wb.k()])
                for (c0, c1) in [(2, 376), (376, 750), (750, E)]:
                    ps = psm()
                    for kc in range(nh):
                        pe(lambda e, ps=ps, wb=wb, kc=kc, c0=c0, c1=c1, nh=nh: e.matmul(ps[:, 0:c1 - c0], wb[:, kc, :], g3[:, kc, c0:c1], start=(kc == 0), stop=(kc == nh - 1)), [wb.k(), og.k(kc)], [ps.k()])
                    resid(x, fo, ps, c0, c1, md["GA2"], st)
            hb0 += nh
        return ua, ub, ca, cbb

    rs_tmp = {}

    def resid(x, fo, ps, c0, c1, GA, st):
        p1 = min(c1, SB0)
        if c0 < p1:
            dve(lambda e: e.scalar_tensor_tensor(x[:, fo, c0:p1], ps[:, 0:p1 - c0], GA[:, fo, 0:1], x[:, fo, c0:p1], ALU.mult, ALU.add), [ps.k(), GA.k(), x.k(fo)], [x.k(fo)])
        if c1 > SB0:
            s0 = max(c0, SB0)
            assert s0 == SB0 and c1 == E
            key = id(st)
            if key not in rs_tmp:
                rs_tmp[key] = sb(st, "rs_tmp", [128, 96])
            rt = rs_tmp[key]
            off = SB0 - c0
            dve(lambda e: e.tensor_tensor(rt[:, :].rearrange("p (s t) -> p s t", t=6), ps[:, off:off + 96].rearrange("p (s t) -> p s t", t=6), GA[:, fo, 1:NGRP].unsqueeze(2).to_broadcast([128, NSQ, 6]), ALU.mult), [ps.k(), GA.k()], [rt.k()])
            dve(lambda e: e.tensor_tensor(x[:, fo, SB0:E], x[:, fo, SB0:E], rt[:, :], ALU.add), [rt.k(), x.k(fo)], [x.k(fo)])

    def main_program():
        mark("init")
        for l in range(2):
            with ExitStack() as stA:
                x = sb(stA, f"xA{l}", [128, KC, E])
                if l == 0:
                    dma(lambda e: e.dma_start(out=x[:, :, :], in_=xT.rearrange("(k p) n -> p k n", p=128)), [], [x.k()])
                else:
                    dma(lambda e: e.dma_start(out=x[:, :, :], in_=x_sp.rearrange("(k p) n -> p k n", p=128)), [("x_sp", None)], [x.k()])
                with ExitStack() as st:
                    md = phase_A(l, st, x, l == 0)
                    hl = sb(st, "hl", [128, KC, NGRP])
                    norm_mod(st, x, md["A1"], md["B1"], None, hl, "a")
                    dma(lambda e: e.dma_start(out=shifto[l].rearrange("(k p) s -> p k s", p=128), in_=hl[:, :, :]), [hl.k()], [("shifto", l)])
                    sst = sb(st, "sst", [128, KC, NSQ])
                    dma(lambda e: e.dma_start(out=sst[:, :, :], in_=shiftT[l].rearrange("(k p) s -> p k s", p=128)), [], [sst.k()])
                    for fc in range(KC):
                        dve(lambda e, fc=fc: e.tensor_copy(sview(h[:, fc, :])[:, :, 1], sst[:, fc, :]), [sst.k()], [h.k(fc)])
                    mark(f"A{l}n")
                    halo_exchange(st, 2 * l)
                    mark(f"A{l}")
                    dma(lambda e: e.dma_start(out=x_sp.rearrange("(k p) n -> p k n", p=128), in_=x[:, :, :]), [x.k()], [("x_sp", None)])
                    keep = {}
                    S.barrier()
                    mdk = {}
                    for nm in ("GA1", "A2", "B2", "GA2"):
                        mdk[nm] = md[nm]
                    for nm in ("GA1", "A2", "B2", "GA2"):
                        t = keep_tiles[nm]
                        dve(lambda e, t=t, nm=nm: e.tensor_copy(t[:, :, :], md[nm][:, :, :]), [md[nm].k()], [t.k()])
                        keep[nm] = t
                    S.barrier()
            with ExitStack() as stB:
                phase_B(l, stB)
                mark(f"B{l}")
                S.barrier()
            with ExitStack() as stC:
                x = sb(stC, f"xC{l}", [128, KC, E])
                dma(lambda e: e.dma_start(out=x[:, :, :], in_=x_sp.rearrange("(k p) n -> p k n", p=128)), [("x_sp", None)], [x.k()])
                fua, fub_, fca, fcb = phase_CD(l, stC, x, keep)
                if l == 0:
                    dma(lambda e: e.dma_start(out=x_sp.rearrange("(k p) n -> p k n", p=128), in_=x[:, :, :]), [x.k()], [("x_sp", None)])
                else:
                    load_fin = None
                    rstd = fua; tmp = fub_; sq = [fca, fcb]
                    n = 0
                    for (c0, c1) in TB:
                        ps = psm()
                        for fc in range(KC):
                            q = sq[n % 2]; n += 1
                            act(lambda e, q=q, fc=fc, c0=c0, c1=c1: e.activation(q[:, 0:c1 - c0], x[:, fc, c0:c1], AF.Square), [x.k(fc)], [q.k()])
                            pe(lambda e, ps=ps, q=q, fc=fc, c0=c0, c1=c1: e.matmul(ps[:, 0:c1 - c0], ones, q[:, 0:c1 - c0], start=(fc == 0), stop=(fc == KC - 1)), [q.k(), cst.k()], [ps.k()])
                        act(lambda e, ps=ps, c0=c0, c1=c1: e.activation(tmp[:, c0:c1], ps[:, 0:c1 - c0], AF.Sqrt, bias=eps_t[:, 0:1], scale=1.0 / D), [ps.k(), eps_t.k()], [tmp.k()])
                    dve(lambda e: e.reciprocal(rstd[:, :], tmp[:, :]), [tmp.k()], [rstd.k()])
                    for fc in range(KC):
                        dve(lambda e, fc=fc: e.scalar_tensor_tensor(x[:, fc, :], x[:, fc, :], spt[:, FNG + fc:FNG + fc + 1], rstd[:, :], ALU.mult, ALU.mult), [x.k(fc), spt.k(), rstd.k()], [x.k(fc)])
                    dma(lambda e: e.dma_start(out=yT[:, 0:NPT].rearrange("(k p) n -> p k n", p=128), in_=x[:, :, 2:SB0]), [x.k()], [("yT", 0)])
                    for fc in range(KC):
                        dma(lambda e, fc=fc: e.dma_start(out=yT[fc * 128:(fc + 1) * 128, NPT:NPT + 64].rearrange("p (s t) -> p s t", t=4), in_=sview(x[:, fc, :])[:, :, 2:6]), [x.k(fc)], [("yT", 1 + fc)])
                S.barrier()

    try:
        main_program()
    except _StopBuild:
        pass
    S.stopped = False
    S.barrier()
    S.emit()
    root.close()
    return nc


_PROG = [None]


def _consts():
    c = np.zeros((128, NCONST), np.float32)
    idx = np.arange(128)
    c[:, C_ID:C_ID + 128] = np.eye(128)
    c[:, C_ONES:C_ONES + 128] = 1.0
    c[:, C_BD:C_BD + 128] = (idx[:, None] // 64 == idx[None, :] // 64)
    su = (idx[:, None] < idx[None, :]).astype(np.float32)
    iu = (idx[:, None] <= idx[None, :]).astype(np.float32)
    c[:, C_MUP:C_MUP + 128] = su; c[:, C_MUP + 128:C_MUP + 256] = iu
    c[:, C_MLP:C_MLP + 128] = (idx[None, :] < idx[:, None])
    c[:, C_UTP:C_UTP + 128] = -DEC * iu; c[:, C_UTP + 128:C_UTP + 256] = -DEC * su
    i64 = np.arange(64)
    same = (i64[:, None] // 4 == i64[None, :] // 4).astype(np.float32)
    su6 = su[:64, :64] * same; iu6 = iu[:64, :64] * same
    c[:64, C_MUS:C_MUS + 64] = su6; c[:64, C_MUS + 64:C_MUS + 128] = iu6
    c[:64, C_MLS:C_MLS + 64] = (i64[None, :] < i64[:, None]) * same
    c[:64, C_UTS:C_UTS + 64] = -DEC * iu6; c[:64, C_UTS + 64:C_UTS + 128] = -DEC * su6
    cm = (i64[None, :] // 4 == np.arange(16)[:, None]).astype(np.float32)
    c[:, C_CMS:C_CMS + 1024] = cm.reshape(1, 1024)
    c[:64, C_RMS:C_RMS + 16] = (i64[:, None] // 4 == np.arange(16)[None, :])
    return c


def kernel(x_prompt, x_sample, c_prompt, c_sample, state_wkv, state_shift, state_conv, state_ffn,
           ada_w, ada_b, norm_g, final_norm_g, w_in, mu_x, mu_rkv, decay_w0, decay_lora1, decay_lora2,
           iclr_a0, iclr_lora1, iclr_lora2, gate_lora1, gate_lora2, vres_v0, vres_lora1, vres_lora2,
           k_k, k_a, r_k, ln_x_w, ln_x_b, conv_w, w_out, ffn_up, ffn_conv, ffn_down):
    f = lambda a: np.ascontiguousarray(np.asarray(a, dtype=np.float32))
    (x_prompt, x_sample, c_prompt, c_sample, state_wkv, state_shift, state_conv, state_ffn) = map(
        f, (x_prompt, x_sample, c_prompt, c_sample, state_wkv, state_shift, state_conv, state_ffn))
    if _PROG[0] is None:
        _PROG[0] = build_program()
    nc = _PROG[0]
    pk = lambda v, n: np.asarray(v, np.float32).reshape(n, 128).T
    smallp = np.zeros((2, 128, NSP), np.float32)
    for l in range(2):
        sp = smallp[l]
        sp[:, ADAB:ADAB + 96] = pk(ada_b[l], 96)
        sp[:, NG0:NG0 + 16] = pk(norm_g[l, 0], 16); sp[:, NG1:NG1 + 16] = pk(norm_g[l, 1], 16)
        sp[:, MUX:MUX + 64] = np.asarray(mu_x[l]).reshape(4, 16, 128).transpose(2, 1, 0).reshape(128, 64)
        sp[:, MURKV:MURKV + 24] = np.asarray(mu_rkv[l]).reshape(3, 8, 128).transpose(2, 0, 1).reshape(128, 24)
        sp[:, A0:A0 + 8] = pk(iclr_a0[l], 8); sp[:, V0:V0 + 8] = pk(vres_v0[0], 8)
        sp[:, KKc:KKc + 8] = pk(k_k[l], 8); sp[:, KAc:KAc + 8] = pk(k_a[l], 8); sp[:, RKc:RKc + 8] = pk(r_k[l], 8)
        sp[:, LNW:LNW + 8] = pk(ln_x_w[l], 8); sp[:, LNB:LNB + 8] = pk(ln_x_b[l], 8)
        sp[:, CW:CW + 24] = np.asarray(conv_w[l]).reshape(3, 8, 128).transpose(2, 1, 0).reshape(128, 24)
        sp[:, FCW:FCW + 264] = np.asarray(ffn_conv[l]).reshape(3, 88, 128).transpose(2, 1, 0).reshape(128, 264)
        sp[:, FNG:FNG + 16] = pk(final_norm_g, 16)
    consts = _consts()
    def tile_w(w, nk):
        w = np.asarray(w, np.float32)
        nb = w.shape[2] // 128
        return np.ascontiguousarray(w.reshape(2, nk, 128, nb, 128).transpose(0, 3, 2, 1, 4).reshape(2, nb, 128, nk * 128))
    shared = dict(smallp=smallp, w0row=f(decay_w0).reshape(2, 1, G), consts=consts, ada_w=tile_w(ada_w, KC), w_in=tile_w(w_in, KC),
                  decay_lora1=f(decay_lora1), decay_lora2=f(decay_lora2), iclr_lora1=f(iclr_lora1), iclr_lora2=f(iclr_lora2),
                  gate_lora1=f(gate_lora1), gate_lora2=f(gate_lora2), vres_lora1=f(vres_lora1), vres_lora2=f(vres_lora2),
                  w_out=tile_w(w_out, KC), ffn_up=tile_w(ffn_up, KC), ffn_down=tile_w(ffn_down, 44))
    in_maps = []
    for c in range(8):
        seq, half = c // 2, c % 2
        xT = np.zeros((D, E), np.float32)
        xT[:, 2:SB0] = x_prompt[seq, half * NPT:(half + 1) * NPT, :].T
        xs = x_sample[16 * c:16 * c + 16]
        xv = xT[:, SB0:].reshape(D, 16, 6)
        xv[:, :, 2:6] = xs.transpose(2, 0, 1)
        cT = np.zeros((D, NGRP), np.float32)
        cT[:, 0] = c_prompt[seq]; cT[:, 1:] = c_sample[16 * c:16 * c + 16].T
        m = dict(shared)
        m.update(xT=xT, cT=cT, flag=np.full((128, 1), float(half), np.float32),
                 shiftT=f(state_shift[:, 16 * c:16 * c + 16, :].transpose(0, 2, 1)),
                 convT=f(state_conv[:, 16 * c:16 * c + 16].transpose(0, 3, 1, 2)),
                 ffnT=f(state_ffn[:, 16 * c:16 * c + 16].transpose(0, 3, 1, 2)),
                 wkv_in=f(state_wkv[:, 16 * c:16 * c + 16].transpose(0, 1, 2, 4, 3)))
        in_maps.append(m)
    res = run_bass_kernel_spmd(nc, in_maps, core_ids=list(range(8))).results
    y_p = np.zeros((4, 2048, D), np.float32); y_s = np.zeros((128, 4, D), np.float32)
    wkv_p = np.zeros((2, 4, 16, 64, 64), np.float32); wkv_s = np.zeros((2, 128, 16, 64, 64), np.float32)
    sh_p = np.zeros((2, 4, D), np.float32); sh_s = np.zeros((2, 128, D), np.float32)
    cv_p = np.zeros((2, 4, 2, G), np.float32); cv_s = np.zeros((2, 128, 2, G), np.float32)
    ff_p = np.zeros((2, 4, 2, 2 * FF), np.float32); ff_s = np.zeros((2, 128, 2, 2 * FF), np.float32)
    for c in range(8):
        seq, half = c // 2, c % 2
        r = res[c]
        yTo = np.asarray(r["yT"])
        y_p[seq, half * NPT:(half + 1) * NPT] = yTo[:, :NPT].T
        y_s[16 * c:16 * c + 16] = yTo[:, NPT:].reshape(D, 16, 4).transpose(1, 2, 0)
        wk = np.asarray(r["wkvo"]).transpose(0, 1, 2, 4, 3); sh = np.asarray(r["shifto"]); cvo = np.asarray(r["convo"]); ffo = np.asarray(r["ffno"])
        wkv_s[:, 16 * c:16 * c + 16] = wk[:, 1:]
        sh_s[:, 16 * c:16 * c + 16] = sh[:, :, 1:].transpose(0, 2, 1)
        cv_s[:, 16 * c:16 * c + 16] = cvo[:, :, 1:, :].transpose(0, 2, 3, 1)
        ff_s[:, 16 * c:16 * c + 16] = ffo[:, :, 1:, :].transpose(0, 2, 3, 1)
        if half == 1:
            wkv_p[:, seq] = wk[:, 0]
            sh_p[:, seq] = sh[:, :, 0]
            cv_p[:, seq] = cvo[:, :, 0, :].transpose(0, 2, 1)
            ff_p[:, seq] = ffo[:, :, 0, :].transpose(0, 2, 1)
    return (y_p, y_s, wkv_p, sh_p, cv_p, ff_p, wkv_s, sh_s, cv_s, ff_s)
```

```python
import numpy as np
import concourse.bass as bass
import concourse.mybir as mybir
from concourse.bass_utils import run_bass_kernel_spmd
from contextlib import ExitStack

F32 = mybir.dt.float32
BF16 = mybir.dt.bfloat16
ALU = mybir.AluOpType
AF = mybir.ActivationFunctionType

D = 2048; KC = 16; G = 1024; FF = 5632; NPT = 1024; NSQ = 16; E = 1122; SB0 = 1026
NGRP = 17
ADAB = 0; NG0 = 96; NG1 = 112; MUX = 128; MURKV = 192; A0 = 216; V0 = 224; KKc = 232; KAc = 240
RKc = 248; LNW = 256; LNB = 264; CW = 272; FCW = 296; FNG = 560; NSP = 576
C_ID = 0; C_ONES = 128; C_BD = 256; C_MUP = 384; C_MLP = 640; C_UTP = 768; C_MUS = 1024; C_MLS = 1152
C_UTS = 1216; C_CMS = 1344; C_RMS = 2368; NCONST = 2384
DEC = 0.6065306597126334


class _Rec:
    def __init__(self):
        self.call = None

    def __getattr__(self, name):
        def f(*a, **k):
            assert self.call is None
            self.call = (name, a, k)
            return None
        return f


class Sched:
    NDMA = 40

    def __init__(self, nc):
        self.nc = nc
        self.ops = []
        self.state = {}
        self.pending_dma = []
        self.floor = {}
        self.stopped = False
        self.last_op = {}

    def _access(self, op, key, write):
        name, sub = key
        ents = self.state.setdefault(name, [])
        hit = [e for e in ents if e[0] is None or sub is None or e[0] == sub]
        for e in hit:
            if e[1] is not None:
                op["deps"].add(e[1])
            if write:
                op["deps"].update(e[2])
        if write:
            for e in hit:
                ents.remove(e)
            ents.append([sub, op["i"], []])
        else:
            if not hit:
                ents.append([sub, None, [op["i"]]])
            else:
                for e in hit:
                    if len(e[2]) > 8:
                        e[2][:] = [x for x in e[2] if self.ops[x]["eng"] != op["eng"] or self.ops[x]["dma"]]
                    e[2].append(op["i"])

    def add(self, eng, fn, r=(), w=(), dma=False, coll=False, extra=()):
        if self.stopped:
            return dict(i=-1)
        if fn is not None:
            rec = _Rec()
            fn(rec)
            name_, a_, k_ = rec.call
            fn = lambda e, name_=name_, a_=a_, k_=k_: getattr(e, name_)(*a_, **k_)
        op = dict(eng=eng, fn=fn, deps=set(extra), dma=dma, coll=coll, i=len(self.ops), sig=None, ndep=0)
        self.ops.append(op)
        for k in r:
            self._access(op, k, isinstance(k[0], str) and k[0].startswith("ps") and k[0][2:].isdigit())
        for k in w:
            self._access(op, k, True)
        op["deps"].discard(op["i"])
        if dma:
            self.pending_dma.append(op["i"])
        elif fn is not None:
            self.last_op[eng] = op["i"]
        return op

    def barrier(self):
        if self.stopped:
            return
        engs = ["pe", "act", "dve", "pool", "sp"]
        deps = list(self.pending_dma)
        self.pending_dma = []
        for e in engs:
            if self.last_op.get(e) is not None:
                deps.append(self.last_op[e])
        for e in engs:
            self.add(e, None, extra=deps)
        self.state = {}

    def _skip(self, p, c):
        return p["eng"] == "pe" and c["eng"] == "pe" and not p["dma"] and not c["dma"]

    def emit(self):
        nc = self.nc
        ops = self.ops
        for op in ops:
            for d in op["deps"]:
                if not self._skip(ops[d], op):
                    ops[d]["ndep"] += 1
        with ExitStack() as st:
            esem = {e: st.enter_context(nc.semaphore("se_" + e)) for e in ["pe", "act", "dve", "pool", "sp"]}
            dsem = [st.enter_context(nc.semaphore(f"sd{i}")) for i in range(self.NDMA)]
            cnt = {e: 0 for e in esem}
            dcnt = [0] * self.NDMA
            nd = 0
            csem = {}
            for op in ops:
                if op["dma"]:
                    if op["coll"]:
                        csem[op["i"]] = st.enter_context(nc.semaphore(f"sc{op['i']}"))
                        op["sig"] = (("c", op["i"]), 1)
                    else:
                        s = nd % self.NDMA
                        nd += 1
                        dcnt[s] += 16
                        op["sig"] = (("d", s), dcnt[s])
                elif op["ndep"] > 0:
                    cnt[op["eng"]] += 1
                    op["sig"] = (("e", op["eng"]), cnt[op["eng"]])

            def semof(k):
                if k[0] == "c":
                    return csem[k[1]]
                return dsem[k[1]] if k[0] == "d" else esem[k[1]]

            def run(en, e):
                waited = {}
                for op in ops:
                    if op["eng"] != en:
                        continue
                    need = {}
                    for d in op["deps"]:
                        p = ops[d]
                        if self._skip(p, op):
                            continue
                        k, v = p["sig"]
                        if waited.get(k, 0) < v:
                            need[k] = max(need.get(k, 0), v)
                    if op["dma"] and not op["coll"]:
                        k, v = op["sig"]
                        if v - 16 > 0 and waited.get(k, 0) < v - 16:
                            need[k] = max(need.get(k, 0), v - 16)
                    for k, v in need.items():
                        e.wait_ge(semof(k), v)
                        waited[k] = v
                    if op["fn"] is None:
                        assert op["sig"] is None
                        continue
                    ins = op["fn"](e)
                    if op["sig"] is not None:
                        k, v = op["sig"]
                        ins.then_inc(semof(k), (1 if op["coll"] else 16) if op["dma"] else 1)

            with nc.Block() as block:
                @block.tensor
                def _(e):
                    run("pe", e)

                @block.scalar
                def _(e):
                    run("act", e)

                @block.vector
                def _(e):
                    run("dve", e)

                @block.gpsimd
                def _(e):
                    run("pool", e)

                @block.sync
                def _(e):
                    run("sp", e)


class T:
    def __init__(self, name, t):
        self.name = name
        self.t = t

    def k(self, sub=None):
        return (self.name, sub)

    def __getitem__(self, idx):
        return self.t[idx]


DEBUG_STOP = None


class _StopBuild(Exception):
    pass


_SCHED = [None]


def mark(name):
    if DEBUG_STOP == name:
        _SCHED[0].stopped = True


def build_program():
    nc = bass.Bass("TRN2", target_bir_lowering=False)
    S = Sched(nc)
    _SCHED[0] = S

    def din(name, shape):
        return nc.dram_tensor(name, shape, F32, kind="ExternalInput").ap()

    def dout(name, shape):
        return nc.dram_tensor(name, shape, F32, kind="ExternalOutput").ap()

    def dint(name, shape):
        return nc.dram_tensor(name, shape, F32, kind="Internal").ap()

    xT = din("xT", [D, E]); cT = din("cT", [D, NGRP]); flag_d = din("flag", [128, 1])
    shiftT = din("shiftT", [2, D, NSQ]); convT = din("convT", [2, G, NSQ, 2]); ffnT = din("ffnT", [2, 2 * FF, NSQ, 2])
    wkv_in = din("wkv_in", [2, NSQ, 16, 64, 64]); smallp = din("smallp", [2, 128, NSP]); w0row = din("w0row", [2, 1, G])
    consts_d = din("consts", [128, NCONST])
    ada_w = din("ada_w", [2, 96, 128, KC * 128]); w_in = din("w_in", [2, 48, 128, KC * 128])
    dl1 = din("decay_lora1", [2, D, 96]); dl2 = din("decay_lora2", [2, 96, G])
    il1 = din("iclr_lora1", [2, D, 96]); il2 = din("iclr_lora2", [2, 96, G])
    gl1 = din("gate_lora1", [2, D, 256]); gl2 = din("gate_lora2", [2, 256, G])
    vl1 = din("vres_lora1", [1, D, 64]); vl2 = din("vres_lora2", [1, 64, G])
    w_out = din("w_out", [2, 16, 128, KC * 128]); ffn_up = din("ffn_up", [2, 88, 128, KC * 128]); ffn_down = din("ffn_down", [2, 16, 128, 44 * 128])
    yT = dout("yT", [D, NPT + 64]); wkvo = dout("wkvo", [2, NGRP, 16, 64, 64]); shifto = dout("shifto", [2, D, NGRP])
    convo = dout("convo", [2, G, NGRP, 2]); ffno = dout("ffno", [2, 2 * FF, NGRP, 2])
    x_sp = dint("x_sp", [D, E]); vf_d = dint("vf_d", [8, 128, E])
    cin_s = [dint(f"cin_s{i}", [128, 128]) for i in range(16)]
    cout_s = [dint(f"cout_s{i}", [256, 128]) for i in range(16)]
    cin_h = [dint(f"cin_h{i}", [128, 32]) for i in range(4)]
    cout_h = [dint(f"cout_h{i}", [256, 32]) for i in range(4)]
    groups = [[0, 1], [2, 3], [4, 5], [6, 7]]
    OUTK = [("yT", None), ("wkvo", None), ("shifto", None), ("convo", None), ("ffno", None)]

    root = ExitStack()

    uid = [0]

    def sb(st, name, shape, dt=F32):
        uid[0] += 1
        nm = f"s{uid[0]}_{name}"
        return T(nm, st.enter_context(nc.sbuf_tensor(nm, shape, dt)))

    def dve(fn, r, w): S.add("dve", fn, r, w)
    def act(fn, r, w): S.add("act", fn, r, w)
    def pe(fn, r, w): S.add("pe", fn, r, w)
    def pool(fn, r, w): S.add("pool", fn, r, w)
    def dma(fn, r, w): S.add("sp", fn, r, w, dma=True)
    def dmac(fn, r, w): S.add("pool", fn, r, w, dma=True)

    h = sb(root, "h", [128, KC, E], BF16)
    og = sb(root, "og", [128, KC * E], BF16)
    o3 = og[:, :].rearrange("p (k e) -> p k e", e=E)
    cst = sb(root, "cst", [128, NCONST])
    spt = sb(root, "spt", [128, NSP])
    flag = sb(root, "flag", [128, 1])
    psb = [T(f"ps{i}", root.enter_context(nc.psum_tensor(f"ps{i}", [128, 512], F32))) for i in range(8)]
    pctr = [0, 0]

    def psm():
        pctr[0] = (pctr[0] + 1) % 4
        return psb[pctr[0]]

    def psx():
        pctr[1] = (pctr[1] + 1) % 4
        return psb[4 + pctr[1]]

    dma(lambda e: e.dma_start(out=cst[:, :], in_=consts_d), [], [cst.k()])
    dma(lambda e: e.dma_start(out=flag[:, :], in_=flag_d), [], [flag.k()])
    ident = cst[:, C_ID:C_ID + 128]; ones = cst[:, C_ONES:C_ONES + 128]; bd = cst[:, C_BD:C_BD + 128]
    TA = [(1, 375), (375, 749), (749, E)]
    TB = [(0, 374), (374, 748), (748, E)]
    RKV_T = [(1, 386), (385, 770), (769, 1026), (1026, E)]

    def sview(ap2d):
        return ap2d[:, SB0:E].rearrange("p (s t) -> p s t", t=6)

    def exchange(idx_list_in, idx_list_out, src_tile, src_ap, dst_tile, dst_ap, cin, cout, rows_cols):
        dma(lambda e: e.dma_start(out=cin, in_=src_ap), [src_tile.k()], [(cin.name, None)])
        S.add("pool", lambda e: e.collective_compute("AllGather", ALU.bypass, replica_groups=groups, ins=[cin], outs=[cout]),
              [(cin.name, None)], [(cout.name, None)], dma=True, coll=True)
        dma(lambda e: e.dma_start(out=dst_ap, in_=cout[0:128, :]), [(cout.name, None)], [dst_tile.k()])

    xcm = [None]

    def load_small(l):
        dma(lambda e: e.dma_start(out=spt[:, :], in_=smallp[l]), [], [spt.k()])

    def phase_A(l, st, x, first):
        load_small(l)
        modt = sb(st, f"modt", [128, 96, NGRP])
        sc = sb(st, "sc", [128, KC, NGRP]); scb = sb(st, "scb", [128, KC, NGRP], BF16)
        dma(lambda e: e.dma_start(out=sc[:, :, :], in_=cT.rearrange("(k p) s -> p k s", p=128)), [], [sc.k()])
        act(lambda e: e.activation(scb[:, :, :], sc[:, :, :], AF.Silu), [sc.k()], [scb.k()])
        mark(f"A{l}s")
        awb = [sb(st, f"awb{i}", [128, KC, 128], BF16) for i in range(3)]
        for j in range(96):
            wb = awb[j % 3]
            dmac(lambda e, wb=wb, j=j: e.dma_start(out=wb[:, :, :], in_=ada_w[l, j].rearrange("p (k n) -> p k n", n=128)), [], [wb.k()])
            for jj in range(1):
                ps = psm()
                for kc in range(KC):
                    pe(lambda e, ps=ps, wb=wb, kc=kc, jj=jj: e.matmul(ps[:, 0:NGRP], wb[:, kc, :], scb[:, kc, :], start=(kc == 0), stop=(kc == KC - 1)), [wb.k(), scb.k()], [ps.k()])
                jb = j
                dve(lambda e, ps=ps, jb=jb: e.tensor_scalar(modt[:, jb, :], ps[:, 0:NGRP], spt[:, ADAB + jb:ADAB + jb + 1], None, ALU.add), [ps.k(), spt.k()], [modt.k(jb)])
                mark(f"A{l}m{jb}")
        mark(f"A{l}m")
        md = {}
        for nm, sci, ngo in (("A1", 16, NG0), ("A2", 64, NG1)):
            t = sb(root if False else st, nm, [128, KC, NGRP])
            md[nm] = t
            dve(lambda e, t=t, sci=sci: e.tensor_scalar(t[:, :, :], modt[:, sci:sci + 16, :], 1.0, None, ALU.add), [modt.k()], [t.k()])
            dve(lambda e, t=t, ngo=ngo: e.tensor_tensor(t[:, :, :], t[:, :, :], spt[:, ngo:ngo + 16].unsqueeze(2).to_broadcast([128, KC, NGRP]), ALU.mult), [t.k(), spt.k()], [t.k()])
        for nm, off in (("B1", 0), ("GA1", 32), ("B2", 48), ("GA2", 80)):
            t = sb(st, nm, [128, KC, NGRP])
            md[nm] = t
            dve(lambda e, t=t, off=off: e.tensor_copy(t[:, :, :], modt[:, off:off + 16, :]), [modt.k()], [t.k()])
        mark(f"A{l}d")
        return md

    keep_tiles = {nm: sb(root, "keep_" + nm, [128, KC, NGRP]) for nm in ("GA1", "A2", "B2", "GA2")}
    eps_t = sb(root, "eps_t", [128, 2])
    dve(lambda e: e.memset(eps_t[:, 0:1], 1e-6), [], [eps_t.k()])
    dve(lambda e: e.memset(eps_t[:, 1:2], 64e-5), [], [eps_t.k()])

    def norm_mod(st, x, A, B, gcol_unused, hl, tagsfx="", pre=None):
        if pre is None:
            rstd = sb(st, "rstd" + tagsfx, [128, E]); tmp = sb(st, "ntmp" + tagsfx, [128, E])
            sq = [sb(st, f"sq{i}" + tagsfx, [128, 512]) for i in range(2)]
        else:
            rstd, tmp, sq = pre
        n = 0
        for (c0, c1) in TB:
            ps = psm()
            for fc in range(KC):
                q = sq[n % 2]; n += 1
                act(lambda e, q=q, fc=fc, c0=c0, c1=c1: e.activation(q[:, 0:c1 - c0], x[:, fc, c0:c1], AF.Square), [x.k(fc)], [q.k()])
                pe(lambda e, ps=ps, q=q, fc=fc, c0=c0, c1=c1: e.matmul(ps[:, 0:c1 - c0], ones, q[:, 0:c1 - c0], start=(fc == 0), stop=(fc == KC - 1)), [q.k(), cst.k()], [ps.k()])
            act(lambda e, ps=ps, c0=c0, c1=c1: e.activation(tmp[:, c0:c1], ps[:, 0:c1 - c0], AF.Sqrt, bias=eps_t[:, 0:1], scale=1.0 / D), [ps.k(), eps_t.k()], [tmp.k()])
        dve(lambda e: e.reciprocal(rstd[:, :], tmp[:, :]), [tmp.k()], [rstd.k()])
        for fc in range(KC):
            dve(lambda e, fc=fc: e.tensor_tensor(tmp[:, :], x[:, fc, :], rstd[:, :], ALU.mult), [x.k(fc), rstd.k()], [tmp.k()])
            act(lambda e, fc=fc: e.activation(h[:, fc, 0:SB0], tmp[:, 0:SB0], AF.Identity, bias=B[:, fc, 0:1], scale=A[:, fc, 0:1]), [tmp.k(), A.k(), B.k()], [h.k(fc)])
            if hl is not None:
                act(lambda e, fc=fc: e.activation(hl[:, fc, 0:1], tmp[:, SB0 - 1:SB0], AF.Identity, bias=B[:, fc, 0:1], scale=A[:, fc, 0:1]), [tmp.k(), A.k(), B.k()], [hl.k()])
            ts = sview(tmp[:, :])
            dve(lambda e, fc=fc, ts=ts: e.tensor_tensor(ts, ts, A[:, fc, 1:NGRP].unsqueeze(2).to_broadcast([128, NSQ, 6]), ALU.mult), [tmp.k(), A.k()], [tmp.k()])
            if hl is not None:
                dve(lambda e, fc=fc, ts=ts: e.tensor_tensor(hl[:, fc, 1:NGRP], ts[:, :, 5], B[:, fc, 1:NGRP], ALU.add), [tmp.k(), B.k()], [hl.k()])
            dve(lambda e, fc=fc, ts=ts: e.tensor_tensor(sview(h[:, fc, :]), ts, B[:, fc, 1:NGRP].unsqueeze(2).to_broadcast([128, NSQ, 6]), ALU.add), [tmp.k(), B.k()], [h.k(fc)])

    def halo_exchange(st, ci, extra_cols=None):
        hs = sb(st, f"hs{ci}", [128, 32]); hr = sb(st, f"hr{ci}", [128, 32])
        dve(lambda e: e.tensor_copy(hs[:, :].rearrange("p (k t) -> p k t", t=2), h[:, :, SB0 - 2:SB0]), [h.k()], [hs.k()])
        exchange(None, None, hs, hs[:, :], hr, hr[:, :], cin_h[ci], cout_h[ci], None)
        dve(lambda e: e.tensor_scalar(h[:, :, 0:2], hr[:, :].rearrange("p (k t) -> p k t", t=2), flag[:, 0:1], None, ALU.mult), [hr.k(), flag.k()], [h.k()])

    def phase_B(l, st):
        nl = 512 if l == 1 else 448
        Wa = og[:, 0:KC * 512].rearrange("p (k n) -> p k n", n=512)
        Wb = og[:, KC * 512:2 * KC * 512].rearrange("p (k n) -> p k n", n=512)
        stg = [sb(st, f"stg{i}", [128, 512]) for i in range(2)]
        omx = sb(st, "omx", [128, 64])
        dve(lambda e: e.tensor_scalar(omx[:, :], spt[:, MUX:MUX + 64], -1.0, 1.0, ALU.mult, ALU.add), [spt.k()], [omx.k()])
        segs = [(dl1, 0, 96, 0), (il1, 96, 96, 1), (gl1, 192, 256, 2)] + ([(vl1, 448, 64, 3)] if l == 1 else [])
        for kc in range(KC):
            sg = stg[kc % 2]
            for (wd, c0, n, mi) in segs:
                ll = 0 if wd is vl1 else l
                dma(lambda e, sg=sg, wd=wd, ll=ll, c0=c0, n=n, kc=kc: e.dma_start(out=sg[:, c0:c0 + n], in_=wd[ll, kc * 128:(kc + 1) * 128, :]), [], [sg.k()])
            for (wd, c0, n, mi) in segs:
                dve(lambda e, sg=sg, c0=c0, n=n, mi=mi, kc=kc: e.tensor_scalar(Wa[:, kc, c0:c0 + n], sg[:, c0:c0 + n], omx[:, kc * 4 + mi:kc * 4 + mi + 1], None, ALU.mult), [sg.k(), omx.k()], [og.k()])
                dve(lambda e, sg=sg, c0=c0, n=n, mi=mi, kc=kc: e.tensor_scalar(Wb[:, kc, c0:c0 + n], sg[:, c0:c0 + n], spt[:, MUX + kc * 4 + mi:MUX + kc * 4 + mi + 1], None, ALU.mult), [sg.k(), spt.k()], [og.k()])
        th = sb(st, "th", [96, E], BF16); ia = sb(st, "ia", [96, E], BF16); gt = sb(st, "gt", [128, 2, E], BF16); vr = sb(st, "vr", [64, E], BF16)
        lgroups = [(0, 96, "th"), (96, 96, "ia"), (192, 128, "g0"), (320, 128, "g1")] + ([(448, 64, "vr")] if l == 1 else [])
        for (c0g, m, nm) in lgroups:
            for (c0, c1) in TA:
                ps = psm()
                for kc in range(KC):
                    pe(lambda e, ps=ps, kc=kc, c0=c0, c1=c1, c0g=c0g, m=m: e.matmul(ps[0:m, 0:c1 - c0], Wa[:, kc, c0g:c0g + m], h[:, kc, c0:c1], start=(kc == 0), stop=False), [og.k(), h.k(kc)], [ps.k()])
                    pe(lambda e, ps=ps, kc=kc, c0=c0, c1=c1, c0g=c0g, m=m: e.matmul(ps[0:m, 0:c1 - c0], Wb[:, kc, c0g:c0g + m], h[:, kc, c0 - 1:c1 - 1], start=False, stop=(kc == KC - 1)), [og.k(), h.k(kc)], [ps.k()])
                if nm == "th":
                    act(lambda e, ps=ps, c0=c0, c1=c1: e.activation(th[:, c0:c1], ps[0:96, 0:c1 - c0], AF.Tanh), [ps.k()], [th.k()])
                elif nm == "ia":
                    act(lambda e, ps=ps, c0=c0, c1=c1: e.activation(ia[:, c0:c1], ps[0:96, 0:c1 - c0], AF.Copy), [ps.k()], [ia.k()])
                elif nm == "vr":
                    act(lambda e, ps=ps, c0=c0, c1=c1: e.activation(vr[:, c0:c1], ps[0:64, 0:c1 - c0], AF.Copy), [ps.k()], [vr.k()])
                else:
                    gi = 0 if nm == "g0" else 1
                    act(lambda e, ps=ps, c0=c0, c1=c1, gi=gi: e.activation(gt[:, gi, c0:c1], ps[:, 0:c1 - c0], AF.Sigmoid), [ps.k()], [gt.k()])
        mark(f"B{l}L")
        dl2t = sb(st, "dl2t", [96, G], BF16); il2t = sb(st, "il2t", [96, G], BF16); gl2t = sb(st, "gl2t", [128, 2, G], BF16); vl2t = sb(st, "vl2t", [64, G], BF16)
        dmac(lambda e: e.dma_start(out=dl2t[:, :], in_=dl2[l]), [], [dl2t.k()])
        dmac(lambda e: e.dma_start(out=il2t[:, :], in_=il2[l]), [], [il2t.k()])
        dmac(lambda e: e.dma_start(out=gl2t[:, :, :], in_=gl2[l].rearrange("(k p) n -> p k n", p=128)), [], [gl2t.k()])
        if l == 1:
            dmac(lambda e: e.dma_start(out=vl2t[:, :], in_=vl2[0]), [], [vl2t.k()])
        w0bc = sb(st, "w0bc", [128, G])
        dma(lambda e: e.dma_start(out=w0bc[:, :], in_=w0row[l].partition_broadcast(128)), [], [w0bc.k()])
        omr = sb(st, "omr", [128, 24])
        dve(lambda e: e.tensor_scalar(omr[:, :], spt[:, MURKV:MURKV + 24], -1.0, 1.0, ALU.mult, ALU.add), [spt.k()], [omr.k()])
        oka = sb(st, "oka", [128, 8])
        dve(lambda e: e.tensor_scalar(oka[:, :], spt[:, KAc:KAc + 8], -1.0, 1.0, ALU.mult, ALU.add), [spt.k()], [oka.k()])

        wbuf = [sb(st, f"wbuf{i}", [128, KC, 128], BF16) for i in range(3)]
        wctr = [0]

        def load_w(blk):
            wb = wbuf[wctr[0] % 3]; wctr[0] += 1
            dmac(lambda e, wb=wb: e.dma_start(out=wb[:, :, :], in_=w_in[l, blk].rearrange("p (k n) -> p k n", n=128)), [], [wb.k()])
            return wb

        def proj(wb, dst, tiles):
            for (c0, c1) in tiles:
                ps = psm()
                for kc in range(KC):
                    pe(lambda e, ps=ps, kc=kc, c0=c0, c1=c1: e.matmul(ps[:, 0:c1 - c0], wb[:, kc, :], h[:, kc, c0:c1], start=(kc == 0), stop=(kc == KC - 1)), [wb.k(), h.k(kc)], [ps.k()])
                act(lambda e, ps=ps, c0=c0, c1=c1: e.activation(dst[:, c0:c1], ps[:, 0:c1 - c0], AF.Copy), [ps.k()], [dst.k()])

        names = ["Wr", "Wk2", "Wv", "Wkkn", "p_r", "p_k", "p_v", "a_sig", "yt"]
        Wt = {n: sb(st, n, [128, E]) for n in names}
        Wr, Wk2, Wv, Wkkn, p_r, p_k, p_v, a_sig, yt = [Wt[n] for n in names]
        for t_ in (Wr, Wk2, Wv, Wkkn, a_sig, yt, p_r, p_k, p_v):
            dve(lambda e, t_=t_: e.memset(t_[:, :], 0.0), [], [t_.k()])

        def cb(name, shape, dt=F32):
            return sb(st, name, shape, dt)
        thc = cb("thc", [96, 64], BF16); zt = cb("zt", [128, 128]); lw = cb("lw", [128, 128])
        E1 = cb("E1", [128, 128]); E0 = cb("E0", [128, 128]); Ei = cb("Ei", [128, 128])
        AR = cb("AR", [128, 256]); bt = cb("bt", [128, 128]); kt = cb("kt", [128, 128]); vc = cb("vc", [128, 64])
        Bt = cb("Bt", [128, 128]); Kt = cb("Kt", [128, 128]); Vt = cb("Vt", [128, 128])
        Vp = [cb(f"Vp{i}", [128, 128]) for i in range(2)]; Up = [cb(f"Up{i}", [128, 128]) for i in range(2)]
        for t_ in Vp + Up:
            dve(lambda e, t_=t_: e.memset(t_[:, :], 0.0), [], [t_.k()])
        LkA = cb("LkA", [128, 2, 256]); LbA = cb("LbA", [128, 2, 256])
        PP = [cb(f"PP{i}", [128, 4, 128]) for i in range(2)]
        TT = [cb(f"TT{i}", [128, 2, 128]) for i in range(2)]
        At = cb("At", [128, 128]); Gs = cb("Gs", [128, 128]); Gp = [cb(f"Gp{i}", [128, 128]) for i in range(2)]
        for t_ in Gp:
            dve(lambda e, t_=t_: e.memset(t_[:, :], 0.0), [], [t_.k()])
        gam = cb("gam", [128, 8]); Wc = cb("Wc", [128, 128])
        RPv = [stg[0], stg[1]]
        Xs = cb("Xs", [128, 128]); Us = cb("Us", [128, 128]); Y1 = cb("Y1", [128, 64]); tS = cb("tS", [128, 128])
        Sp = cb("Sp", [128, 128])
        Ss = cb("Ss", [128, NSQ, 128]); Sbd = cb("Sbd", [128, 4, 128])
        dve(lambda e: e.memset(Ss[:, :, :], 0.0), [], [Ss.k()])
        onat = cb("onat", [128, NGRP, 64])
        Apg = [cb(f"Apg{i}", [128, 64]) for i in range(2)]; Btg = [cb(f"Btg{i}", [64, 128]) for i in range(2)]; Ktg = [cb(f"Ktg{i}", [64, 128]) for i in range(2)]

        def unit(hp, kind, ci, full, Sget, corr=False):
            if kind == "p":
                C = 128; c0 = 2 + 128 * ci; nd = 6; Gn = 1
                cv = lambda t2: t2[:, c0:c0 + C]
                cm = lambda a: a
                MU = cst[0:128, C_MUP:C_MUP + 256]; ML = cst[0:128, C_MLP:C_MLP + 128]; UT = cst[0:128, C_UTP:C_UTP + 256]
            else:
                C = 64; nd = 1; Gn = NSQ
                cv = lambda t2: sview(t2)[:, :, 2:6]
                cm = lambda a: a.rearrange("p (s t) -> p s t", t=4)
                MU = cst[0:64, C_MUS:C_MUS + 128]; ML = cst[0:64, C_MLS:C_MLS + 64]; UT = cst[0:64, C_UTS:C_UTS + 128]
            hc = slice(hp * 128, (hp + 1) * 128)
            ps = psx()
            if kind == "p":
                pe(lambda e: e.matmul(ps[0:C, 0:128], th[:, c0:c0 + C], dl2t[:, hc], start=True, stop=True), [th.k(), dl2t.k()], [ps.k()])
            else:
                dve(lambda e: e.tensor_copy(cm(thc[:, 0:64]), cv(th[:, :])), [th.k()], [thc.k()])
                pe(lambda e: e.matmul(ps[0:C, 0:128], thc[:, 0:C], dl2t[:, hc], start=True, stop=True), [thc.k(), dl2t.k()], [ps.k()])
            dve(lambda e: e.tensor_tensor(zt[0:C, :], ps[0:C, 0:128], w0bc[0:C, hc], ALU.add), [ps.k(), w0bc.k()], [zt.k()])
            act(lambda e: e.activation(lw[0:C, :], zt[0:C, :], AF.Sigmoid), [zt.k()], [lw.k()])
            ps2 = psx()
            pe(lambda e: e.matmul(ps2[:, 0:2 * C], lw[0:C, :], UT, start=True, stop=True), [lw.k(), cst.k()], [ps2.k()])
            act(lambda e: e.activation(E1[:, 0:C], ps2[:, 0:C], AF.Exp), [ps2.k()], [E1.k()])
            act(lambda e: e.activation(E0[:, 0:C], ps2[:, C:2 * C], AF.Exp), [ps2.k()], [E0.k()])
            act(lambda e: e.activation(Ei[:, 0:C], ps2[:, 0:C], AF.Exp, scale=-1.0), [ps2.k()], [Ei.k()])
            dve(lambda e: e.scalar_tensor_tensor(cm(AR[:, 0:C]), cv(Wkkn[:, :]), -1.0, cm(E0[:, 0:C]), ALU.mult, ALU.mult), [Wkkn.k(), E0.k()], [AR.k()])
            dve(lambda e: e.tensor_tensor(cm(AR[:, C:2 * C]), cv(Wr[:, :]), cm(E1[:, 0:C]), ALU.mult), [Wr.k(), E1.k()], [AR.k()])
            dve(lambda e: e.tensor_tensor(cm(bt[:, 0:C]), cv(Wkkn[:, :]), cv(a_sig[:, :]), ALU.mult), [Wkkn.k(), a_sig.k()], [bt.k()])
            dve(lambda e: e.tensor_tensor(bt[:, 0:C], bt[:, 0:C], Ei[:, 0:C], ALU.mult), [bt.k(), Ei.k()], [bt.k()])
            dve(lambda e: e.tensor_tensor(cm(kt[:, 0:C]), cv(Wk2[:, :]), cm(Ei[:, 0:C]), ALU.mult), [Wk2.k(), Ei.k()], [kt.k()])
            if kind == "p":
                vsrc = Wv[:, c0:c0 + C]; vk = Wv.k()
            else:
                dve(lambda e: e.tensor_copy(cm(vc[:, 0:64]), cv(Wv[:, :])), [Wv.k()], [vc.k()])
                vsrc = vc[:, 0:C]; vk = vc.k()
            ps3 = psx()
            pe(lambda e: e.transpose(ps3[0:C, 0:128], bt[:, 0:C], ident), [bt.k(), cst.k()], [ps3.k()])
            pe(lambda e: e.transpose(ps3[0:C, 128:256], kt[:, 0:C], ident), [kt.k(), cst.k()], [ps3.k()])
            pe(lambda e: e.transpose(ps3[0:C, 256:384], vsrc, ident), [vk, cst.k()], [ps3.k()])
            act(lambda e: e.activation(Bt[0:C, :], ps3[0:C, 0:128], AF.Copy), [ps3.k()], [Bt.k()])
            act(lambda e: e.activation(Kt[0:C, :], ps3[0:C, 128:256], AF.Copy), [ps3.k()], [Kt.k()])
            dve(lambda e: e.tensor_copy(Vt[0:C, :], ps3[0:C, 256:384]), [ps3.k()], [Vt.k()])
            dve(lambda e: e.tensor_copy(Vp[0][0:C, 0:64], ps3[0:C, 256:320]), [ps3.k()], [Vp[0].k()])
            dve(lambda e: e.tensor_copy(Vp[1][0:C, 64:128], ps3[0:C, 320:384]), [ps3.k()], [Vp[1].k()])
            pa = psx(); pb = psx(); pc = psx()
            for hh in range(2):
                pr = slice(64 * hh, 64 * hh + 64)
                pe(lambda e, pr=pr, hh=hh: e.matmul(pa[0:C, 256 * hh:256 * hh + 2 * C], kt[pr, 0:C], AR[pr, 0:2 * C], start=True, stop=True), [kt.k(), AR.k()], [pa.k()])
                pe(lambda e, pr=pr, hh=hh: e.matmul(pb[0:C, 256 * hh:256 * hh + 2 * C], bt[pr, 0:C], AR[pr, 0:2 * C], start=True, stop=True), [bt.k(), AR.k()], [pb.k()])
                pe(lambda e, pr=pr, hh=hh: e.matmul(pc[0:C, 128 * hh:128 * hh + C], AR[pr, 0:C], bt[pr, 0:C], start=True, stop=True), [bt.k(), AR.k()], [pc.k()])
            MUb = MU.unsqueeze(1).to_broadcast([C, 2, 2 * C]); MLb = ML.unsqueeze(1).to_broadcast([C, 2, C])
            v256 = lambda p_: p_[0:C, 0:512].rearrange("p (s n) -> p s n", n=256)[:, :, 0:2 * C]
            v128 = lambda p_, ns: p_[0:C, 0:128 * ns].rearrange("p (s n) -> p s n", n=128)[:, :, 0:C]
            dve(lambda e: e.tensor_tensor(LbA[0:C, :, 0:2 * C], v256(pb), MUb, ALU.mult), [pb.k(), cst.k()], [LbA.k()])
            dve(lambda e: e.tensor_tensor(PP[0][0:C, 0:2, 0:C], v128(pc, 2), MLb, ALU.mult), [pc.k(), cst.k()], [PP[0].k()])
            dve(lambda e: e.tensor_tensor(LkA[0:C, :, 0:2 * C], v256(pa), MUb, ALU.mult), [pa.k(), cst.k()], [LkA.k()])
            dve(lambda e: e.tensor_tensor(TT[0][0:C, :, 0:C], LbA[0:C, :, 0:C], cst[0:C, C_ID:C_ID + C].unsqueeze(1).to_broadcast([C, 2, C]), ALU.add), [LbA.k(), cst.k()], [TT[0].k()])
            cur = 0
            for it in range(1, nd + 1):
                nxt = 1 - cur
                qs = psx()
                for hh in range(2):
                    Pt_ap = LbA[0:C, hh, 0:C] if it == 1 else PP[cur][0:C, 2 + hh, 0:C]
                    Pt_k = LbA.k() if it == 1 else PP[cur].k()
                    pe(lambda e, hh=hh, Pt_ap=Pt_ap, cur=cur: e.matmul(qs[0:C, 128 * hh:128 * hh + C], Pt_ap, PP[cur][0:C, hh, 0:C], start=True, stop=True), [Pt_k, PP[cur].k()], [qs.k()])
                if it < nd:
                    for hh in range(2):
                        Pt_ap = LbA[0:C, hh, 0:C] if it == 1 else PP[cur][0:C, 2 + hh, 0:C]
                        Pt_k = LbA.k() if it == 1 else PP[cur].k()
                        pe(lambda e, hh=hh, Pt_ap=Pt_ap, cur=cur: e.matmul(qs[0:C, 256 + 128 * hh:256 + 128 * hh + C], PP[cur][0:C, hh, 0:C], Pt_ap, start=True, stop=True), [Pt_k, PP[cur].k()], [qs.k()])
                if it > 1:
                    qt = psx()
                    ti = (it - 2) % 2
                    for hh in range(2):
                        pe(lambda e, hh=hh, cur=cur, ti=ti: e.matmul(qt[0:C, 128 * hh:128 * hh + C], PP[cur][0:C, hh, 0:C], TT[ti][0:C, hh, 0:C], start=True, stop=True), [PP[cur].k(), TT[ti].k()], [qt.k()])
                    dve(lambda e, qt=qt, ti=ti: e.tensor_tensor(TT[1 - ti][0:C, :, 0:C], v128(qt, 2), TT[ti][0:C, :, 0:C], ALU.add), [qt.k(), TT[ti].k()], [TT[1 - ti].k()])
                nsl = 4 if it < nd else 2
                act(lambda e, qs=qs, nxt=nxt, nsl=nsl: e.activation(PP[nxt][0:C, 0:nsl, 0:C], v128(qs, nsl), AF.Copy), [qs.k()], [PP[nxt].k()])
                cur = nxt
            qt = psx()
            ti = (nd - 1) % 2
            for hh in range(2):
                pe(lambda e, hh=hh, cur=cur, ti=ti: e.matmul(qt[0:C, 128 * hh:128 * hh + C], PP[cur][0:C, hh, 0:C], TT[ti][0:C, hh, 0:C], start=True, stop=True), [PP[cur].k(), TT[ti].k()], [qt.k()])
            TF = TT[1 - ti]
            dve(lambda e: e.tensor_tensor(TF[0:C, :, 0:C], v128(qt, 2), TT[ti][0:C, :, 0:C], ALU.add), [qt.k(), TT[ti].k()], [TF.k()])
            if corr:
                pat = psx()
                pe(lambda e: e.transpose(pat[0:C, 0:128], AR[:, 0:C], ident), [AR.k(), cst.k()], [pat.k()])
                act(lambda e: e.activation(At[0:C, :], pat[0:C, 0:128], AF.Copy), [pat.k()], [At.k()])
                pg = psx()
                for hh in range(2):
                    pe(lambda e, hh=hh: e.matmul(pg[0:C, 64 * hh:64 * hh + 64], TF[0:C, hh, 0:C], At[0:C, 64 * hh:64 * hh + 64], start=True, stop=True), [TF.k(), At.k()], [pg.k()])
                dve(lambda e: e.tensor_copy(Gs[0:C, :], pg[0:C, 0:128]), [pg.k()], [Gs.k()])
                act(lambda e: e.activation(Gp[0][0:C, 0:64], pg[0:C, 0:64], AF.Copy), [pg.k()], [Gp[0].k()])
                act(lambda e: e.activation(Gp[1][0:C, 64:128], pg[0:C, 64:128], AF.Copy), [pg.k()], [Gp[1].k()])
                prp = psx()
                for hh in range(2):
                    pe(lambda e, hh=hh: e.matmul(prp[:, 0:C], Gp[hh][0:C, :], LbA[0:C, hh, C:2 * C], start=(hh == 0), stop=(hh == 1)), [Gp[hh].k(), LbA.k()], [prp.k()])
                rpt = RPv[ci // 4]; rc0 = (ci % 4) * 128
                dve(lambda e: e.tensor_tensor(rpt[:, rc0:rc0 + C], prp[:, 0:C], AR[:, C:2 * C], ALU.add), [prp.k(), AR.k()], [rpt.k()])
                pnt = psx()
                pe(lambda e: e.matmul(pnt[:, 0:128], Gs[0:C, :], Bt[0:C, :], start=True, stop=True), [Gs.k(), Bt.k()], [pnt.k()])
                dve(lambda e: e.tensor_tensor(onat[:, :, :].rearrange("p g j -> p (g j)")[:, ci * 128:(ci + 1) * 128], pnt[:, 0:128], bd, ALU.mult), [pnt.k(), cst.k()], [onat.k()])
                dve(lambda e: e.tensor_copy(gam[:, ci:ci + 1], E1[:, C - 1:C]), [E1.k()], [gam.k()])
            px = psx()
            for g in range(Gn):
                Sg_ap, Sg_k = Sget(g)
                if kind == "p":
                    lhs = AR[:, 0:C]; lk = AR.k()
                else:
                    ap_ = Apg[g % 2]
                    dve(lambda e, ap_=ap_, g=g: e.tensor_tensor(ap_[:, :], AR[:, 0:64], cst[:, C_CMS + g * 64:C_CMS + (g + 1) * 64], ALU.mult), [AR.k(), cst.k()], [ap_.k()])
                    lhs = ap_[:, :]; lk = ap_.k()
                pe(lambda e, lhs=lhs, Sg_ap=Sg_ap, g=g: e.matmul(px[0:C, 0:128], lhs, Sg_ap, start=(g == 0), stop=(kind == "s" and g == Gn - 1)), [lk, Sg_k], [px.k()])
            if kind == "p":
                for hh in range(2):
                    pe(lambda e, hh=hh: e.matmul(px[0:C, 0:128], LkA[0:C, hh, 0:C], Vp[hh][0:C, :], start=False, stop=(hh == 1)), [LkA.k(), Vp[hh].k()], [px.k()])
                dve(lambda e: e.tensor_copy(Xs[0:C, :], px[0:C, 0:128]), [px.k()], [Xs.k()])
            else:
                px2 = psx()
                for hh in range(2):
                    pe(lambda e, hh=hh: e.matmul(px2[0:C, 0:128], LkA[0:C, hh, 0:C], Vp[hh][0:C, :], start=(hh == 0), stop=(hh == 1)), [LkA.k(), Vp[hh].k()], [px2.k()])
                dve(lambda e: e.tensor_copy(Xs[0:C, :], px[0:C, 0:128]), [px.k()], [Xs.k()])
                dve(lambda e: e.tensor_tensor(Xs[0:C, :], Xs[0:C, :], px2[0:C, 0:128], ALU.add), [px2.k(), Xs.k()], [Xs.k()])
            pu = psx()
            for hh in range(2):
                pe(lambda e, hh=hh: e.matmul(pu[0:C, 64 * hh:64 * hh + 64], TF[0:C, hh, 0:C], Xs[0:C, 64 * hh:64 * hh + 64], start=True, stop=True), [TF.k(), Xs.k()], [pu.k()])
            dve(lambda e: e.tensor_copy(Us[0:C, :], pu[0:C, 0:128]), [pu.k()], [Us.k()])
            if full:
                act(lambda e: e.activation(Up[0][0:C, 0:64], pu[0:C, 0:64], AF.Copy), [pu.k()], [Up[0].k()])
                act(lambda e: e.activation(Up[1][0:C, 64:128], pu[0:C, 64:128], AF.Copy), [pu.k()], [Up[1].k()])
                py = psx()
                for hh in range(2):
                    pe(lambda e, hh=hh: e.matmul(py[:, 0:C], Up[hh][0:C, :], LbA[0:C, hh, C:2 * C], start=(hh == 0), stop=False), [Up[hh].k(), LbA.k()], [py.k()])
                    pe(lambda e, hh=hh: e.matmul(py[:, 0:C], Vp[hh][0:C, :], LkA[0:C, hh, C:2 * C], start=False, stop=(hh == 1 and kind == "s")), [Vp[hh].k(), LkA.k()], [py.k()])
                if kind == "p":
                    Sg_ap, Sg_k = Sget(0)
                    pe(lambda e, Sg_ap=Sg_ap: e.matmul(py[:, 0:C], Sg_ap, AR[:, C:2 * C], start=False, stop=True), [Sg_k, AR.k()], [py.k()])
                    act(lambda e: e.activation(yt[:, c0:c0 + C], py[:, 0:C], AF.Copy), [py.k()], [yt.k()])
                else:
                    py1 = psx()
                    for g in range(Gn):
                        Sg_ap, Sg_k = Sget(g)
                        pe(lambda e, Sg_ap=Sg_ap, g=g: e.matmul(py1[:, 4 * g:4 * g + 4], Sg_ap, AR[:, C + 4 * g:C + 4 * g + 4], start=True, stop=True), [Sg_k, AR.k()], [py1.k()])
                    act(lambda e: e.activation(Y1[:, 0:64], py1[:, 0:64], AF.Copy), [py1.k()], [Y1.k()])
                    dve(lambda e: e.tensor_tensor(cv(yt[:, :]), cm(py[:, 0:64]), cm(Y1[:, 0:64]), ALU.add), [py.k(), Y1.k()], [yt.k()])
            for g in range(Gn):
                Sg_ap, Sg_k = Sget(g)
                pq = psx()
                if kind == "p":
                    lb_ = Bt[0:C, :]; lk_ = Kt[0:C, :]; kb = Bt.k(); kk_ = Kt.k()
                else:
                    bg_ = Btg[g % 2]; kg_ = Ktg[g % 2]
                    dve(lambda e, bg_=bg_, g=g: e.tensor_scalar(bg_[:, :], Bt[0:64, :], cst[0:64, C_RMS + g:C_RMS + g + 1], None, ALU.mult), [Bt.k(), cst.k()], [bg_.k()])
                    dve(lambda e, kg_=kg_, g=g: e.tensor_scalar(kg_[:, :], Kt[0:64, :], cst[0:64, C_RMS + g:C_RMS + g + 1], None, ALU.mult), [Kt.k(), cst.k()], [kg_.k()])
                    lb_ = bg_[:, :]; lk_ = kg_[:, :]; kb = bg_.k(); kk_ = kg_.k()
                pe(lambda e, pq=pq, lb_=lb_: e.matmul(pq[:, 0:128], lb_, Us[0:C, :], start=True, stop=False), [kb, Us.k()], [pq.k()])
                pe(lambda e, pq=pq, lk_=lk_: e.matmul(pq[:, 0:128], lk_, Vt[0:C, :], start=False, stop=True), [kk_, Vt.k()], [pq.k()])
                dve(lambda e, pq=pq: e.tensor_tensor(tS[:, :], pq[:, 0:128], bd, ALU.mult), [pq.k(), cst.k()], [tS.k()])
                dve(lambda e, Sg_ap=Sg_ap: e.tensor_tensor(Sg_ap, Sg_ap, tS[:, :], ALU.add), [Sg_k, tS.k()], [Sg_k])
                gcol = (C - 1) if kind == "p" else (4 * g + 3)
                dve(lambda e, Sg_ap=Sg_ap, gcol=gcol: e.tensor_scalar(Sg_ap, Sg_ap, E1[:, gcol:gcol + 1], None, ALU.mult), [Sg_k, E1.k()], [Sg_k])

        def state_out(hp, S_ap, S_k, gi):
            act(lambda e: e.activation(onat[0:64, gi, :], S_ap[0:64, 0:64], AF.Copy), [S_k], [onat.k()])
            dve(lambda e: e.tensor_copy(onat[64:128, gi, :], S_ap[64:128, 64:128]), [S_k], [onat.k()])

        for hp in range(8):
            hc = slice(hp * 128, (hp + 1) * 128)
            wr = load_w(hp)
            wk = load_w(8 + hp)
            wv = load_w(16 + hp)
            proj(wr, p_r, RKV_T); proj(wk, p_k, RKV_T); proj(wv, p_v, RKV_T)
            for (c0, c1) in TA:
                ps = psm()
                pe(lambda e, ps=ps, c0=c0, c1=c1: e.matmul(ps[:, 0:c1 - c0], il2t[:, hc], ia[:, c0:c1], start=True, stop=True), [il2t.k(), ia.k()], [ps.k()])
                act(lambda e, ps=ps, c0=c0, c1=c1: e.activation(a_sig[:, c0:c1], ps[:, 0:c1 - c0], AF.Sigmoid, bias=spt[:, A0 + hp:A0 + hp + 1]), [ps.k(), spt.k()], [a_sig.k()])
            for i, (p_, dst) in enumerate(((p_r, Wr), (p_k, Wk2), (p_v, Wv))):
                mc = MURKV + i * 8 + hp
                dve(lambda e, p_=p_, i=i: e.tensor_scalar(Wkkn[:, 2:E], p_[:, 2:E], omr[:, i * 8 + hp:i * 8 + hp + 1], None, ALU.mult), [p_.k(), omr.k()], [Wkkn.k()])
                dve(lambda e, p_=p_, dst=dst, mc=mc: e.scalar_tensor_tensor(dst[:, 2:E], p_[:, 1:E - 1], spt[:, mc:mc + 1], Wkkn[:, 2:E], ALU.mult, ALU.add), [p_.k(), spt.k(), Wkkn.k()], [dst.k()])
            if l == 0:
                dma(lambda e: e.dma_start(out=vf_d[hp], in_=Wv[:, :]), [Wv.k()], [("vf_d", hp)])
            else:
                dma(lambda e: e.dma_start(out=p_v[:, :], in_=vf_d[hp]), [("vf_d", hp)], [p_v.k()])
                for (c0, c1) in TA:
                    ps = psm()
                    pe(lambda e, ps=ps, c0=c0, c1=c1: e.matmul(ps[:, 0:c1 - c0], vl2t[:, hc], vr[:, c0:c1], start=True, stop=True), [vl2t.k(), vr.k()], [ps.k()])
                    act(lambda e, ps=ps, c0=c0, c1=c1: e.activation(p_r[:, c0:c1], ps[:, 0:c1 - c0], AF.Sigmoid, bias=spt[:, V0 + hp:V0 + hp + 1]), [ps.k(), spt.k()], [p_r.k()])
                dve(lambda e: e.tensor_tensor(p_v[:, 1:E], p_v[:, 1:E], Wv[:, 1:E], ALU.subtract), [p_v.k(), Wv.k()], [p_v.k()])
                dve(lambda e: e.tensor_tensor(p_v[:, 1:E], p_v[:, 1:E], p_r[:, 1:E], ALU.mult), [p_v.k(), p_r.k()], [p_v.k()])
                dve(lambda e: e.tensor_tensor(Wv[:, 1:E], Wv[:, 1:E], p_v[:, 1:E], ALU.add), [p_v.k(), Wv.k()], [Wv.k()])
            dve(lambda e: e.tensor_scalar(Wkkn[:, :], Wk2[:, :], spt[:, KKc + hp:KKc + hp + 1], None, ALU.mult), [Wk2.k(), spt.k()], [Wkkn.k()])
            dve(lambda e: e.tensor_tensor(p_k[:, :], Wkkn[:, :], Wkkn[:, :], ALU.mult), [Wkkn.k()], [p_k.k()])
            for (c0, c1) in TB:
                ps = psm()
                pe(lambda e, ps=ps, c0=c0, c1=c1: e.matmul(ps[:, 0:c1 - c0], bd, p_k[:, c0:c1], start=True, stop=True), [cst.k(), p_k.k()], [ps.k()])
                dve(lambda e, ps=ps, c0=c0, c1=c1: e.tensor_scalar(p_v[:, c0:c1], ps[:, 0:c1 - c0], 1e-24, None, ALU.max), [ps.k()], [p_v.k()])
            act(lambda e: e.activation(p_v[:, :], p_v[:, :], AF.Sqrt), [p_v.k()], [p_v.k()])
            dve(lambda e: e.reciprocal(p_k[:, :], p_v[:, :]), [p_v.k()], [p_k.k()])
            dve(lambda e: e.tensor_tensor(Wkkn[:, :], Wkkn[:, :], p_k[:, :], ALU.mult), [Wkkn.k(), p_k.k()], [Wkkn.k()])
            dve(lambda e: e.tensor_scalar(p_k[:, :], a_sig[:, :], spt[:, KAc + hp:KAc + hp + 1], oka[:, hp:hp + 1], ALU.mult, ALU.add), [a_sig.k(), spt.k(), oka.k()], [p_k.k()])
            dve(lambda e: e.tensor_tensor(Wk2[:, :], Wk2[:, :], p_k[:, :], ALU.mult), [Wk2.k(), p_k.k()], [Wk2.k()])
            mark(f"B{l}pre{hp}")
            dve(lambda e: e.memset(Sp[:, :], 0.0), [], [Sp.k()])
            for ci in range(8):
                unit(hp, "p", ci, True, lambda g: (Sp[:, :], Sp.k()), corr=True)
            xi = l * 8 + hp
            exchange(None, None, Sp, Sp[:, :], Wc, Wc[:, :], cin_s[xi], cout_s[xi], None)
            dve(lambda e: e.tensor_scalar(Wc[:, :], Wc[:, :], flag[:, 0:1], None, ALU.mult), [Wc.k(), flag.k()], [Wc.k()])
            for ci in range(8):
                c0_ = 2 + 128 * ci
                rpt = RPv[ci // 4]; rc0 = (ci % 4) * 128
                pyc = psx()
                pe(lambda e, pyc=pyc, rpt=rpt, rc0=rc0: e.matmul(pyc[:, 0:128], Wc[:, :], rpt[:, rc0:rc0 + 128], start=True, stop=True), [Wc.k(), rpt.k()], [pyc.k()])
                dve(lambda e, pyc=pyc, c0_=c0_: e.tensor_tensor(yt[:, c0_:c0_ + 128], yt[:, c0_:c0_ + 128], pyc[:, 0:128], ALU.add), [pyc.k(), yt.k()], [yt.k()])
                pw = psx()
                pe(lambda e, pw=pw, ci=ci: e.matmul(pw[:, 0:128], onat[:, :, :].rearrange("p g j -> p (g j)")[:, ci * 128:(ci + 1) * 128], Wc[:, :], start=True, stop=True), [onat.k(), Wc.k()], [pw.k()])
                dve(lambda e, pw=pw: e.tensor_tensor(Wc[:, :], Wc[:, :], pw[:, 0:128], ALU.add), [pw.k(), Wc.k()], [Wc.k()])
                dve(lambda e, ci=ci: e.tensor_scalar(Wc[:, :], Wc[:, :], gam[:, ci:ci + 1], None, ALU.mult), [Wc.k(), gam.k()], [Wc.k()])
            dve(lambda e: e.tensor_tensor(Sp[:, :], Sp[:, :], Wc[:, :], ALU.add), [Sp.k(), Wc.k()], [Sp.k()])
            mark(f"B{l}p2x_{hp}")
            state_out(hp, Sp[:, :], Sp.k(), 0)
            mark(f"B{l}p2_{hp}")
            dma(lambda e: e.dma_start(out=Ss[0:64, :, 0:64], in_=wkv_in[l, :, 2 * hp].rearrange("s j i -> j s i")), [], [Ss.k()])
            dma(lambda e: e.dma_start(out=Ss[64:128, :, 64:128], in_=wkv_in[l, :, 2 * hp + 1].rearrange("s j i -> j s i")), [], [Ss.k()])
            unit(hp, "s", 0, True, lambda g: (Ss[:, g, :], Ss.k()))
            for g in range(NSQ):
                state_out(hp, Ss[:, g, :], Ss.k(), 1 + g)
            dma(lambda e: e.dma_start(out=wkvo[l, :, 2 * hp:2 * hp + 2].rearrange("g h j i -> (h j) g i"), in_=onat[:, :, :]), [onat.k()], [("wkvo", (l, hp))])
            mark(f"B{l}s_{hp}")
            dve(lambda e: e.tensor_tensor(p_k[:, :], yt[:, :], yt[:, :], ALU.mult), [yt.k()], [p_k.k()])
            for (c0, c1) in TB:
                ps = psm()
                pe(lambda e, ps=ps, c0=c0, c1=c1: e.matmul(ps[:, 0:c1 - c0], bd, yt[:, c0:c1], start=True, stop=True), [cst.k(), yt.k()], [ps.k()])
                act(lambda e, ps=ps, c0=c0, c1=c1: e.activation(p_v[:, c0:c1], ps[:, 0:c1 - c0], AF.Copy, scale=1.0 / 64), [ps.k()], [p_v.k()])
                ps2 = psm()
                pe(lambda e, ps2=ps2, c0=c0, c1=c1: e.matmul(ps2[:, 0:c1 - c0], bd, p_k[:, c0:c1], start=True, stop=True), [cst.k(), p_k.k()], [ps2.k()])
                act(lambda e, ps2=ps2, c0=c0, c1=c1: e.activation(p_r[:, c0:c1], ps2[:, 0:c1 - c0], AF.Copy, scale=1.0 / 64), [ps2.k()], [p_r.k()])
            dve(lambda e: e.tensor_tensor(p_k[:, :], p_v[:, :], p_v[:, :], ALU.mult), [p_v.k()], [p_k.k()])
            dve(lambda e: e.tensor_tensor(p_r[:, :], p_r[:, :], p_k[:, :], ALU.subtract), [p_r.k(), p_k.k()], [p_r.k()])
            dve(lambda e: e.tensor_scalar(p_r[:, :], p_r[:, :], 0.0, None, ALU.max), [p_r.k()], [p_r.k()])
            act(lambda e: e.activation(p_r[:, :], p_r[:, :], AF.Sqrt, bias=eps_t[:, 1:2], scale=1.0), [p_r.k(), eps_t.k()], [p_r.k()])
            dve(lambda e: e.reciprocal(p_k[:, :], p_r[:, :]), [p_r.k()], [p_k.k()])
            dve(lambda e: e.tensor_tensor(yt[:, :], yt[:, :], p_v[:, :], ALU.subtract), [yt.k(), p_v.k()], [yt.k()])
            dve(lambda e: e.tensor_tensor(yt[:, :], yt[:, :], p_k[:, :], ALU.mult), [yt.k(), p_k.k()], [yt.k()])
            dve(lambda e: e.tensor_scalar(yt[:, :], yt[:, :], spt[:, LNW + hp:LNW + hp + 1], spt[:, LNB + hp:LNB + hp + 1], ALU.mult, ALU.add), [yt.k(), spt.k()], [yt.k()])
            dve(lambda e: e.scalar_tensor_tensor(p_k[:, :], Wr[:, :], spt[:, RKc + hp:RKc + hp + 1], Wk2[:, :], ALU.mult, ALU.mult), [Wr.k(), Wk2.k(), spt.k()], [p_k.k()])
            for (c0, c1) in TB:
                ps = psm()
                pe(lambda e, ps=ps, c0=c0, c1=c1: e.matmul(ps[:, 0:c1 - c0], bd, p_k[:, c0:c1], start=True, stop=True), [cst.k(), p_k.k()], [ps.k()])
                dve(lambda e, ps=ps, c0=c0, c1=c1: e.tensor_tensor(p_v[:, c0:c1], ps[:, 0:c1 - c0], Wv[:, c0:c1], ALU.mult), [ps.k(), Wv.k()], [p_v.k()])
            dve(lambda e: e.tensor_tensor(yt[:, :], yt[:, :], p_v[:, :], ALU.add), [yt.k(), p_v.k()], [yt.k()])
            for (c0, c1) in TA:
                ps = psm()
                for k2 in range(2):
                    pe(lambda e, ps=ps, c0=c0, c1=c1, k2=k2: e.matmul(ps[:, 0:c1 - c0], gl2t[:, k2, hc], gt[:, k2, c0:c1], start=(k2 == 0), stop=(k2 == 1)), [gl2t.k(), gt.k()], [ps.k()])
                dve(lambda e, ps=ps, c0=c0, c1=c1: e.tensor_tensor(o3[:, hp, c0:c1], ps[:, 0:c1 - c0], yt[:, c0:c1], ALU.mult), [ps.k(), yt.k()], [og.k(hp)])
            mark(f"B{l}hp{hp}")
        for fc in range(8):
            wbg = load_w(24 + fc)
            wcg = load_w(32 + fc)
            whc = load_w(40 + fc)
            proj(wbg, p_r, TB); proj(wcg, p_k, TB); proj(whc, p_v, TB)
            dve(lambda e: e.tensor_tensor(yt[:, :], p_k[:, :], p_v[:, :], ALU.mult), [p_k.k(), p_v.k()], [yt.k()])
            dma(lambda e: e.dma_start(out=sview(yt[:, :])[:, :, 0:2], in_=convT[l, fc * 128:(fc + 1) * 128]), [], [yt.k()])
            cw = CW + fc * 3
            dve(lambda e: e.tensor_scalar(a_sig[:, 2:E], yt[:, 0:E - 2], spt[:, cw:cw + 1], None, ALU.mult), [yt.k(), spt.k()], [a_sig.k()])
            dve(lambda e: e.scalar_tensor_tensor(a_sig[:, 2:E], yt[:, 1:E - 1], spt[:, cw + 1:cw + 2], a_sig[:, 2:E], ALU.mult, ALU.add), [yt.k(), spt.k(), a_sig.k()], [a_sig.k()])
            dve(lambda e: e.scalar_tensor_tensor(a_sig[:, 2:E], yt[:, 2:E], spt[:, cw + 2:cw + 3], a_sig[:, 2:E], ALU.mult, ALU.add), [yt.k(), spt.k(), a_sig.k()], [a_sig.k()])
            dve(lambda e: e.tensor_tensor(o3[:, 8 + fc, 2:E], a_sig[:, 2:E], p_r[:, 2:E], ALU.mult), [a_sig.k(), p_r.k()], [og.k(8 + fc)])
            dma(lambda e: e.dma_start(out=convo[l, fc * 128:(fc + 1) * 128, 0, :], in_=yt[:, SB0 - 2:SB0]), [yt.k()], [("convo", (l, fc, 0))])
            dma(lambda e: e.dma_start(out=convo[l, fc * 128:(fc + 1) * 128, 1:NGRP, :], in_=sview(yt[:, :])[:, :, 4:6]), [yt.k()], [("convo", (l, fc, 1))])

    def phase_CD(l, st, x, md):
        fub = [sb(st, f"fub{i}", [128, KC, 128], BF16) for i in range(4)]
        wob = fub[0:2]
        ua = sb(st, "ua", [128, E]); ub = sb(st, "ub", [128, E]); ca = sb(st, "ca", [128, E]); cbb = sb(st, "cbb", [128, E])
        for fo in range(KC):
            wb = wob[fo % 2]
            dmac(lambda e, wb=wb, fo=fo: e.dma_start(out=wb[:, :, :], in_=w_out[l, fo].rearrange("p (k n) -> p k n", n=128)), [], [wb.k()])
            for (c0, c1) in TB:
                ps = psm()
                for kc in range(KC):
                    pe(lambda e, ps=ps, wb=wb, kc=kc, c0=c0, c1=c1: e.matmul(ps[:, 0:c1 - c0], wb[:, kc, :], o3[:, kc, c0:c1], start=(kc == 0), stop=(kc == KC - 1)), [wb.k(), og.k(kc)], [ps.k()])
                resid(x, fo, ps, c0, c1, md["GA1"], st)
        hl2 = None
        norm_mod(st, x, md["A2"], md["B2"], None, None, "b", pre=(ua, ub, [ca, cbb]))
        mark(f"C{l}")
        halo_exchange(st, 2 * l + 1)
        g3 = og[:, 0:15 * E].rearrange("p (k e) -> p k e", e=E)
        fdb = [sb(st, f"fdb{i}", [128, 15, 128], BF16) for i in range(2)]
        fctr = [0]
        hb0 = 0
        for nh in (15, 15, 14):
            for hbl in range(nh):
                hb = hb0 + hbl
                res = []
                for half, dst, cdst in ((0, ua, ca), (1, ub, cbb)):
                    blk = half * 44 + hb
                    wb = fub[fctr[0] % 4]; fctr[0] += 1
                    dmac(lambda e, wb=wb, blk=blk: e.dma_start(out=wb[:, :, :], in_=ffn_up[l, blk].rearrange("p (k n) -> p k n", n=128)), [], [wb.k()])
                    for (c0, c1) in TB:
                        ps = psm()
                        for kc in range(KC):
                            pe(lambda e, ps=ps, wb=wb, kc=kc, c0=c0, c1=c1: e.matmul(ps[:, 0:c1 - c0], wb[:, kc, :], h[:, kc, c0:c1], start=(kc == 0), stop=(kc == KC - 1)), [wb.k(), h.k(kc)], [ps.k()])
                        act(lambda e, ps=ps, c0=c0, c1=c1, dst=dst: e.activation(dst[:, c0:c1], ps[:, 0:c1 - c0], AF.Copy), [ps.k()], [dst.k()])
                    dma(lambda e, dst=dst, blk=blk: e.dma_start(out=sview(dst[:, :])[:, :, 0:2], in_=ffnT[l, blk * 128:(blk + 1) * 128]), [], [dst.k()])
                    fw = FCW + blk * 3
                    dve(lambda e, dst=dst, cdst=cdst, fw=fw: e.tensor_scalar(cdst[:, 2:E], dst[:, 0:E - 2], spt[:, fw:fw + 1], None, ALU.mult), [dst.k(), spt.k()], [cdst.k()])
                    dve(lambda e, dst=dst, cdst=cdst, fw=fw: e.scalar_tensor_tensor(cdst[:, 2:E], dst[:, 1:E - 1], spt[:, fw + 1:fw + 2], cdst[:, 2:E], ALU.mult, ALU.add), [dst.k(), spt.k(), cdst.k()], [cdst.k()])
                    dve(lambda e, dst=dst, cdst=cdst, fw=fw: e.scalar_tensor_tensor(cdst[:, 2:E], dst[:, 2:E], spt[:, fw + 2:fw + 3], cdst[:, 2:E], ALU.mult, ALU.add), [dst.k(), spt.k(), cdst.k()], [cdst.k()])
                    dma(lambda e, dst=dst, blk=blk: e.dma_start(out=ffno[l, blk * 128:(blk + 1) * 128, 0, :], in_=dst[:, SB0 - 2:SB0]), [dst.k()], [("ffno", (l, blk, 0))])
                    dma(lambda e, dst=dst, blk=blk: e.dma_start(out=ffno[l, blk * 128:(blk + 1) * 128, 1:NGRP, :], in_=sview(dst[:, :])[:, :, 4:6]), [dst.k()], [("ffno", (l, blk, 1))])
                act(lambda e: e.activation(ua[:, 2:E], ca[:, 2:E], AF.Silu), [ca.k()], [ua.k()])
                dve(lambda e, hbl=hbl: e.tensor_tensor(g3[:, hbl, 2:E], ua[:, 2:E], cbb[:, 2:E], ALU.mult), [ua.k(), cbb.k()], [og.k(hbl)])
            for fo in range(KC):
                wb = fdb[fo % 2]
                dmac(lambda e, wb=wb, fo=fo, nh=nh, hb0=hb0: e.dma_start(out=wb[:, 0:nh, :], in_=ffn_down[l, fo, :, hb0 * 128:(hb0 + nh) * 128].rearrange("p (k n) -> p k n", n=128)), [], [wb.k()])
                for (c0, c1) in [(2, 376), (376, 750), (750, E)]:
                    ps = psm()
                    for kc in range(nh):
                        pe(lambda e, ps=ps, wb=wb, kc=kc, c0=c0, c1=c1, nh=nh: e.matmul(ps[:, 0:c1 - c0], wb[:, kc, :], g3[:, kc, c0:c1], start=(kc == 0), stop=(kc == nh - 1)), [wb.k(), og.k(kc)], [ps.k()])
                    resid(x, fo, ps, c0, c1, md["GA2"], st)
            hb0 += nh
        return ua, ub, ca, cbb

    rs_tmp = {}

    def resid(x, fo, ps, c0, c1, GA, st):
        p1 = min(c1, SB0)
        if c0 < p1:
            dve(lambda e: e.scalar_tensor_tensor(x[:, fo, c0:p1], ps[:, 0:p1 - c0], GA[:, fo, 0:1], x[:, fo, c0:p1], ALU.mult, ALU.add), [ps.k(), GA.k(), x.k(fo)], [x.k(fo)])
        if c1 > SB0:
            s0 = max(c0, SB0)
            assert s0 == SB0 and c1 == E
            key = id(st)
            if key not in rs_tmp:
                rs_tmp[key] = sb(st, "rs_tmp", [128, 96])
            rt = rs_tmp[key]
            off = SB0 - c0
            dve(lambda e: e.tensor_tensor(rt[:, :].rearrange("p (s t) -> p s t", t=6), ps[:, off:off + 96].rearrange("p (s t) -> p s t", t=6), GA[:, fo, 1:NGRP].unsqueeze(2).to_broadcast([128, NSQ, 6]), ALU.mult), [ps.k(), GA.k()], [rt.k()])
            dve(lambda e: e.tensor_tensor(x[:, fo, SB0:E], x[:, fo, SB0:E], rt[:, :], ALU.add), [rt.k(), x.k(fo)], [x.k(fo)])

    def main_program():
        mark("init")
        for l in range(2):
            with ExitStack() as stA:
                x = sb(stA, f"xA{l}", [128, KC, E])
                if l == 0:
                    dma(lambda e: e.dma_start(out=x[:, :, :], in_=xT.rearrange("(k p) n -> p k n", p=128)), [], [x.k()])
                else:
                    dma(lambda e: e.dma_start(out=x[:, :, :], in_=x_sp.rearrange("(k p) n -> p k n", p=128)), [("x_sp", None)], [x.k()])
                with ExitStack() as st:
                    md = phase_A(l, st, x, l == 0)
                    hl = sb(st, "hl", [128, KC, NGRP])
                    norm_mod(st, x, md["A1"], md["B1"], None, hl, "a")
                    dma(lambda e: e.dma_start(out=shifto[l].rearrange("(k p) s -> p k s", p=128), in_=hl[:, :, :]), [hl.k()], [("shifto", l)])
                    sst = sb(st, "sst", [128, KC, NSQ])
                    dma(lambda e: e.dma_start(out=sst[:, :, :], in_=shiftT[l].rearrange("(k p) s -> p k s", p=128)), [], [sst.k()])
                    for fc in range(KC):
                        dve(lambda e, fc=fc: e.tensor_copy(sview(h[:, fc, :])[:, :, 1], sst[:, fc, :]), [sst.k()], [h.k(fc)])
                    mark(f"A{l}n")
                    halo_exchange(st, 2 * l)
                    mark(f"A{l}")
                    dma(lambda e: e.dma_start(out=x_sp.rearrange("(k p) n -> p k n", p=128), in_=x[:, :, :]), [x.k()], [("x_sp", None)])
                    keep = {}
                    mdk = {}
                    for nm in ("GA1", "A2", "B2", "GA2"):
                        mdk[nm] = md[nm]
                    for nm in ("GA1", "A2", "B2", "GA2"):
                        t = keep_tiles[nm]
                        dve(lambda e, t=t, nm=nm: e.tensor_copy(t[:, :, :], md[nm][:, :, :]), [md[nm].k()], [t.k()])
                        keep[nm] = t
                    S.barrier()
            with ExitStack() as stB:
                phase_B(l, stB)
                mark(f"B{l}")
                S.barrier()
            with ExitStack() as stC:
                x = sb(stC, f"xC{l}", [128, KC, E])
                dma(lambda e: e.dma_start(out=x[:, :, :], in_=x_sp.rearrange("(k p) n -> p k n", p=128)), [("x_sp", None)], [x.k()])
                fua, fub_, fca, fcb = phase_CD(l, stC, x, keep)
                if l == 0:
                    dma(lambda e: e.dma_start(out=x_sp.rearrange("(k p) n -> p k n", p=128), in_=x[:, :, :]), [x.k()], [("x_sp", None)])
                else:
                    load_fin = None
                    rstd = fua; tmp = fub_; sq = [fca, fcb]
                    n = 0
                    for (c0, c1) in TB:
                        ps = psm()
                        for fc in range(KC):
                            q = sq[n % 2]; n += 1
                            act(lambda e, q=q, fc=fc, c0=c0, c1=c1: e.activation(q[:, 0:c1 - c0], x[:, fc, c0:c1], AF.Square), [x.k(fc)], [q.k()])
                            pe(lambda e, ps=ps, q=q, fc=fc, c0=c0, c1=c1: e.matmul(ps[:, 0:c1 - c0], ones, q[:, 0:c1 - c0], start=(fc == 0), stop=(fc == KC - 1)), [q.k(), cst.k()], [ps.k()])
                        act(lambda e, ps=ps, c0=c0, c1=c1: e.activation(tmp[:, c0:c1], ps[:, 0:c1 - c0], AF.Sqrt, bias=eps_t[:, 0:1], scale=1.0 / D), [ps.k(), eps_t.k()], [tmp.k()])
                    dve(lambda e: e.reciprocal(rstd[:, :], tmp[:, :]), [tmp.k()], [rstd.k()])
                    for fc in range(KC):
                        dve(lambda e, fc=fc: e.scalar_tensor_tensor(x[:, fc, :], x[:, fc, :], spt[:, FNG + fc:FNG + fc + 1], rstd[:, :], ALU.mult, ALU.mult), [x.k(fc), spt.k(), rstd.k()], [x.k(fc)])
                    dma(lambda e: e.dma_start(out=yT[:, 0:NPT].rearrange("(k p) n -> p k n", p=128), in_=x[:, :, 2:SB0]), [x.k()], [("yT", 0)])
                    for fc in range(KC):
                        dma(lambda e, fc=fc: e.dma_start(out=yT[fc * 128:(fc + 1) * 128, NPT:NPT + 64].rearrange("p (s t) -> p s t", t=4), in_=sview(x[:, fc, :])[:, :, 2:6]), [x.k(fc)], [("yT", 1 + fc)])
                S.barrier()

    try:
        main_program()
    except _StopBuild:
        pass
    S.stopped = False
    S.barrier()
    S.emit()
    root.close()
    return nc


_PROG = [None]


def _consts():
    c = np.zeros((128, NCONST), np.float32)
    idx = np.arange(128)
    c[:, C_ID:C_ID + 128] = np.eye(128)
    c[:, C_ONES:C_ONES + 128] = 1.0
    c[:, C_BD:C_BD + 128] = (idx[:, None] // 64 == idx[None, :] // 64)
    su = (idx[:, None] < idx[None, :]).astype(np.float32)
    iu = (idx[:, None] <= idx[None, :]).astype(np.float32)
    c[:, C_MUP:C_MUP + 128] = su; c[:, C_MUP + 128:C_MUP + 256] = iu
    c[:, C_MLP:C_MLP + 128] = (idx[None, :] < idx[:, None])
    c[:, C_UTP:C_UTP + 128] = -DEC * iu; c[:, C_UTP + 128:C_UTP + 256] = -DEC * su
    i64 = np.arange(64)
    same = (i64[:, None] // 4 == i64[None, :] // 4).astype(np.float32)
    su6 = su[:64, :64] * same; iu6 = iu[:64, :64] * same
    c[:64, C_MUS:C_MUS + 64] = su6; c[:64, C_MUS + 64:C_MUS + 128] = iu6
    c[:64, C_MLS:C_MLS + 64] = (i64[None, :] < i64[:, None]) * same
    c[:64, C_UTS:C_UTS + 64] = -DEC * iu6; c[:64, C_UTS + 64:C_UTS + 128] = -DEC * su6
    cm = (i64[None, :] // 4 == np.arange(16)[:, None]).astype(np.float32)
    c[:, C_CMS:C_CMS + 1024] = cm.reshape(1, 1024)
    c[:64, C_RMS:C_RMS + 16] = (i64[:, None] // 4 == np.arange(16)[None, :])
    return c


def kernel(x_prompt, x_sample, c_prompt, c_sample, state_wkv, state_shift, state_conv, state_ffn,
           ada_w, ada_b, norm_g, final_norm_g, w_in, mu_x, mu_rkv, decay_w0, decay_lora1, decay_lora2,
           iclr_a0, iclr_lora1, iclr_lora2, gate_lora1, gate_lora2, vres_v0, vres_lora1, vres_lora2,
           k_k, k_a, r_k, ln_x_w, ln_x_b, conv_w, w_out, ffn_up, ffn_conv, ffn_down):
    f = lambda a: np.ascontiguousarray(np.asarray(a, dtype=np.float32))
    (x_prompt, x_sample, c_prompt, c_sample, state_wkv, state_shift, state_conv, state_ffn) = map(
        f, (x_prompt, x_sample, c_prompt, c_sample, state_wkv, state_shift, state_conv, state_ffn))
    if _PROG[0] is None:
        _PROG[0] = build_program()
    nc = _PROG[0]
    pk = lambda v, n: np.asarray(v, np.float32).reshape(n, 128).T
    smallp = np.zeros((2, 128, NSP), np.float32)
    for l in range(2):
        sp = smallp[l]
        sp[:, ADAB:ADAB + 96] = pk(ada_b[l], 96)
        sp[:, NG0:NG0 + 16] = pk(norm_g[l, 0], 16); sp[:, NG1:NG1 + 16] = pk(norm_g[l, 1], 16)
        sp[:, MUX:MUX + 64] = np.asarray(mu_x[l]).reshape(4, 16, 128).transpose(2, 1, 0).reshape(128, 64)
        sp[:, MURKV:MURKV + 24] = np.asarray(mu_rkv[l]).reshape(3, 8, 128).transpose(2, 0, 1).reshape(128, 24)
        sp[:, A0:A0 + 8] = pk(iclr_a0[l], 8); sp[:, V0:V0 + 8] = pk(vres_v0[0], 8)
        sp[:, KKc:KKc + 8] = pk(k_k[l], 8); sp[:, KAc:KAc + 8] = pk(k_a[l], 8); sp[:, RKc:RKc + 8] = pk(r_k[l], 8)
        sp[:, LNW:LNW + 8] = pk(ln_x_w[l], 8); sp[:, LNB:LNB + 8] = pk(ln_x_b[l], 8)
        sp[:, CW:CW + 24] = np.asarray(conv_w[l]).reshape(3, 8, 128).transpose(2, 1, 0).reshape(128, 24)
        sp[:, FCW:FCW + 264] = np.asarray(ffn_conv[l]).reshape(3, 88, 128).transpose(2, 1, 0).reshape(128, 264)
        sp[:, FNG:FNG + 16] = pk(final_norm_g, 16)
    consts = _consts()
    def tile_w(w, nk):
        w = np.asarray(w, np.float32)
        nb = w.shape[2] // 128
        return np.ascontiguousarray(w.reshape(2, nk, 128, nb, 128).transpose(0, 3, 2, 1, 4).reshape(2, nb, 128, nk * 128))
    shared = dict(smallp=smallp, w0row=f(decay_w0).reshape(2, 1, G), consts=consts, ada_w=tile_w(ada_w, KC), w_in=tile_w(w_in, KC),
                  decay_lora1=f(decay_lora1), decay_lora2=f(decay_lora2), iclr_lora1=f(iclr_lora1), iclr_lora2=f(iclr_lora2),
                  gate_lora1=f(gate_lora1), gate_lora2=f(gate_lora2), vres_lora1=f(vres_lora1), vres_lora2=f(vres_lora2),
                  w_out=tile_w(w_out, KC), ffn_up=tile_w(ffn_up, KC), ffn_down=tile_w(ffn_down, 44))
    in_maps = []
    for c in range(8):
        seq, half = c // 2, c % 2
        xT = np.zeros((D, E), np.float32)
        xT[:, 2:SB0] = x_prompt[seq, half * NPT:(half + 1) * NPT, :].T
        xs = x_sample[16 * c:16 * c + 16]
        xv = xT[:, SB0:].reshape(D, 16, 6)
        xv[:, :, 2:6] = xs.transpose(2, 0, 1)
        cT = np.zeros((D, NGRP), np.float32)
        cT[:, 0] = c_prompt[seq]; cT[:, 1:] = c_sample[16 * c:16 * c + 16].T
        m = dict(shared)
        m.update(xT=xT, cT=cT, flag=np.full((128, 1), float(half), np.float32),
                 shiftT=f(state_shift[:, 16 * c:16 * c + 16, :].transpose(0, 2, 1)),
                 convT=f(state_conv[:, 16 * c:16 * c + 16].transpose(0, 3, 1, 2)),
                 ffnT=f(state_ffn[:, 16 * c:16 * c + 16].transpose(0, 3, 1, 2)),
                 wkv_in=f(state_wkv[:, 16 * c:16 * c + 16].transpose(0, 1, 2, 4, 3)))
        in_maps.append(m)
    res = run_bass_kernel_spmd(nc, in_maps, core_ids=list(range(8))).results
    y_p = np.zeros((4, 2048, D), np.float32); y_s = np.zeros((128, 4, D), np.float32)
    wkv_p = np.zeros((2, 4, 16, 64, 64), np.float32); wkv_s = np.zeros((2, 128, 16, 64, 64), np.float32)
    sh_p = np.zeros((2, 4, D), np.float32); sh_s = np.zeros((2, 128, D), np.float32)
    cv_p = np.zeros((2, 4, 2, G), np.float32); cv_s = np.zeros((2, 128, 2, G), np.float32)
    ff_p = np.zeros((2, 4, 2, 2 * FF), np.float32); ff_s = np.zeros((2, 128, 2, 2 * FF), np.float32)
    for c in range(8):
        seq, half = c // 2, c % 2
        r = res[c]
        yTo = np.asarray(r["yT"])
        y_p[seq, half * NPT:(half + 1) * NPT] = yTo[:, :NPT].T
        y_s[16 * c:16 * c + 16] = yTo[:, NPT:].reshape(D, 16, 4).transpose(1, 2, 0)
        wk = np.asarray(r["wkvo"]).transpose(0, 1, 2, 4, 3); sh = np.asarray(r["shifto"]); cvo = np.asarray(r["convo"]); ffo = np.asarray(r["ffno"])
        wkv_s[:, 16 * c:16 * c + 16] = wk[:, 1:]
        sh_s[:, 16 * c:16 * c + 16] = sh[:, :, 1:].transpose(0, 2, 1)
        cv_s[:, 16 * c:16 * c + 16] = cvo[:, :, 1:, :].transpose(0, 2, 3, 1)
        ff_s[:, 16 * c:16 * c + 16] = ffo[:, :, 1:, :].transpose(0, 2, 3, 1)
        if half == 1:
            wkv_p[:, seq] = wk[:, 0]
            sh_p[:, seq] = sh[:, :, 0]
            cv_p[:, seq] = cvo[:, :, 0, :].transpose(0, 2, 1)
            ff_p[:, seq] = ffo[:, :, 0, :].transpose(0, 2, 1)
    return (y_p, y_s, wkv_p, sh_p, cv_p, ff_p, wkv_s, sh_s, cv_s, ff_s)
```

```python
import numpy as np
import concourse.bass as bass
import concourse.mybir as mybir
from concourse.bass_utils import run_bass_kernel_spmd
from contextlib import ExitStack

F32 = mybir.dt.float32
BF16 = mybir.dt.bfloat16
ALU = mybir.AluOpType
AF = mybir.ActivationFunctionType

D = 2048; KC = 16; G = 1024; FF = 5632; NPT = 1024; NSQ = 16; E = 1122; SB0 = 1026
NGRP = 17
ADAB = 0; NG0 = 96; NG1 = 112; MUX = 128; MURKV = 192; A0 = 216; V0 = 224; KKc = 232; KAc = 240
RKc = 248; LNW = 256; LNB = 264; CW = 272; FCW = 296; FNG = 560; NSP = 576
C_ID = 0; C_ONES = 128; C_BD = 256; C_MUP = 384; C_MLP = 640; C_UTP = 768; C_MUS = 1024; C_MLS = 1152
C_UTS = 1216; C_CMS = 1344; C_RMS = 2368; NCONST = 2384
DEC = 0.6065306597126334


class _Rec:
    def __init__(self):
        self.call = None

    def __getattr__(self, name):
        def f(*a, **k):
            assert self.call is None
            self.call = (name, a, k)
            return None
        return f


class Sched:
    NDMA = 40

    def __init__(self, nc):
        self.nc = nc
        self.ops = []
        self.state = {}
        self.pending_dma = []
        self.floor = {}
        self.stopped = False
        self.last_op = {}

    def _access(self, op, key, write):
        name, sub = key
        ents = self.state.setdefault(name, [])
        hit = [e for e in ents if e[0] is None or sub is None or e[0] == sub]
        for e in hit:
            if e[1] is not None:
                op["deps"].add(e[1])
            if write:
                op["deps"].update(e[2])
        if write:
            for e in hit:
                ents.remove(e)
            ents.append([sub, op["i"], []])
        else:
            if not hit:
                ents.append([sub, None, [op["i"]]])
            else:
                for e in hit:
                    if len(e[2]) > 8:
                        e[2][:] = [x for x in e[2] if self.ops[x]["eng"] != op["eng"] or self.ops[x]["dma"]]
                    e[2].append(op["i"])

    def add(self, eng, fn, r=(), w=(), dma=False, coll=False, extra=()):
        if self.stopped:
            return dict(i=-1)
        if fn is not None:
            rec = _Rec()
            fn(rec)
            name_, a_, k_ = rec.call
            fn = lambda e, name_=name_, a_=a_, k_=k_: getattr(e, name_)(*a_, **k_)
        op = dict(eng=eng, fn=fn, deps=set(extra), dma=dma, coll=coll, i=len(self.ops), sig=None, ndep=0)
        self.ops.append(op)
        for k in r:
            self._access(op, k, isinstance(k[0], str) and k[0].startswith("ps") and k[0][2:].isdigit())
        for k in w:
            self._access(op, k, True)
        op["deps"].discard(op["i"])
        if dma:
            self.pending_dma.append(op["i"])
        elif fn is not None:
            self.last_op[eng] = op["i"]
        return op

    def barrier(self):
        if self.stopped:
            return
        engs = ["pe", "act", "dve", "pool", "sp"]
        deps = list(self.pending_dma)
        self.pending_dma = []
        for e in engs:
            if self.last_op.get(e) is not None:
                deps.append(self.last_op[e])
        for e in engs:
            self.add(e, None, extra=deps)
        self.state = {}

    def _skip(self, p, c):
        return p["eng"] == "pe" and c["eng"] == "pe" and not p["dma"] and not c["dma"]

    def emit(self):
        nc = self.nc
        ops = self.ops
        for op in ops:
            for d in op["deps"]:
                if not self._skip(ops[d], op):
                    ops[d]["ndep"] += 1
        with ExitStack() as st:
            esem = {e: st.enter_context(nc.semaphore("se_" + e)) for e in ["pe", "act", "dve", "pool", "sp"]}
            dsem = [st.enter_context(nc.semaphore(f"sd{i}")) for i in range(self.NDMA)]
            cnt = {e: 0 for e in esem}
            dcnt = [0] * self.NDMA
            nd = 0
            csem = {}
            for op in ops:
                if op["dma"]:
                    if op["coll"]:
                        csem[op["i"]] = st.enter_context(nc.semaphore(f"sc{op['i']}"))
                        op["sig"] = (("c", op["i"]), 1)
                    else:
                        s = nd % self.NDMA
                        nd += 1
                        dcnt[s] += 16
                        op["sig"] = (("d", s), dcnt[s])
                elif op["ndep"] > 0:
                    cnt[op["eng"]] += 1
                    op["sig"] = (("e", op["eng"]), cnt[op["eng"]])

            def semof(k):
                if k[0] == "c":
                    return csem[k[1]]
                return dsem[k[1]] if k[0] == "d" else esem[k[1]]

            def run(en, e):
                waited = {}
                for op in ops:
                    if op["eng"] != en:
                        continue
                    need = {}
                    for d in op["deps"]:
                        p = ops[d]
                        if self._skip(p, op):
                            continue
                        k, v = p["sig"]
                        if waited.get(k, 0) < v:
                            need[k] = max(need.get(k, 0), v)
                    if op["dma"] and not op["coll"]:
                        k, v = op["sig"]
                        if v - 16 > 0 and waited.get(k, 0) < v - 16:
                            need[k] = max(need.get(k, 0), v - 16)
                    for k, v in need.items():
                        e.wait_ge(semof(k), v)
                        waited[k] = v
                    if op["fn"] is None:
                        assert op["sig"] is None
                        continue
                    ins = op["fn"](e)
                    if op["sig"] is not None:
                        k, v = op["sig"]
                        ins.then_inc(semof(k), (1 if op["coll"] else 16) if op["dma"] else 1)

            with nc.Block() as block:
                @block.tensor
                def _(e):
                    run("pe", e)

                @block.scalar
                def _(e):
                    run("act", e)

                @block.vector
                def _(e):
                    run("dve", e)

                @block.gpsimd
                def _(e):
                    run("pool", e)

                @block.sync
                def _(e):
                    run("sp", e)


class T:
    def __init__(self, name, t):
        self.name = name
        self.t = t

    def k(self, sub=None):
        return (self.name, sub)

    def __getitem__(self, idx):
        return self.t[idx]


DEBUG_STOP = None


class _StopBuild(Exception):
    pass


_SCHED = [None]


def mark(name):
    if DEBUG_STOP == name:
        _SCHED[0].stopped = True


def build_program():
    nc = bass.Bass("TRN2", target_bir_lowering=False)
    S = Sched(nc)
    _SCHED[0] = S

    def din(name, shape):
        return nc.dram_tensor(name, shape, F32, kind="ExternalInput").ap()

    def dout(name, shape):
        return nc.dram_tensor(name, shape, F32, kind="ExternalOutput").ap()

    def dint(name, shape):
        return nc.dram_tensor(name, shape, F32, kind="Internal").ap()

    xT = din("xT", [D, E]); cT = din("cT", [D, NGRP]); flag_d = din("flag", [128, 1])
    shiftT = din("shiftT", [2, D, NSQ]); convT = din("convT", [2, G, NSQ, 2]); ffnT = din("ffnT", [2, 2 * FF, NSQ, 2])
    wkv_in = din("wkv_in", [2, NSQ, 16, 64, 64]); smallp = din("smallp", [2, 128, NSP]); w0row = din("w0row", [2, 1, G])
    consts_d = din("consts", [128, NCONST])
    ada_w = din("ada_w", [2, 96, 128, KC * 128]); w_in = din("w_in", [2, 48, 128, KC * 128])
    dl1 = din("decay_lora1", [2, D, 96]); dl2 = din("decay_lora2", [2, 96, G])
    il1 = din("iclr_lora1", [2, D, 96]); il2 = din("iclr_lora2", [2, 96, G])
    gl1 = din("gate_lora1", [2, D, 256]); gl2 = din("gate_lora2", [2, 256, G])
    vl1 = din("vres_lora1", [1, D, 64]); vl2 = din("vres_lora2", [1, 64, G])
    w_out = din("w_out", [2, 16, 128, KC * 128]); ffn_up = din("ffn_up", [2, 88, 128, KC * 128]); ffn_down = din("ffn_down", [2, 16, 128, 44 * 128])
    yT = dout("yT", [D, NPT + 64]); wkvo = dout("wkvo", [2, NGRP, 16, 64, 64]); shifto = dout("shifto", [2, D, NGRP])
    convo = dout("convo", [2, G, NGRP, 2]); ffno = dout("ffno", [2, 2 * FF, NGRP, 2])
    x_sp = dint("x_sp", [D, E]); vf_d = dint("vf_d", [8, 128, E])
    cin_s = [dint(f"cin_s{i}", [128, 128]) for i in range(16)]
    cout_s = [dint(f"cout_s{i}", [256, 128]) for i in range(16)]
    cin_h = [dint(f"cin_h{i}", [128, 32]) for i in range(4)]
    cout_h = [dint(f"cout_h{i}", [256, 32]) for i in range(4)]
    groups = [[0, 1], [2, 3], [4, 5], [6, 7]]
    OUTK = [("yT", None), ("wkvo", None), ("shifto", None), ("convo", None), ("ffno", None)]

    root = ExitStack()

    uid = [0]

    def sb(st, name, shape, dt=F32):
        uid[0] += 1
        nm = f"s{uid[0]}_{name}"
        return T(nm, st.enter_context(nc.sbuf_tensor(nm, shape, dt)))

    def dve(fn, r, w): S.add("dve", fn, r, w)
    def act(fn, r, w): S.add("act", fn, r, w)
    def pe(fn, r, w): S.add("pe", fn, r, w)
    def pool(fn, r, w): S.add("pool", fn, r, w)
    def dma(fn, r, w): S.add("sp", fn, r, w, dma=True)
    def dmac(fn, r, w): S.add("pool", fn, r, w, dma=True)

    h = sb(root, "h", [128, KC, E], BF16)
    og = sb(root, "og", [128, KC * E], BF16)
    o3 = og[:, :].rearrange("p (k e) -> p k e", e=E)
    cst = sb(root, "cst", [128, NCONST])
    spt = sb(root, "spt", [128, NSP])
    flag = sb(root, "flag", [128, 1])
    psb = [T(f"ps{i}", root.enter_context(nc.psum_tensor(f"ps{i}", [128, 512], F32))) for i in range(8)]
    pctr = [0, 0]

    def psm():
        pctr[0] = (pctr[0] + 1) % 4
        return psb[pctr[0]]

    def psx():
        pctr[1] = (pctr[1] + 1) % 4
        return psb[4 + pctr[1]]

    dma(lambda e: e.dma_start(out=cst[:, :], in_=consts_d), [], [cst.k()])
    dma(lambda e: e.dma_start(out=flag[:, :], in_=flag_d), [], [flag.k()])
    ident = cst[:, C_ID:C_ID + 128]; ones = cst[:, C_ONES:C_ONES + 128]; bd = cst[:, C_BD:C_BD + 128]
    TA = [(1, 375), (375, 749), (749, E)]
    TB = [(0, 374), (374, 748), (748, E)]
    RKV_T = [(1, 386), (385, 770), (769, 1026), (1026, E)]

    def sview(ap2d):
        return ap2d[:, SB0:E].rearrange("p (s t) -> p s t", t=6)

    def exchange(idx_list_in, idx_list_out, src_tile, src_ap, dst_tile, dst_ap, cin, cout, rows_cols):
        dma(lambda e: e.dma_start(out=cin, in_=src_ap), [src_tile.k()], [(cin.name, None)])
        S.add("pool", lambda e: e.collective_compute("AllGather", ALU.bypass, replica_groups=groups, ins=[cin], outs=[cout]),
              [(cin.name, None)], [(cout.name, None)], dma=True, coll=True)
        dma(lambda e: e.dma_start(out=dst_ap, in_=cout[0:128, :]), [(cout.name, None)], [dst_tile.k()])

    xcm = [None]

    def load_small(l):
        dma(lambda e: e.dma_start(out=spt[:, :], in_=smallp[l]), [], [spt.k()])

    def phase_A(l, st, x, first):
        load_small(l)
        modt = sb(st, f"modt", [128, 96, NGRP])
        sc = sb(st, "sc", [128, KC, NGRP]); scb = sb(st, "scb", [128, KC, NGRP], BF16)
        dma(lambda e: e.dma_start(out=sc[:, :, :], in_=cT.rearrange("(k p) s -> p k s", p=128)), [], [sc.k()])
        act(lambda e: e.activation(scb[:, :, :], sc[:, :, :], AF.Silu), [sc.k()], [scb.k()])
        mark(f"A{l}s")
        awb = [sb(st, f"awb{i}", [128, KC, 128], BF16) for i in range(3)]
        for j in range(96):
            wb = awb[j % 3]
            dmac(lambda e, wb=wb, j=j: e.dma_start(out=wb[:, :, :], in_=ada_w[l, j].rearrange("p (k n) -> p k n", n=128)), [], [wb.k()])
            for jj in range(1):
                ps = psm()
                for kc in range(KC):
                    pe(lambda e, ps=ps, wb=wb, kc=kc, jj=jj: e.matmul(ps[:, 0:NGRP], wb[:, kc, :], scb[:, kc, :], start=(kc == 0), stop=(kc == KC - 1)), [wb.k(), scb.k()], [ps.k()])
                jb = j
                dve(lambda e, ps=ps, jb=jb: e.tensor_scalar(modt[:, jb, :], ps[:, 0:NGRP], spt[:, ADAB + jb:ADAB + jb + 1], None, ALU.add), [ps.k(), spt.k()], [modt.k(jb)])
                mark(f"A{l}m{jb}")
        mark(f"A{l}m")
        md = {}
        for nm, sci, ngo in (("A1", 16, NG0), ("A2", 64, NG1)):
            t = sb(root if False else st, nm, [128, KC, NGRP])
            md[nm] = t
            dve(lambda e, t=t, sci=sci: e.tensor_scalar(t[:, :, :], modt[:, sci:sci + 16, :], 1.0, None, ALU.add), [modt.k()], [t.k()])
            dve(lambda e, t=t, ngo=ngo: e.tensor_tensor(t[:, :, :], t[:, :, :], spt[:, ngo:ngo + 16].unsqueeze(2).to_broadcast([128, KC, NGRP]), ALU.mult), [t.k(), spt.k()], [t.k()])
        for nm, off in (("B1", 0), ("GA1", 32), ("B2", 48), ("GA2", 80)):
            t = sb(st, nm, [128, KC, NGRP])
            md[nm] = t
            dve(lambda e, t=t, off=off: e.tensor_copy(t[:, :, :], modt[:, off:off + 16, :]), [modt.k()], [t.k()])
        mark(f"A{l}d")
        return md

    keep_tiles = {nm: sb(root, "keep_" + nm, [128, KC, NGRP]) for nm in ("GA1", "A2", "B2", "GA2")}
    eps_t = sb(root, "eps_t", [128, 2])
    dve(lambda e: e.memset(eps_t[:, 0:1], 1e-6), [], [eps_t.k()])
    dve(lambda e: e.memset(eps_t[:, 1:2], 64e-5), [], [eps_t.k()])

    def norm_mod(st, x, A, B, gcol_unused, hl, tagsfx="", pre=None):
        if pre is None:
            rstd = sb(st, "rstd" + tagsfx, [128, E]); tmp = sb(st, "ntmp" + tagsfx, [128, E])
            sq = [sb(st, f"sq{i}" + tagsfx, [128, 512]) for i in range(2)]
        else:
            rstd, tmp, sq = pre
        n = 0
        for (c0, c1) in TB:
            ps = psm()
            for fc in range(KC):
                q = sq[n % 2]; n += 1
                act(lambda e, q=q, fc=fc, c0=c0, c1=c1: e.activation(q[:, 0:c1 - c0], x[:, fc, c0:c1], AF.Square), [x.k(fc)], [q.k()])
                pe(lambda e, ps=ps, q=q, fc=fc, c0=c0, c1=c1: e.matmul(ps[:, 0:c1 - c0], ones, q[:, 0:c1 - c0], start=(fc == 0), stop=(fc == KC - 1)), [q.k(), cst.k()], [ps.k()])
            act(lambda e, ps=ps, c0=c0, c1=c1: e.activation(tmp[:, c0:c1], ps[:, 0:c1 - c0], AF.Sqrt, bias=eps_t[:, 0:1], scale=1.0 / D), [ps.k(), eps_t.k()], [tmp.k()])
        dve(lambda e: e.reciprocal(rstd[:, :], tmp[:, :]), [tmp.k()], [rstd.k()])
        for fc in range(KC):
            dve(lambda e, fc=fc: e.tensor_tensor(tmp[:, :], x[:, fc, :], rstd[:, :], ALU.mult), [x.k(fc), rstd.k()], [tmp.k()])
            act(lambda e, fc=fc: e.activation(h[:, fc, 0:SB0], tmp[:, 0:SB0], AF.Identity, bias=B[:, fc, 0:1], scale=A[:, fc, 0:1]), [tmp.k(), A.k(), B.k()], [h.k(fc)])
            if hl is not None:
                act(lambda e, fc=fc: e.activation(hl[:, fc, 0:1], tmp[:, SB0 - 1:SB0], AF.Identity, bias=B[:, fc, 0:1], scale=A[:, fc, 0:1]), [tmp.k(), A.k(), B.k()], [hl.k()])
            ts = sview(tmp[:, :])
            dve(lambda e, fc=fc, ts=ts: e.tensor_tensor(ts, ts, A[:, fc, 1:NGRP].unsqueeze(2).to_broadcast([128, NSQ, 6]), ALU.mult), [tmp.k(), A.k()], [tmp.k()])
            if hl is not None:
                dve(lambda e, fc=fc, ts=ts: e.tensor_tensor(hl[:, fc, 1:NGRP], ts[:, :, 5], B[:, fc, 1:NGRP], ALU.add), [tmp.k(), B.k()], [hl.k()])
            dve(lambda e, fc=fc, ts=ts: e.tensor_tensor(sview(h[:, fc, :]), ts, B[:, fc, 1:NGRP].unsqueeze(2).to_broadcast([128, NSQ, 6]), ALU.add), [tmp.k(), B.k()], [h.k(fc)])

    def halo_exchange(st, ci, extra_cols=None):
        hs = sb(st, f"hs{ci}", [128, 32]); hr = sb(st, f"hr{ci}", [128, 32])
        dve(lambda e: e.tensor_copy(hs[:, :].rearrange("p (k t) -> p k t", t=2), h[:, :, SB0 - 2:SB0]), [h.k()], [hs.k()])
        exchange(None, None, hs, hs[:, :], hr, hr[:, :], cin_h[ci], cout_h[ci], None)
        dve(lambda e: e.tensor_scalar(h[:, :, 0:2], hr[:, :].rearrange("p (k t) -> p k t", t=2), flag[:, 0:1], None, ALU.mult), [hr.k(), flag.k()], [h.k()])

    def phase_B(l, st):
        nl = 512 if l == 1 else 448
        Wa = og[:, 0:KC * 512].rearrange("p (k n) -> p k n", n=512)
        Wb = og[:, KC * 512:2 * KC * 512].rearrange("p (k n) -> p k n", n=512)
        stg = [sb(st, f"stg{i}", [128, 512]) for i in range(2)]
        omx = sb(st, "omx", [128, 64])
        dve(lambda e: e.tensor_scalar(omx[:, :], spt[:, MUX:MUX + 64], -1.0, 1.0, ALU.mult, ALU.add), [spt.k()], [omx.k()])
        segs = [(dl1, 0, 96, 0), (il1, 96, 96, 1), (gl1, 192, 256, 2)] + ([(vl1, 448, 64, 3)] if l == 1 else [])
        for kc in range(KC):
            sg = stg[kc % 2]
            for (wd, c0, n, mi) in segs:
                ll = 0 if wd is vl1 else l
                dma(lambda e, sg=sg, wd=wd, ll=ll, c0=c0, n=n, kc=kc: e.dma_start(out=sg[:, c0:c0 + n], in_=wd[ll, kc * 128:(kc + 1) * 128, :]), [], [sg.k()])
            for (wd, c0, n, mi) in segs:
                dve(lambda e, sg=sg, c0=c0, n=n, mi=mi, kc=kc: e.tensor_scalar(Wa[:, kc, c0:c0 + n], sg[:, c0:c0 + n], omx[:, kc * 4 + mi:kc * 4 + mi + 1], None, ALU.mult), [sg.k(), omx.k()], [og.k()])
                dve(lambda e, sg=sg, c0=c0, n=n, mi=mi, kc=kc: e.tensor_scalar(Wb[:, kc, c0:c0 + n], sg[:, c0:c0 + n], spt[:, MUX + kc * 4 + mi:MUX + kc * 4 + mi + 1], None, ALU.mult), [sg.k(), spt.k()], [og.k()])
        th = sb(st, "th", [96, E], BF16); ia = sb(st, "ia", [96, E], BF16); gt = sb(st, "gt", [128, 2, E], BF16); vr = sb(st, "vr", [64, E], BF16)
        lgroups = [(0, 96, "th"), (96, 96, "ia"), (192, 128, "g0"), (320, 128, "g1")] + ([(448, 64, "vr")] if l == 1 else [])
        for (c0g, m, nm) in lgroups:
            for (c0, c1) in TA:
                ps = psm()
                for kc in range(KC):
                    pe(lambda e, ps=ps, kc=kc, c0=c0, c1=c1, c0g=c0g, m=m: e.matmul(ps[0:m, 0:c1 - c0], Wa[:, kc, c0g:c0g + m], h[:, kc, c0:c1], start=(kc == 0), stop=False), [og.k(), h.k(kc)], [ps.k()])
                    pe(lambda e, ps=ps, kc=kc, c0=c0, c1=c1, c0g=c0g, m=m: e.matmul(ps[0:m, 0:c1 - c0], Wb[:, kc, c0g:c0g + m], h[:, kc, c0 - 1:c1 - 1], start=False, stop=(kc == KC - 1)), [og.k(), h.k(kc)], [ps.k()])
                if nm == "th":
                    act(lambda e, ps=ps, c0=c0, c1=c1: e.activation(th[:, c0:c1], ps[0:96, 0:c1 - c0], AF.Tanh), [ps.k()], [th.k()])
                elif nm == "ia":
                    act(lambda e, ps=ps, c0=c0, c1=c1: e.activation(ia[:, c0:c1], ps[0:96, 0:c1 - c0], AF.Copy), [ps.k()], [ia.k()])
                elif nm == "vr":
                    act(lambda e, ps=ps, c0=c0, c1=c1: e.activation(vr[:, c0:c1], ps[0:64, 0:c1 - c0], AF.Copy), [ps.k()], [vr.k()])
                else:
                    gi = 0 if nm == "g0" else 1
                    act(lambda e, ps=ps, c0=c0, c1=c1, gi=gi: e.activation(gt[:, gi, c0:c1], ps[:, 0:c1 - c0], AF.Sigmoid), [ps.k()], [gt.k()])
        mark(f"B{l}L")
        dl2t = sb(st, "dl2t", [96, G], BF16); il2t = sb(st, "il2t", [96, G], BF16); gl2t = sb(st, "gl2t", [128, 2, G], BF16); vl2t = sb(st, "vl2t", [64, G], BF16)
        dmac(lambda e: e.dma_start(out=dl2t[:, :], in_=dl2[l]), [], [dl2t.k()])
        dmac(lambda e: e.dma_start(out=il2t[:, :], in_=il2[l]), [], [il2t.k()])
        dmac(lambda e: e.dma_start(out=gl2t[:, :, :], in_=gl2[l].rearrange("(k p) n -> p k n", p=128)), [], [gl2t.k()])
        if l == 1:
            dmac(lambda e: e.dma_start(out=vl2t[:, :], in_=vl2[0]), [], [vl2t.k()])
        w0bc = sb(st, "w0bc", [128, G])
        dma(lambda e: e.dma_start(out=w0bc[:, :], in_=w0row[l].partition_broadcast(128)), [], [w0bc.k()])
        omr = sb(st, "omr", [128, 24])
        dve(lambda e: e.tensor_scalar(omr[:, :], spt[:, MURKV:MURKV + 24], -1.0, 1.0, ALU.mult, ALU.add), [spt.k()], [omr.k()])
        oka = sb(st, "oka", [128, 8])
        dve(lambda e: e.tensor_scalar(oka[:, :], spt[:, KAc:KAc + 8], -1.0, 1.0, ALU.mult, ALU.add), [spt.k()], [oka.k()])

        wbuf = [sb(st, f"wbuf{i}", [128, KC, 128], BF16) for i in range(3)]
        wctr = [0]

        def load_w(blk):
            wb = wbuf[wctr[0] % 3]; wctr[0] += 1
            dmac(lambda e, wb=wb: e.dma_start(out=wb[:, :, :], in_=w_in[l, blk].rearrange("p (k n) -> p k n", n=128)), [], [wb.k()])
            return wb

        def proj(wb, dst, tiles):
            for (c0, c1) in tiles:
                ps = psm()
                for kc in range(KC):
                    pe(lambda e, ps=ps, kc=kc, c0=c0, c1=c1: e.matmul(ps[:, 0:c1 - c0], wb[:, kc, :], h[:, kc, c0:c1], start=(kc == 0), stop=(kc == KC - 1)), [wb.k(), h.k(kc)], [ps.k()])
                act(lambda e, ps=ps, c0=c0, c1=c1: e.activation(dst[:, c0:c1], ps[:, 0:c1 - c0], AF.Copy), [ps.k()], [dst.k()])

        names = ["Wr", "Wk2", "Wv", "Wkkn", "p_r", "p_k", "p_v", "a_sig", "yt"]
        Wt = {n: sb(st, n, [128, E]) for n in names}
        Wr, Wk2, Wv, Wkkn, p_r, p_k, p_v, a_sig, yt = [Wt[n] for n in names]
        for t_ in (Wr, Wk2, Wv, Wkkn, a_sig, yt, p_r, p_k, p_v):
            dve(lambda e, t_=t_: e.memset(t_[:, :], 0.0), [], [t_.k()])

        def cb(name, shape, dt=F32):
            return sb(st, name, shape, dt)
        thc = cb("thc", [96, 64], BF16); zt = cb("zt", [128, 128]); lw = cb("lw", [128, 128])
        E1 = cb("E1", [128, 128]); E0 = cb("E0", [128, 128]); Ei = cb("Ei", [128, 128])
        AR = cb("AR", [128, 256]); bt = cb("bt", [128, 128]); kt = cb("kt", [128, 128]); vc = cb("vc", [128, 64])
        Bt = cb("Bt", [128, 128]); Kt = cb("Kt", [128, 128]); Vt = cb("Vt", [128, 128])
        Vp = [cb(f"Vp{i}", [128, 128]) for i in range(2)]; Up = [cb(f"Up{i}", [128, 128]) for i in range(2)]
        for t_ in Vp + Up:
            dve(lambda e, t_=t_: e.memset(t_[:, :], 0.0), [], [t_.k()])
        LkA = cb("LkA", [128, 2, 256]); LbA = cb("LbA", [128, 2, 256])
        PP = [cb(f"PP{i}", [128, 4, 128]) for i in range(2)]
        TT = [cb(f"TT{i}", [128, 2, 128]) for i in range(2)]
        At = cb("At", [128, 128]); Gs = cb("Gs", [128, 128]); Gp = [cb(f"Gp{i}", [128, 128]) for i in range(2)]
        for t_ in Gp:
            dve(lambda e, t_=t_: e.memset(t_[:, :], 0.0), [], [t_.k()])
        gam = cb("gam", [128, 8]); Wc = cb("Wc", [128, 128])
        RPv = [stg[0], stg[1]]
        Xs = cb("Xs", [128, 128]); Us = cb("Us", [128, 128]); Y1 = cb("Y1", [128, 64]); tS = cb("tS", [128, 128])
        Sp = cb("Sp", [128, 128])
        Ss = cb("Ss", [128, NSQ, 128]); Sbd = cb("Sbd", [128, 4, 128])
        dve(lambda e: e.memset(Ss[:, :, :], 0.0), [], [Ss.k()])
        onat = cb("onat", [128, NGRP, 64])
        Apg = [cb(f"Apg{i}", [128, 64]) for i in range(2)]; Btg = [cb(f"Btg{i}", [64, 128]) for i in range(2)]; Ktg = [cb(f"Ktg{i}", [64, 128]) for i in range(2)]

        def unit(hp, kind, ci, full, Sget, corr=False):
            if kind == "p":
                C = 128; c0 = 2 + 128 * ci; nd = 6; Gn = 1
                cv = lambda t2: t2[:, c0:c0 + C]
                cm = lambda a: a
                MU = cst[0:128, C_MUP:C_MUP + 256]; ML = cst[0:128, C_MLP:C_MLP + 128]; UT = cst[0:128, C_UTP:C_UTP + 256]
            else:
                C = 64; nd = 1; Gn = NSQ
                cv = lambda t2: sview(t2)[:, :, 2:6]
                cm = lambda a: a.rearrange("p (s t) -> p s t", t=4)
                MU = cst[0:64, C_MUS:C_MUS + 128]; ML = cst[0:64, C_MLS:C_MLS + 64]; UT = cst[0:64, C_UTS:C_UTS + 128]
            hc = slice(hp * 128, (hp + 1) * 128)
            ps = psx()
            if kind == "p":
                pe(lambda e: e.matmul(ps[0:C, 0:128], th[:, c0:c0 + C], dl2t[:, hc], start=True, stop=True), [th.k(), dl2t.k()], [ps.k()])
            else:
                dve(lambda e: e.tensor_copy(cm(thc[:, 0:64]), cv(th[:, :])), [th.k()], [thc.k()])
                pe(lambda e: e.matmul(ps[0:C, 0:128], thc[:, 0:C], dl2t[:, hc], start=True, stop=True), [thc.k(), dl2t.k()], [ps.k()])
            dve(lambda e: e.tensor_tensor(zt[0:C, :], ps[0:C, 0:128], w0bc[0:C, hc], ALU.add), [ps.k(), w0bc.k()], [zt.k()])
            act(lambda e: e.activation(lw[0:C, :], zt[0:C, :], AF.Exp, scale=-1.0), [zt.k()], [lw.k()])
            dve(lambda e: e.tensor_scalar(lw[0:C, :], lw[0:C, :], 1.0, None, ALU.add), [lw.k()], [lw.k()])
            dve(lambda e: e.reciprocal(lw[0:C, :], lw[0:C, :]), [lw.k()], [lw.k()])
            ps2 = psx()
            pe(lambda e: e.matmul(ps2[:, 0:2 * C], lw[0:C, :], UT, start=True, stop=True), [lw.k(), cst.k()], [ps2.k()])
            act(lambda e: e.activation(E1[:, 0:C], ps2[:, 0:C], AF.Exp), [ps2.k()], [E1.k()])
            act(lambda e: e.activation(E0[:, 0:C], ps2[:, C:2 * C], AF.Exp), [ps2.k()], [E0.k()])
            act(lambda e: e.activation(Ei[:, 0:C], ps2[:, 0:C], AF.Exp, scale=-1.0), [ps2.k()], [Ei.k()])
            dve(lambda e: e.scalar_tensor_tensor(cm(AR[:, 0:C]), cv(Wkkn[:, :]), -1.0, cm(E0[:, 0:C]), ALU.mult, ALU.mult), [Wkkn.k(), E0.k()], [AR.k()])
            dve(lambda e: e.tensor_tensor(cm(AR[:, C:2 * C]), cv(Wr[:, :]), cm(E1[:, 0:C]), ALU.mult), [Wr.k(), E1.k()], [AR.k()])
            dve(lambda e: e.tensor_tensor(cm(bt[:, 0:C]), cv(Wkkn[:, :]), cv(a_sig[:, :]), ALU.mult), [Wkkn.k(), a_sig.k()], [bt.k()])
            dve(lambda e: e.tensor_tensor(bt[:, 0:C], bt[:, 0:C], Ei[:, 0:C], ALU.mult), [bt.k(), Ei.k()], [bt.k()])
            dve(lambda e: e.tensor_tensor(cm(kt[:, 0:C]), cv(Wk2[:, :]), cm(Ei[:, 0:C]), ALU.mult), [Wk2.k(), Ei.k()], [kt.k()])
            if kind == "p":
                vsrc = Wv[:, c0:c0 + C]; vk = Wv.k()
            else:
                dve(lambda e: e.tensor_copy(cm(vc[:, 0:64]), cv(Wv[:, :])), [Wv.k()], [vc.k()])
                vsrc = vc[:, 0:C]; vk = vc.k()
            ps3 = psx()
            pe(lambda e: e.transpose(ps3[0:C, 0:128], bt[:, 0:C], ident), [bt.k(), cst.k()], [ps3.k()])
            pe(lambda e: e.transpose(ps3[0:C, 128:256], kt[:, 0:C], ident), [kt.k(), cst.k()], [ps3.k()])
            pe(lambda e: e.transpose(ps3[0:C, 256:384], vsrc, ident), [vk, cst.k()], [ps3.k()])
            act(lambda e: e.activation(Bt[0:C, :], ps3[0:C, 0:128], AF.Copy), [ps3.k()], [Bt.k()])
            act(lambda e: e.activation(Kt[0:C, :], ps3[0:C, 128:256], AF.Copy), [ps3.k()], [Kt.k()])
            dve(lambda e: e.tensor_copy(Vt[0:C, :], ps3[0:C, 256:384]), [ps3.k()], [Vt.k()])
            dve(lambda e: e.tensor_copy(Vp[0][0:C, 0:64], ps3[0:C, 256:320]), [ps3.k()], [Vp[0].k()])
            dve(lambda e: e.tensor_copy(Vp[1][0:C, 64:128], ps3[0:C, 320:384]), [ps3.k()], [Vp[1].k()])
            pa = psx(); pb = psx(); pc = psx()
            for hh in range(2):
                pr = slice(64 * hh, 64 * hh + 64)
                pe(lambda e, pr=pr, hh=hh: e.matmul(pa[0:C, 256 * hh:256 * hh + 2 * C], kt[pr, 0:C], AR[pr, 0:2 * C], start=True, stop=True), [kt.k(), AR.k()], [pa.k()])
                pe(lambda e, pr=pr, hh=hh: e.matmul(pb[0:C, 256 * hh:256 * hh + 2 * C], bt[pr, 0:C], AR[pr, 0:2 * C], start=True, stop=True), [bt.k(), AR.k()], [pb.k()])
                pe(lambda e, pr=pr, hh=hh: e.matmul(pc[0:C, 128 * hh:128 * hh + C], AR[pr, 0:C], bt[pr, 0:C], start=True, stop=True), [bt.k(), AR.k()], [pc.k()])
            MUb = MU.unsqueeze(1).to_broadcast([C, 2, 2 * C]); MLb = ML.unsqueeze(1).to_broadcast([C, 2, C])
            v256 = lambda p_: p_[0:C, 0:512].rearrange("p (s n) -> p s n", n=256)[:, :, 0:2 * C]
            v128 = lambda p_, ns: p_[0:C, 0:128 * ns].rearrange("p (s n) -> p s n", n=128)[:, :, 0:C]
            dve(lambda e: e.tensor_tensor(LbA[0:C, :, 0:2 * C], v256(pb), MUb, ALU.mult), [pb.k(), cst.k()], [LbA.k()])
            dve(lambda e: e.tensor_tensor(PP[0][0:C, 0:2, 0:C], v128(pc, 2), MLb, ALU.mult), [pc.k(), cst.k()], [PP[0].k()])
            dve(lambda e: e.tensor_tensor(LkA[0:C, :, 0:2 * C], v256(pa), MUb, ALU.mult), [pa.k(), cst.k()], [LkA.k()])
            dve(lambda e: e.tensor_tensor(TT[0][0:C, :, 0:C], LbA[0:C, :, 0:C], cst[0:C, C_ID:C_ID + C].unsqueeze(1).to_broadcast([C, 2, C]), ALU.add), [LbA.k(), cst.k()], [TT[0].k()])
            cur = 0
            for it in range(1, nd + 1):
                nxt = 1 - cur
                qs = psx()
                for hh in range(2):
                    Pt_ap = LbA[0:C, hh, 0:C] if it == 1 else PP[cur][0:C, 2 + hh, 0:C]
                    Pt_k = LbA.k() if it == 1 else PP[cur].k()
                    pe(lambda e, hh=hh, Pt_ap=Pt_ap, cur=cur: e.matmul(qs[0:C, 128 * hh:128 * hh + C], Pt_ap, PP[cur][0:C, hh, 0:C], start=True, stop=True), [Pt_k, PP[cur].k()], [qs.k()])
                if it < nd:
                    for hh in range(2):
                        Pt_ap = LbA[0:C, hh, 0:C] if it == 1 else PP[cur][0:C, 2 + hh, 0:C]
                        Pt_k = LbA.k() if it == 1 else PP[cur].k()
                        pe(lambda e, hh=hh, Pt_ap=Pt_ap, cur=cur: e.matmul(qs[0:C, 256 + 128 * hh:256 + 128 * hh + C], PP[cur][0:C, hh, 0:C], Pt_ap, start=True, stop=True), [Pt_k, PP[cur].k()], [qs.k()])
                if it > 1:
                    qt = psx()
                    ti = (it - 2) % 2
                    for hh in range(2):
                        pe(lambda e, hh=hh, cur=cur, ti=ti: e.matmul(qt[0:C, 128 * hh:128 * hh + C], PP[cur][0:C, hh, 0:C], TT[ti][0:C, hh, 0:C], start=True, stop=True), [PP[cur].k(), TT[ti].k()], [qt.k()])
                    dve(lambda e, qt=qt, ti=ti: e.tensor_tensor(TT[1 - ti][0:C, :, 0:C], v128(qt, 2), TT[ti][0:C, :, 0:C], ALU.add), [qt.k(), TT[ti].k()], [TT[1 - ti].k()])
                nsl = 4 if it < nd else 2
                act(lambda e, qs=qs, nxt=nxt, nsl=nsl: e.activation(PP[nxt][0:C, 0:nsl, 0:C], v128(qs, nsl), AF.Copy), [qs.k()], [PP[nxt].k()])
                cur = nxt
            qt = psx()
            ti = (nd - 1) % 2
            for hh in range(2):
                pe(lambda e, hh=hh, cur=cur, ti=ti: e.matmul(qt[0:C, 128 * hh:128 * hh + C], PP[cur][0:C, hh, 0:C], TT[ti][0:C, hh, 0:C], start=True, stop=True), [PP[cur].k(), TT[ti].k()], [qt.k()])
            TF = TT[1 - ti]
            dve(lambda e: e.tensor_tensor(TF[0:C, :, 0:C], v128(qt, 2), TT[ti][0:C, :, 0:C], ALU.add), [qt.k(), TT[ti].k()], [TF.k()])
            if corr:
                pat = psx()
                pe(lambda e: e.transpose(pat[0:C, 0:128], AR[:, 0:C], ident), [AR.k(), cst.k()], [pat.k()])
                act(lambda e: e.activation(At[0:C, :], pat[0:C, 0:128], AF.Copy), [pat.k()], [At.k()])
                pg = psx()
                for hh in range(2):
                    pe(lambda e, hh=hh: e.matmul(pg[0:C, 64 * hh:64 * hh + 64], TF[0:C, hh, 0:C], At[0:C, 64 * hh:64 * hh + 64], start=True, stop=True), [TF.k(), At.k()], [pg.k()])
                dve(lambda e: e.tensor_copy(Gs[0:C, :], pg[0:C, 0:128]), [pg.k()], [Gs.k()])
                act(lambda e: e.activation(Gp[0][0:C, 0:64], pg[0:C, 0:64], AF.Copy), [pg.k()], [Gp[0].k()])
                act(lambda e: e.activation(Gp[1][0:C, 64:128], pg[0:C, 64:128], AF.Copy), [pg.k()], [Gp[1].k()])
                prp = psx()
                for hh in range(2):
                    pe(lambda e, hh=hh: e.matmul(prp[:, 0:C], Gp[hh][0:C, :], LbA[0:C, hh, C:2 * C], start=(hh == 0), stop=(hh == 1)), [Gp[hh].k(), LbA.k()], [prp.k()])
                rpt = RPv[ci // 4]; rc0 = (ci % 4) * 128
                dve(lambda e: e.tensor_tensor(rpt[:, rc0:rc0 + C], prp[:, 0:C], AR[:, C:2 * C], ALU.add), [prp.k(), AR.k()], [rpt.k()])
                pnt = psx()
                pe(lambda e: e.matmul(pnt[:, 0:128], Gs[0:C, :], Bt[0:C, :], start=True, stop=True), [Gs.k(), Bt.k()], [pnt.k()])
                dve(lambda e: e.tensor_tensor(onat[:, :, :].rearrange("p g j -> p (g j)")[:, ci * 128:(ci + 1) * 128], pnt[:, 0:128], bd, ALU.mult), [pnt.k(), cst.k()], [onat.k()])
                dve(lambda e: e.tensor_copy(gam[:, ci:ci + 1], E1[:, C - 1:C]), [E1.k()], [gam.k()])
            px = psx()
            for g in range(Gn):
                Sg_ap, Sg_k = Sget(g)
                if kind == "p":
                    lhs = AR[:, 0:C]; lk = AR.k()
                else:
                    ap_ = Apg[g % 2]
                    dve(lambda e, ap_=ap_, g=g: e.tensor_tensor(ap_[:, :], AR[:, 0:64], cst[:, C_CMS + g * 64:C_CMS + (g + 1) * 64], ALU.mult), [AR.k(), cst.k()], [ap_.k()])
                    lhs = ap_[:, :]; lk = ap_.k()
                pe(lambda e, lhs=lhs, Sg_ap=Sg_ap, g=g: e.matmul(px[0:C, 0:128], lhs, Sg_ap, start=(g == 0), stop=(kind == "s" and g == Gn - 1)), [lk, Sg_k], [px.k()])
            if kind == "p":
                for hh in range(2):
                    pe(lambda e, hh=hh: e.matmul(px[0:C, 0:128], LkA[0:C, hh, 0:C], Vp[hh][0:C, :], start=False, stop=(hh == 1)), [LkA.k(), Vp[hh].k()], [px.k()])
                dve(lambda e: e.tensor_copy(Xs[0:C, :], px[0:C, 0:128]), [px.k()], [Xs.k()])
            else:
                px2 = psx()
                for hh in range(2):
                    pe(lambda e, hh=hh: e.matmul(px2[0:C, 0:128], LkA[0:C, hh, 0:C], Vp[hh][0:C, :], start=(hh == 0), stop=(hh == 1)), [LkA.k(), Vp[hh].k()], [px2.k()])
                dve(lambda e: e.tensor_copy(Xs[0:C, :], px[0:C, 0:128]), [px.k()], [Xs.k()])
                dve(lambda e: e.tensor_tensor(Xs[0:C, :], Xs[0:C, :], px2[0:C, 0:128], ALU.add), [px2.k(), Xs.k()], [Xs.k()])
            pu = psx()
            for hh in range(2):
                pe(lambda e, hh=hh: e.matmul(pu[0:C, 64 * hh:64 * hh + 64], TF[0:C, hh, 0:C], Xs[0:C, 64 * hh:64 * hh + 64], start=True, stop=True), [TF.k(), Xs.k()], [pu.k()])
            dve(lambda e: e.tensor_copy(Us[0:C, :], pu[0:C, 0:128]), [pu.k()], [Us.k()])
            if full:
                act(lambda e: e.activation(Up[0][0:C, 0:64], pu[0:C, 0:64], AF.Copy), [pu.k()], [Up[0].k()])
                act(lambda e: e.activation(Up[1][0:C, 64:128], pu[0:C, 64:128], AF.Copy), [pu.k()], [Up[1].k()])
                py = psx()
                for hh in range(2):
                    pe(lambda e, hh=hh: e.matmul(py[:, 0:C], Up[hh][0:C, :], LbA[0:C, hh, C:2 * C], start=(hh == 0), stop=False), [Up[hh].k(), LbA.k()], [py.k()])
                    pe(lambda e, hh=hh: e.matmul(py[:, 0:C], Vp[hh][0:C, :], LkA[0:C, hh, C:2 * C], start=False, stop=(hh == 1 and kind == "s")), [Vp[hh].k(), LkA.k()], [py.k()])
                if kind == "p":
                    Sg_ap, Sg_k = Sget(0)
                    pe(lambda e, Sg_ap=Sg_ap: e.matmul(py[:, 0:C], Sg_ap, AR[:, C:2 * C], start=False, stop=True), [Sg_k, AR.k()], [py.k()])
                    act(lambda e: e.activation(yt[:, c0:c0 + C], py[:, 0:C], AF.Copy), [py.k()], [yt.k()])
                else:
                    py1 = psx()
                    for g in range(Gn):
                        Sg_ap, Sg_k = Sget(g)
                        pe(lambda e, Sg_ap=Sg_ap, g=g: e.matmul(py1[:, 4 * g:4 * g + 4], Sg_ap, AR[:, C + 4 * g:C + 4 * g + 4], start=True, stop=True), [Sg_k, AR.k()], [py1.k()])
                    act(lambda e: e.activation(Y1[:, 0:64], py1[:, 0:64], AF.Copy), [py1.k()], [Y1.k()])
                    dve(lambda e: e.tensor_tensor(cv(yt[:, :]), cm(py[:, 0:64]), cm(Y1[:, 0:64]), ALU.add), [py.k(), Y1.k()], [yt.k()])
            for g in range(Gn):
                Sg_ap, Sg_k = Sget(g)
                pq = psx()
                if kind == "p":
                    lb_ = Bt[0:C, :]; lk_ = Kt[0:C, :]; kb = Bt.k(); kk_ = Kt.k()
                else:
                    bg_ = Btg[g % 2]; kg_ = Ktg[g % 2]
                    dve(lambda e, bg_=bg_, g=g: e.tensor_scalar(bg_[:, :], Bt[0:64, :], cst[0:64, C_RMS + g:C_RMS + g + 1], None, ALU.mult), [Bt.k(), cst.k()], [bg_.k()])
                    dve(lambda e, kg_=kg_, g=g: e.tensor_scalar(kg_[:, :], Kt[0:64, :], cst[0:64, C_RMS + g:C_RMS + g + 1], None, ALU.mult), [Kt.k(), cst.k()], [kg_.k()])
                    lb_ = bg_[:, :]; lk_ = kg_[:, :]; kb = bg_.k(); kk_ = kg_.k()
                pe(lambda e, pq=pq, lb_=lb_: e.matmul(pq[:, 0:128], lb_, Us[0:C, :], start=True, stop=False), [kb, Us.k()], [pq.k()])
                pe(lambda e, pq=pq, lk_=lk_: e.matmul(pq[:, 0:128], lk_, Vt[0:C, :], start=False, stop=True), [kk_, Vt.k()], [pq.k()])
                dve(lambda e, pq=pq: e.tensor_tensor(tS[:, :], pq[:, 0:128], bd, ALU.mult), [pq.k(), cst.k()], [tS.k()])
                dve(lambda e, Sg_ap=Sg_ap: e.tensor_tensor(Sg_ap, Sg_ap, tS[:, :], ALU.add), [Sg_k, tS.k()], [Sg_k])
                gcol = (C - 1) if kind == "p" else (4 * g + 3)
                dve(lambda e, Sg_ap=Sg_ap, gcol=gcol: e.tensor_scalar(Sg_ap, Sg_ap, E1[:, gcol:gcol + 1], None, ALU.mult), [Sg_k, E1.k()], [Sg_k])

        def state_out(hp, S_ap, S_k, gi):
            act(lambda e: e.activation(onat[0:64, gi, :], S_ap[0:64, 0:64], AF.Copy), [S_k], [onat.k()])
            dve(lambda e: e.tensor_copy(onat[64:128, gi, :], S_ap[64:128, 64:128]), [S_k], [onat.k()])

        for hp in range(8):
            hc = slice(hp * 128, (hp + 1) * 128)
            wr = load_w(hp)
            wk = load_w(8 + hp)
            wv = load_w(16 + hp)
            proj(wr, p_r, RKV_T); proj(wk, p_k, RKV_T); proj(wv, p_v, RKV_T)
            for (c0, c1) in TA:
                ps = psm()
                pe(lambda e, ps=ps, c0=c0, c1=c1: e.matmul(ps[:, 0:c1 - c0], il2t[:, hc], ia[:, c0:c1], start=True, stop=True), [il2t.k(), ia.k()], [ps.k()])
                act(lambda e, ps=ps, c0=c0, c1=c1: e.activation(a_sig[:, c0:c1], ps[:, 0:c1 - c0], AF.Sigmoid, bias=spt[:, A0 + hp:A0 + hp + 1]), [ps.k(), spt.k()], [a_sig.k()])
            for i, (p_, dst) in enumerate(((p_r, Wr), (p_k, Wk2), (p_v, Wv))):
                mc = MURKV + i * 8 + hp
                dve(lambda e, p_=p_, i=i: e.tensor_scalar(Wkkn[:, 2:E], p_[:, 2:E], omr[:, i * 8 + hp:i * 8 + hp + 1], None, ALU.mult), [p_.k(), omr.k()], [Wkkn.k()])
                dve(lambda e, p_=p_, dst=dst, mc=mc: e.scalar_tensor_tensor(dst[:, 2:E], p_[:, 1:E - 1], spt[:, mc:mc + 1], Wkkn[:, 2:E], ALU.mult, ALU.add), [p_.k(), spt.k(), Wkkn.k()], [dst.k()])
            if l == 0:
                dma(lambda e: e.dma_start(out=vf_d[hp], in_=Wv[:, :]), [Wv.k()], [("vf_d", hp)])
            else:
                dma(lambda e: e.dma_start(out=p_v[:, :], in_=vf_d[hp]), [("vf_d", hp)], [p_v.k()])
                for (c0, c1) in TA:
                    ps = psm()
                    pe(lambda e, ps=ps, c0=c0, c1=c1: e.matmul(ps[:, 0:c1 - c0], vl2t[:, hc], vr[:, c0:c1], start=True, stop=True), [vl2t.k(), vr.k()], [ps.k()])
                    act(lambda e, ps=ps, c0=c0, c1=c1: e.activation(p_r[:, c0:c1], ps[:, 0:c1 - c0], AF.Sigmoid, bias=spt[:, V0 + hp:V0 + hp + 1]), [ps.k(), spt.k()], [p_r.k()])
                dve(lambda e: e.tensor_tensor(p_v[:, 1:E], p_v[:, 1:E], Wv[:, 1:E], ALU.subtract), [p_v.k(), Wv.k()], [p_v.k()])
                dve(lambda e: e.tensor_tensor(p_v[:, 1:E], p_v[:, 1:E], p_r[:, 1:E], ALU.mult), [p_v.k(), p_r.k()], [p_v.k()])
                dve(lambda e: e.tensor_tensor(Wv[:, 1:E], Wv[:, 1:E], p_v[:, 1:E], ALU.add), [p_v.k(), Wv.k()], [Wv.k()])
            dve(lambda e: e.tensor_scalar(Wkkn[:, :], Wk2[:, :], spt[:, KKc + hp:KKc + hp + 1], None, ALU.mult), [Wk2.k(), spt.k()], [Wkkn.k()])
            dve(lambda e: e.tensor_tensor(p_k[:, :], Wkkn[:, :], Wkkn[:, :], ALU.mult), [Wkkn.k()], [p_k.k()])
            for (c0, c1) in TB:
                ps = psm()
                pe(lambda e, ps=ps, c0=c0, c1=c1: e.matmul(ps[:, 0:c1 - c0], bd, p_k[:, c0:c1], start=True, stop=True), [cst.k(), p_k.k()], [ps.k()])
                dve(lambda e, ps=ps, c0=c0, c1=c1: e.tensor_scalar(p_v[:, c0:c1], ps[:, 0:c1 - c0], 1e-24, None, ALU.max), [ps.k()], [p_v.k()])
            act(lambda e: e.activation(p_v[:, :], p_v[:, :], AF.Sqrt), [p_v.k()], [p_v.k()])
            dve(lambda e: e.reciprocal(p_k[:, :], p_v[:, :]), [p_v.k()], [p_k.k()])
            dve(lambda e: e.tensor_tensor(Wkkn[:, :], Wkkn[:, :], p_k[:, :], ALU.mult), [Wkkn.k(), p_k.k()], [Wkkn.k()])
            dve(lambda e: e.tensor_scalar(p_k[:, :], a_sig[:, :], spt[:, KAc + hp:KAc + hp + 1], oka[:, hp:hp + 1], ALU.mult, ALU.add), [a_sig.k(), spt.k(), oka.k()], [p_k.k()])
            dve(lambda e: e.tensor_tensor(Wk2[:, :], Wk2[:, :], p_k[:, :], ALU.mult), [Wk2.k(), p_k.k()], [Wk2.k()])
            mark(f"B{l}pre{hp}")
            dve(lambda e: e.memset(Sp[:, :], 0.0), [], [Sp.k()])
            for ci in range(8):
                unit(hp, "p", ci, True, lambda g: (Sp[:, :], Sp.k()), corr=True)
            xi = l * 8 + hp
            exchange(None, None, Sp, Sp[:, :], Wc, Wc[:, :], cin_s[xi], cout_s[xi], None)
            dve(lambda e: e.tensor_scalar(Wc[:, :], Wc[:, :], flag[:, 0:1], None, ALU.mult), [Wc.k(), flag.k()], [Wc.k()])
            for ci in range(8):
                c0_ = 2 + 128 * ci
                rpt = RPv[ci // 4]; rc0 = (ci % 4) * 128
                pyc = psx()
                pe(lambda e, pyc=pyc, rpt=rpt, rc0=rc0: e.matmul(pyc[:, 0:128], Wc[:, :], rpt[:, rc0:rc0 + 128], start=True, stop=True), [Wc.k(), rpt.k()], [pyc.k()])
                dve(lambda e, pyc=pyc, c0_=c0_: e.tensor_tensor(yt[:, c0_:c0_ + 128], yt[:, c0_:c0_ + 128], pyc[:, 0:128], ALU.add), [pyc.k(), yt.k()], [yt.k()])
                pw = psx()
                pe(lambda e, pw=pw, ci=ci: e.matmul(pw[:, 0:128], onat[:, :, :].rearrange("p g j -> p (g j)")[:, ci * 128:(ci + 1) * 128], Wc[:, :], start=True, stop=True), [onat.k(), Wc.k()], [pw.k()])
                dve(lambda e, pw=pw: e.tensor_tensor(Wc[:, :], Wc[:, :], pw[:, 0:128], ALU.add), [pw.k(), Wc.k()], [Wc.k()])
                dve(lambda e, ci=ci: e.tensor_scalar(Wc[:, :], Wc[:, :], gam[:, ci:ci + 1], None, ALU.mult), [Wc.k(), gam.k()], [Wc.k()])
            dve(lambda e: e.tensor_tensor(Sp[:, :], Sp[:, :], Wc[:, :], ALU.add), [Sp.k(), Wc.k()], [Sp.k()])
            mark(f"B{l}p2x_{hp}")
            state_out(hp, Sp[:, :], Sp.k(), 0)
            mark(f"B{l}p2_{hp}")
            dma(lambda e: e.dma_start(out=Ss[0:64, :, 0:64], in_=wkv_in[l, :, 2 * hp].rearrange("s j i -> j s i")), [], [Ss.k()])
            dma(lambda e: e.dma_start(out=Ss[64:128, :, 64:128], in_=wkv_in[l, :, 2 * hp + 1].rearrange("s j i -> j s i")), [], [Ss.k()])
            unit(hp, "s", 0, True, lambda g: (Ss[:, g, :], Ss.k()))
            for g in range(NSQ):
                state_out(hp, Ss[:, g, :], Ss.k(), 1 + g)
            dma(lambda e: e.dma_start(out=wkvo[l, :, 2 * hp:2 * hp + 2].rearrange("g h j i -> (h j) g i"), in_=onat[:, :, :]), [onat.k()], [("wkvo", (l, hp))])
            mark(f"B{l}s_{hp}")
            dve(lambda e: e.tensor_tensor(p_k[:, :], yt[:, :], yt[:, :], ALU.mult), [yt.k()], [p_k.k()])
            for (c0, c1) in TB:
                ps = psm()
                pe(lambda e, ps=ps, c0=c0, c1=c1: e.matmul(ps[:, 0:c1 - c0], bd, yt[:, c0:c1], start=True, stop=True), [cst.k(), yt.k()], [ps.k()])
                act(lambda e, ps=ps, c0=c0, c1=c1: e.activation(p_v[:, c0:c1], ps[:, 0:c1 - c0], AF.Copy, scale=1.0 / 64), [ps.k()], [p_v.k()])
                ps2 = psm()
                pe(lambda e, ps2=ps2, c0=c0, c1=c1: e.matmul(ps2[:, 0:c1 - c0], bd, p_k[:, c0:c1], start=True, stop=True), [cst.k(), p_k.k()], [ps2.k()])
                act(lambda e, ps2=ps2, c0=c0, c1=c1: e.activation(p_r[:, c0:c1], ps2[:, 0:c1 - c0], AF.Copy, scale=1.0 / 64), [ps2.k()], [p_r.k()])
            dve(lambda e: e.tensor_tensor(p_k[:, :], p_v[:, :], p_v[:, :], ALU.mult), [p_v.k()], [p_k.k()])
            dve(lambda e: e.tensor_tensor(p_r[:, :], p_r[:, :], p_k[:, :], ALU.subtract), [p_r.k(), p_k.k()], [p_r.k()])
            dve(lambda e: e.tensor_scalar(p_r[:, :], p_r[:, :], 0.0, None, ALU.max), [p_r.k()], [p_r.k()])
            act(lambda e: e.activation(p_r[:, :], p_r[:, :], AF.Sqrt, bias=eps_t[:, 1:2], scale=1.0), [p_r.k(), eps_t.k()], [p_r.k()])
            dve(lambda e: e.reciprocal(p_k[:, :], p_r[:, :]), [p_r.k()], [p_k.k()])
            dve(lambda e: e.tensor_tensor(yt[:, :], yt[:, :], p_v[:, :], ALU.subtract), [yt.k(), p_v.k()], [yt.k()])
            dve(lambda e: e.tensor_tensor(yt[:, :], yt[:, :], p_k[:, :], ALU.mult), [yt.k(), p_k.k()], [yt.k()])
            dve(lambda e: e.tensor_scalar(yt[:, :], yt[:, :], spt[:, LNW + hp:LNW + hp + 1], spt[:, LNB + hp:LNB + hp + 1], ALU.mult, ALU.add), [yt.k(), spt.k()], [yt.k()])
            dve(lambda e: e.scalar_tensor_tensor(p_k[:, :], Wr[:, :], spt[:, RKc + hp:RKc + hp + 1], Wk2[:, :], ALU.mult, ALU.mult), [Wr.k(), Wk2.k(), spt.k()], [p_k.k()])
            for (c0, c1) in TB:
                ps = psm()
                pe(lambda e, ps=ps, c0=c0, c1=c1: e.matmul(ps[:, 0:c1 - c0], bd, p_k[:, c0:c1], start=True, stop=True), [cst.k(), p_k.k()], [ps.k()])
                dve(lambda e, ps=ps, c0=c0, c1=c1: e.tensor_tensor(p_v[:, c0:c1], ps[:, 0:c1 - c0], Wv[:, c0:c1], ALU.mult), [ps.k(), Wv.k()], [p_v.k()])
            dve(lambda e: e.tensor_tensor(yt[:, :], yt[:, :], p_v[:, :], ALU.add), [yt.k(), p_v.k()], [yt.k()])
            for (c0, c1) in TA:
                ps = psm()
                for k2 in range(2):
                    pe(lambda e, ps=ps, c0=c0, c1=c1, k2=k2: e.matmul(ps[:, 0:c1 - c0], gl2t[:, k2, hc], gt[:, k2, c0:c1], start=(k2 == 0), stop=(k2 == 1)), [gl2t.k(), gt.k()], [ps.k()])
                dve(lambda e, ps=ps, c0=c0, c1=c1: e.tensor_tensor(o3[:, hp, c0:c1], ps[:, 0:c1 - c0], yt[:, c0:c1], ALU.mult), [ps.k(), yt.k()], [og.k(hp)])
            mark(f"B{l}hp{hp}")
        for fc in range(8):
            wbg = load_w(24 + fc)
            wcg = load_w(32 + fc)
            whc = load_w(40 + fc)
            proj(wbg, p_r, TB); proj(wcg, p_k, TB); proj(whc, p_v, TB)
            dve(lambda e: e.tensor_tensor(yt[:, :], p_k[:, :], p_v[:, :], ALU.mult), [p_k.k(), p_v.k()], [yt.k()])
            dma(lambda e: e.dma_start(out=sview(yt[:, :])[:, :, 0:2], in_=convT[l, fc * 128:(fc + 1) * 128]), [], [yt.k()])
            cw = CW + fc * 3
            dve(lambda e: e.tensor_scalar(a_sig[:, 2:E], yt[:, 0:E - 2], spt[:, cw:cw + 1], None, ALU.mult), [yt.k(), spt.k()], [a_sig.k()])
            dve(lambda e: e.scalar_tensor_tensor(a_sig[:, 2:E], yt[:, 1:E - 1], spt[:, cw + 1:cw + 2], a_sig[:, 2:E], ALU.mult, ALU.add), [yt.k(), spt.k(), a_sig.k()], [a_sig.k()])
            dve(lambda e: e.scalar_tensor_tensor(a_sig[:, 2:E], yt[:, 2:E], spt[:, cw + 2:cw + 3], a_sig[:, 2:E], ALU.mult, ALU.add), [yt.k(), spt.k(), a_sig.k()], [a_sig.k()])
            dve(lambda e: e.tensor_tensor(o3[:, 8 + fc, 2:E], a_sig[:, 2:E], p_r[:, 2:E], ALU.mult), [a_sig.k(), p_r.k()], [og.k(8 + fc)])
            dma(lambda e: e.dma_start(out=convo[l, fc * 128:(fc + 1) * 128, 0, :], in_=yt[:, SB0 - 2:SB0]), [yt.k()], [("convo", (l, fc, 0))])
            dma(lambda e: e.dma_start(out=convo[l, fc * 128:(fc + 1) * 128, 1:NGRP, :], in_=sview(yt[:, :])[:, :, 4:6]), [yt.k()], [("convo", (l, fc, 1))])

    def phase_CD(l, st, x, md):
        fub = [sb(st, f"fub{i}", [128, KC, 128], BF16) for i in range(4)]
        wob = fub[0:2]
        ua = sb(st, "ua", [128, E]); ub = sb(st, "ub", [128, E]); ca = sb(st, "ca", [128, E]); cbb = sb(st, "cbb", [128, E])
        for fo in range(KC):
            wb = wob[fo % 2]
            dmac(lambda e, wb=wb, fo=fo: e.dma_start(out=wb[:, :, :], in_=w_out[l, fo].rearrange("p (k n) -> p k n", n=128)), [], [wb.k()])
            for (c0, c1) in TB:
                ps = psm()
                for kc in range(KC):
                    pe(lambda e, ps=ps, wb=wb, kc=kc, c0=c0, c1=c1: e.matmul(ps[:, 0:c1 - c0], wb[:, kc, :], o3[:, kc, c0:c1], start=(kc == 0), stop=(kc == KC - 1)), [wb.k(), og.k(kc)], [ps.k()])
                resid(x, fo, ps, c0, c1, md["GA1"], st)
        hl2 = None
        norm_mod(st, x, md["A2"], md["B2"], None, None, "b", pre=(ua, ub, [ca, cbb]))
        mark(f"C{l}")
        halo_exchange(st, 2 * l + 1)
        g3 = og[:, 0:15 * E].rearrange("p (k e) -> p k e", e=E)
        fdb = [sb(st, f"fdb{i}", [128, 15, 128], BF16) for i in range(2)]
        fctr = [0]
        hb0 = 0
        for nh in (15, 15, 14):
            for hbl in range(nh):
                hb = hb0 + hbl
                res = []
                for half, dst, cdst in ((0, ua, ca), (1, ub, cbb)):
                    blk = half * 44 + hb
                    wb = fub[fctr[0] % 4]; fctr[0] += 1
                    dmac(lambda e, wb=wb, blk=blk: e.dma_start(out=wb[:, :, :], in_=ffn_up[l, blk].rearrange("p (k n) -> p k n", n=128)), [], [wb.k()])
                    for (c0, c1) in TB:
                        ps = psm()
                        for kc in range(KC):
                            pe(lambda e, ps=ps, wb=wb, kc=kc, c0=c0, c1=c1: e.matmul(ps[:, 0:c1 - c0], wb[:, kc, :], h[:, kc, c0:c1], start=(kc == 0), stop=(kc == KC - 1)), [wb.k(), h.k(kc)], [ps.k()])
                        act(lambda e, ps=ps, c0=c0, c1=c1, dst=dst: e.activation(dst[:, c0:c1], ps[:, 0:c1 - c0], AF.Copy), [ps.k()], [dst.k()])
                    dma(lambda e, dst=dst, blk=blk: e.dma_start(out=sview(dst[:, :])[:, :, 0:2], in_=ffnT[l, blk * 128:(blk + 1) * 128]), [], [dst.k()])
                    fw = FCW + blk * 3
                    dve(lambda e, dst=dst, cdst=cdst, fw=fw: e.tensor_scalar(cdst[:, 2:E], dst[:, 0:E - 2], spt[:, fw:fw + 1], None, ALU.mult), [dst.k(), spt.k()], [cdst.k()])
                    dve(lambda e, dst=dst, cdst=cdst, fw=fw: e.scalar_tensor_tensor(cdst[:, 2:E], dst[:, 1:E - 1], spt[:, fw + 1:fw + 2], cdst[:, 2:E], ALU.mult, ALU.add), [dst.k(), spt.k(), cdst.k()], [cdst.k()])
                    dve(lambda e, dst=dst, cdst=cdst, fw=fw: e.scalar_tensor_tensor(cdst[:, 2:E], dst[:, 2:E], spt[:, fw + 2:fw + 3], cdst[:, 2:E], ALU.mult, ALU.add), [dst.k(), spt.k(), cdst.k()], [cdst.k()])
                    dma(lambda e, dst=dst, blk=blk: e.dma_start(out=ffno[l, blk * 128:(blk + 1) * 128, 0, :], in_=dst[:, SB0 - 2:SB0]), [dst.k()], [("ffno", (l, blk, 0))])
                    dma(lambda e, dst=dst, blk=blk: e.dma_start(out=ffno[l, blk * 128:(blk + 1) * 128, 1:NGRP, :], in_=sview(dst[:, :])[:, :, 4:6]), [dst.k()], [("ffno", (l, blk, 1))])
                act(lambda e: e.activation(ua[:, 2:E], ca[:, 2:E], AF.Silu), [ca.k()], [ua.k()])
                dve(lambda e, hbl=hbl: e.tensor_tensor(g3[:, hbl, 2:E], ua[:, 2:E], cbb[:, 2:E], ALU.mult), [ua.k(), cbb.k()], [og.k(hbl)])
            for fo in range(KC):
                wb = fdb[fo % 2]
                dmac(lambda e, wb=wb, fo=fo, nh=nh, hb0=hb0: e.dma_start(out=wb[:, 0:nh, :], in_=ffn_down[l, fo, :, hb0 * 128:(hb0 + nh) * 128].rearrange("p (k n) -> p k n", n=128)), [], [wb.k()])
                for (c0, c1) in [(2, 376), (376, 750), (750, E)]:
                    ps = psm()
                    for kc in range(nh):
                        pe(lambda e, ps=ps, wb=wb, kc=kc, c0=c0, c1=c1, nh=nh: e.matmul(ps[:, 0:c1 - c0], wb[:, kc, :], g3[:, kc, c0:c1], start=(kc == 0), stop=(kc == nh - 1)), [wb.k(), og.k(kc)], [ps.k()])
                    resid(x, fo, ps, c0, c1, md["GA2"], st)
            hb0 += nh
        return ua, ub, ca, cbb

    rs_tmp = {}

    def resid(x, fo, ps, c0, c1, GA, st):
        p1 = min(c1, SB0)
        if c0 < p1:
            dve(lambda e: e.scalar_tensor_tensor(x[:, fo, c0:p1], ps[:, 0:p1 - c0], GA[:, fo, 0:1], x[:, fo, c0:p1], ALU.mult, ALU.add), [ps.k(), GA.k(), x.k(fo)], [x.k(fo)])
        if c1 > SB0:
            s0 = max(c0, SB0)
            assert s0 == SB0 and c1 == E
            key = id(st)
            if key not in rs_tmp:
                rs_tmp[key] = sb(st, "rs_tmp", [128, 96])
            rt = rs_tmp[key]
            off = SB0 - c0
            dve(lambda e: e.tensor_tensor(rt[:, :].rearrange("p (s t) -> p s t", t=6), ps[:, off:off + 96].rearrange("p (s t) -> p s t", t=6), GA[:, fo, 1:NGRP].unsqueeze(2).to_broadcast([128, NSQ, 6]), ALU.mult), [ps.k(), GA.k()], [rt.k()])
            dve(lambda e: e.tensor_tensor(x[:, fo, SB0:E], x[:, fo, SB0:E], rt[:, :], ALU.add), [rt.k(), x.k(fo)], [x.k(fo)])

    def main_program():
        mark("init")
        for l in range(2):
            with ExitStack() as stA:
                x = sb(stA, f"xA{l}", [128, KC, E])
                if l == 0:
                    dma(lambda e: e.dma_start(out=x[:, :, :], in_=xT.rearrange("(k p) n -> p k n", p=128)), [], [x.k()])
                else:
                    dma(lambda e: e.dma_start(out=x[:, :, :], in_=x_sp.rearrange("(k p) n -> p k n", p=128)), [("x_sp", None)], [x.k()])
                with ExitStack() as st:
                    md = phase_A(l, st, x, l == 0)
                    hl = sb(st, "hl", [128, KC, NGRP])
                    norm_mod(st, x, md["A1"], md["B1"], None, hl, "a")
                    dma(lambda e: e.dma_start(out=shifto[l].rearrange("(k p) s -> p k s", p=128), in_=hl[:, :, :]), [hl.k()], [("shifto", l)])
                    sst = sb(st, "sst", [128, KC, NSQ])
                    dma(lambda e: e.dma_start(out=sst[:, :, :], in_=shiftT[l].rearrange("(k p) s -> p k s", p=128)), [], [sst.k()])
                    for fc in range(KC):
                        dve(lambda e, fc=fc: e.tensor_copy(sview(h[:, fc, :])[:, :, 1], sst[:, fc, :]), [sst.k()], [h.k(fc)])
                    mark(f"A{l}n")
                    halo_exchange(st, 2 * l)
                    mark(f"A{l}")
                    dma(lambda e: e.dma_start(out=x_sp.rearrange("(k p) n -> p k n", p=128), in_=x[:, :, :]), [x.k()], [("x_sp", None)])
                    keep = {}
                    mdk = {}
                    for nm in ("GA1", "A2", "B2", "GA2"):
                        mdk[nm] = md[nm]
                    for nm in ("GA1", "A2", "B2", "GA2"):
                        t = keep_tiles[nm]
                        dve(lambda e, t=t, nm=nm: e.tensor_copy(t[:, :, :], md[nm][:, :, :]), [md[nm].k()], [t.k()])
                        keep[nm] = t
                    S.barrier()
            with ExitStack() as stB:
                phase_B(l, stB)
                mark(f"B{l}")
                S.barrier()
            with ExitStack() as stC:
                x = sb(stC, f"xC{l}", [128, KC, E])
                dma(lambda e: e.dma_start(out=x[:, :, :], in_=x_sp.rearrange("(k p) n -> p k n", p=128)), [("x_sp", None)], [x.k()])
                fua, fub_, fca, fcb = phase_CD(l, stC, x, keep)
                if l == 0:
                    dma(lambda e: e.dma_start(out=x_sp.rearrange("(k p) n -> p k n", p=128), in_=x[:, :, :]), [x.k()], [("x_sp", None)])
                else:
                    load_fin = None
                    rstd = fua; tmp = fub_; sq = [fca, fcb]
                    n = 0
                    for (c0, c1) in TB:
                        ps = psm()
                        for fc in range(KC):
                            q = sq[n % 2]; n += 1
                            act(lambda e, q=q, fc=fc, c0=c0, c1=c1: e.activation(q[:, 0:c1 - c0], x[:, fc, c0:c1], AF.Square), [x.k(fc)], [q.k()])
                            pe(lambda e, ps=ps, q=q, fc=fc, c0=c0, c1=c1: e.matmul(ps[:, 0:c1 - c0], ones, q[:, 0:c1 - c0], start=(fc == 0), stop=(fc == KC - 1)), [q.k(), cst.k()], [ps.k()])
                        act(lambda e, ps=ps, c0=c0, c1=c1: e.activation(tmp[:, c0:c1], ps[:, 0:c1 - c0], AF.Sqrt, bias=eps_t[:, 0:1], scale=1.0 / D), [ps.k(), eps_t.k()], [tmp.k()])
                    dve(lambda e: e.reciprocal(rstd[:, :], tmp[:, :]), [tmp.k()], [rstd.k()])
                    for fc in range(KC):
                        dve(lambda e, fc=fc: e.scalar_tensor_tensor(x[:, fc, :], x[:, fc, :], spt[:, FNG + fc:FNG + fc + 1], rstd[:, :], ALU.mult, ALU.mult), [x.k(fc), spt.k(), rstd.k()], [x.k(fc)])
                    dma(lambda e: e.dma_start(out=yT[:, 0:NPT].rearrange("(k p) n -> p k n", p=128), in_=x[:, :, 2:SB0]), [x.k()], [("yT", 0)])
                    for fc in range(KC):
                        dma(lambda e, fc=fc: e.dma_start(out=yT[fc * 128:(fc + 1) * 128, NPT:NPT + 64].rearrange("p (s t) -> p s t", t=4), in_=sview(x[:, fc, :])[:, :, 2:6]), [x.k(fc)], [("yT", 1 + fc)])
                S.barrier()

    try:
        main_program()
    except _StopBuild:
        pass
    S.stopped = False
    S.barrier()
    S.emit()
    root.close()
    return nc


_PROG = [None]


def _consts():
    c = np.zeros((128, NCONST), np.float32)
    idx = np.arange(128)
    c[:, C_ID:C_ID + 128] = np.eye(128)
    c[:, C_ONES:C_ONES + 128] = 1.0
    c[:, C_BD:C_BD + 128] = (idx[:, None] // 64 == idx[None, :] // 64)
    su = (idx[:, None] < idx[None, :]).astype(np.float32)
    iu = (idx[:, None] <= idx[None, :]).astype(np.float32)
    c[:, C_MUP:C_MUP + 128] = su; c[:, C_MUP + 128:C_MUP + 256] = iu
    c[:, C_MLP:C_MLP + 128] = (idx[None, :] < idx[:, None])
    c[:, C_UTP:C_UTP + 128] = -DEC * iu; c[:, C_UTP + 128:C_UTP + 256] = -DEC * su
    i64 = np.arange(64)
    same = (i64[:, None] // 4 == i64[None, :] // 4).astype(np.float32)
    su6 = su[:64, :64] * same; iu6 = iu[:64, :64] * same
    c[:64, C_MUS:C_MUS + 64] = su6; c[:64, C_MUS + 64:C_MUS + 128] = iu6
    c[:64, C_MLS:C_MLS + 64] = (i64[None, :] < i64[:, None]) * same
    c[:64, C_UTS:C_UTS + 64] = -DEC * iu6; c[:64, C_UTS + 64:C_UTS + 128] = -DEC * su6
    cm = (i64[None, :] // 4 == np.arange(16)[:, None]).astype(np.float32)
    c[:, C_CMS:C_CMS + 1024] = cm.reshape(1, 1024)
    c[:64, C_RMS:C_RMS + 16] = (i64[:, None] // 4 == np.arange(16)[None, :])
    return c


def kernel(x_prompt, x_sample, c_prompt, c_sample, state_wkv, state_shift, state_conv, state_ffn,
           ada_w, ada_b, norm_g, final_norm_g, w_in, mu_x, mu_rkv, decay_w0, decay_lora1, decay_lora2,
           iclr_a0, iclr_lora1, iclr_lora2, gate_lora1, gate_lora2, vres_v0, vres_lora1, vres_lora2,
           k_k, k_a, r_k, ln_x_w, ln_x_b, conv_w, w_out, ffn_up, ffn_conv, ffn_down):
    f = lambda a: np.ascontiguousarray(np.asarray(a, dtype=np.float32))
    (x_prompt, x_sample, c_prompt, c_sample, state_wkv, state_shift, state_conv, state_ffn) = map(
        f, (x_prompt, x_sample, c_prompt, c_sample, state_wkv, state_shift, state_conv, state_ffn))
    if _PROG[0] is None:
        _PROG[0] = build_program()
    nc = _PROG[0]
    pk = lambda v, n: np.asarray(v, np.float32).reshape(n, 128).T
    smallp = np.zeros((2, 128, NSP), np.float32)
    for l in range(2):
        sp = smallp[l]
        sp[:, ADAB:ADAB + 96] = pk(ada_b[l], 96)
        sp[:, NG0:NG0 + 16] = pk(norm_g[l, 0], 16); sp[:, NG1:NG1 + 16] = pk(norm_g[l, 1], 16)
        sp[:, MUX:MUX + 64] = np.asarray(mu_x[l]).reshape(4, 16, 128).transpose(2, 1, 0).reshape(128, 64)
        sp[:, MURKV:MURKV + 24] = np.asarray(mu_rkv[l]).reshape(3, 8, 128).transpose(2, 0, 1).reshape(128, 24)
        sp[:, A0:A0 + 8] = pk(iclr_a0[l], 8); sp[:, V0:V0 + 8] = pk(vres_v0[0], 8)
        sp[:, KKc:KKc + 8] = pk(k_k[l], 8); sp[:, KAc:KAc + 8] = pk(k_a[l], 8); sp[:, RKc:RKc + 8] = pk(r_k[l], 8)
        sp[:, LNW:LNW + 8] = pk(ln_x_w[l], 8); sp[:, LNB:LNB + 8] = pk(ln_x_b[l], 8)
        sp[:, CW:CW + 24] = np.asarray(conv_w[l]).reshape(3, 8, 128).transpose(2, 1, 0).reshape(128, 24)
        sp[:, FCW:FCW + 264] = np.asarray(ffn_conv[l]).reshape(3, 88, 128).transpose(2, 1, 0).reshape(128, 264)
        sp[:, FNG:FNG + 16] = pk(final_norm_g, 16)
    consts = _consts()
    def tile_w(w, nk):
        w = np.asarray(w, np.float32)
        nb = w.shape[2] // 128
        return np.ascontiguousarray(w.reshape(2, nk, 128, nb, 128).transpose(0, 3, 2, 1, 4).reshape(2, nb, 128, nk * 128))
    shared = dict(smallp=smallp, w0row=f(decay_w0).reshape(2, 1, G), consts=consts, ada_w=tile_w(ada_w, KC), w_in=tile_w(w_in, KC),
                  decay_lora1=f(decay_lora1), decay_lora2=f(decay_lora2), iclr_lora1=f(iclr_lora1), iclr_lora2=f(iclr_lora2),
                  gate_lora1=f(gate_lora1), gate_lora2=f(gate_lora2), vres_lora1=f(vres_lora1), vres_lora2=f(vres_lora2),
                  w_out=tile_w(w_out, KC), ffn_up=tile_w(ffn_up, KC), ffn_down=tile_w(ffn_down, 44))
    in_maps = []
    for c in range(8):
        seq, half = c // 2, c % 2
        xT = np.zeros((D, E), np.float32)
        xT[:, 2:SB0] = x_prompt[seq, half * NPT:(half + 1) * NPT, :].T
        xs = x_sample[16 * c:16 * c + 16]
        xv = xT[:, SB0:].reshape(D, 16, 6)
        xv[:, :, 2:6] = xs.transpose(2, 0, 1)
        cT = np.zeros((D, NGRP), np.float32)
        cT[:, 0] = c_prompt[seq]; cT[:, 1:] = c_sample[16 * c:16 * c + 16].T
        m = dict(shared)
        m.update(xT=xT, cT=cT, flag=np.full((128, 1), float(half), np.float32),
                 shiftT=f(state_shift[:, 16 * c:16 * c + 16, :].transpose(0, 2, 1)),
                 convT=f(state_conv[:, 16 * c:16 * c + 16].transpose(0, 3, 1, 2)),
                 ffnT=f(state_ffn[:, 16 * c:16 * c + 16].transpose(0, 3, 1, 2)),
                 wkv_in=f(state_wkv[:, 16 * c:16 * c + 16].transpose(0, 1, 2, 4, 3)))
        in_maps.append(m)
    res = run_bass_kernel_spmd(nc, in_maps, core_ids=list(range(8))).results
    y_p = np.zeros((4, 2048, D), np.float32); y_s = np.zeros((128, 4, D), np.float32)
    wkv_p = np.zeros((2, 4, 16, 64, 64), np.float32); wkv_s = np.zeros((2, 128, 16, 64, 64), np.float32)
    sh_p = np.zeros((2, 4, D), np.float32); sh_s = np.zeros((2, 128, D), np.float32)
    cv_p = np.zeros((2, 4, 2, G), np.float32); cv_s = np.zeros((2, 128, 2, G), np.float32)
    ff_p = np.zeros((2, 4, 2, 2 * FF), np.float32); ff_s = np.zeros((2, 128, 2, 2 * FF), np.float32)
    for c in range(8):
        seq, half = c // 2, c % 2
        r = res[c]
        yTo = np.asarray(r["yT"])
        y_p[seq, half * NPT:(half + 1) * NPT] = yTo[:, :NPT].T
        y_s[16 * c:16 * c + 16] = yTo[:, NPT:].reshape(D, 16, 4).transpose(1, 2, 0)
        wk = np.asarray(r["wkvo"]).transpose(0, 1, 2, 4, 3); sh = np.asarray(r["shifto"]); cvo = np.asarray(r["convo"]); ffo = np.asarray(r["ffno"])
        wkv_s[:, 16 * c:16 * c + 16] = wk[:, 1:]
        sh_s[:, 16 * c:16 * c + 16] = sh[:, :, 1:].transpose(0, 2, 1)
        cv_s[:, 16 * c:16 * c + 16] = cvo[:, :, 1:, :].transpose(0, 2, 3, 1)
        ff_s[:, 16 * c:16 * c + 16] = ffo[:, :, 1:, :].transpose(0, 2, 3, 1)
        if half == 1:
            wkv_p[:, seq] = wk[:, 0]
            sh_p[:, seq] = sh[:, :, 0]
            cv_p[:, seq] = cvo[:, :, 0, :].transpose(0, 2, 1)
            ff_p[:, seq] = ffo[:, :, 0, :].transpose(0, 2, 1)
    return (y_p, y_s, wkv_p, sh_p, cv_p, ff_p, wkv_s, sh_s, cv_s, ff_s)
```
